# Optimizing a Trainium2 kernel written in Bass

```python
import math
import jax
import jax.numpy as jnp
from jax import lax
import numpy as np

D_MODEL = 1024
BATCH = 8
SEQ = 2048
DEPTH = 2
DEC_BATCH = 32
DEC_SEQ = 8
PAST_LEN = 8192
PAGE_SIZE = 128

N_META = 16
D_FF = 2816
SSD_HEADS = 8
SSD_HEAD_DIM = 64
SSD_INNER = SSD_HEADS * SSD_HEAD_DIM
SSD_GROUPS = 2
SSD_STATE = 64
SSD_CONV = 4
SSD_CONV_DIM = SSD_INNER + 2 * SSD_GROUPS * SSD_STATE
CHUNK = 128
SB_HEADS = 8
SB_HEAD_DIM = 64
SB_WIDTH = SB_HEADS * SB_HEAD_DIM
BLOCK_Q = 128
D_MIX = SSD_INNER + SB_WIDTH
IN_COLS = SSD_INNER + SSD_CONV_DIM + SSD_HEADS + 3 * SB_WIDTH
EPS = 1e-6

kernel_name = 'hymba_ssd_stickbreaking_macaron_step'


def rms_norm(x, g):
    xf = x.astype(jnp.float32)
    y = xf * lax.rsqrt(jnp.mean(xf * xf, axis=-1, keepdims=True) + EPS)
    return (y * g.astype(jnp.float32)).astype(x.dtype)


def swiglu(h, w_gu, w_down):
    g, u = jnp.split(h @ w_gu, 2, axis=-1)
    return (jax.nn.silu(g) * u) @ w_down


def causal_conv(u, buf, w, b):
    L = u.shape[1]
    up = jnp.concatenate([buf.astype(u.dtype), u], axis=1)
    acc = up[:, 0:L] * w[0]
    for j in range(1, SSD_CONV):
        acc = acc + up[:, j:j + L] * w[j]
    return jax.nn.silu(acc + b), up[:, L:]


def ssd_scan(x, dt, A, Bm, Cm, h0):
    b, L, nh, p = x.shape
    n = Bm.shape[-1]
    Q = CHUNK if L % CHUNK == 0 else L
    c = L // Q
    x = x.reshape(b, c, Q, nh, p)
    dt = dt.reshape(b, c, Q, nh)
    Bm = Bm.reshape(b, c, Q, nh, n)
    Cm = Cm.reshape(b, c, Q, nh, n)
    cs = jnp.cumsum(dt * A, axis=2)
    xdt = x * dt[..., None]
    seg = cs[:, :, :, None, :] - cs[:, :, None, :, :]
    causal = jnp.tril(jnp.ones((Q, Q), dtype=bool))[None, None, :, :, None]
    decay = jnp.exp(jnp.where(causal, seg, -jnp.inf))
    scores = jnp.einsum('bcthn,bcshn->bctsh', Cm, Bm) * decay
    y_diag = jnp.einsum('bctsh,bcshp->bcthp', scores, xdt)
    to_end = jnp.exp(cs[:, :, -1:, :] - cs)
    states = jnp.einsum('bcshn,bcsh,bcshp->bchpn', Bm, to_end, xdt)
    chunk_decay = jnp.exp(cs[:, :, -1, :])

    def step(h_prev, inp):
        st, dec = inp
        return h_prev * dec[..., None, None] + st, h_prev

    h_last, h_prev = lax.scan(step, h0, (jnp.moveaxis(states, 1, 0), jnp.moveaxis(chunk_decay, 1, 0)))
    h_prev = jnp.moveaxis(h_prev, 0, 1)
    y_off = jnp.einsum('bcthn,bcth,bchpn->bcthp', Cm, jnp.exp(cs), h_prev)
    return (y_diag + y_off).reshape(b, L, nh, p), h_last


def sb_block(q, start, k, v, key_valid, bias):
    qlen, klen = q.shape[1], k.shape[1]
    z = jnp.einsum('bqhd,bkhd->bhqk', q, k).astype(jnp.float32) * (SB_HEAD_DIM ** -0.5)
    z = z + bias.astype(jnp.float32)[None, :, None, None]
    qpos = start + jnp.arange(qlen)
    kpos = jnp.arange(klen)
    allowed = (kpos[None, :] < qpos[:, None]) & key_valid[None, :]
    log_stay = jnp.where(allowed, jax.nn.log_sigmoid(-z), 0.0)
    log_after = lax.cumsum(log_stay, axis=3, reverse=True) - log_stay
    w = jnp.where(allowed, jnp.exp(jax.nn.log_sigmoid(z) + log_after), 0.0)
    return jnp.einsum('bhqk,bkhd->bqhd', w.astype(v.dtype), v)


def mixer(h, lp, conv_buf, ssm_h0, k_past, v_past, front_pad):
    b, L, _ = h.shape
    f32 = jnp.float32
    sizes = (SSD_INNER, SSD_CONV_DIM, SSD_HEADS, SB_WIDTH, SB_WIDTH, SB_WIDTH)
    z, xbc, dt_raw, q, k, v = jnp.split(h @ lp['w_in'], np.cumsum(sizes)[:-1].tolist(), axis=-1)
    xbc_c, new_conv = causal_conv(xbc, conv_buf, lp['conv_w'], lp['conv_b'])
    xs, Bm, Cm = jnp.split(xbc_c, [SSD_INNER, SSD_INNER + SSD_GROUPS * SSD_STATE], axis=-1)
    hpg = SSD_HEADS // SSD_GROUPS
    xs = xs.reshape(b, L, SSD_HEADS, SSD_HEAD_DIM).astype(f32)
    Bm = jnp.repeat(Bm.reshape(b, L, SSD_GROUPS, SSD_STATE), hpg, axis=2).astype(f32)
    Cm = jnp.repeat(Cm.reshape(b, L, SSD_GROUPS, SSD_STATE), hpg, axis=2).astype(f32)
    dt = jax.nn.softplus(dt_raw.astype(f32) + lp['dt_bias'].astype(f32))
    A = -jnp.exp(lp['A_log'].astype(f32))

    def padf(a):
        return jnp.pad(a, [(0, 0), (front_pad, 0)] + [(0, 0)] * (a.ndim - 2))

    y, h_last = ssd_scan(padf(xs), padf(dt), A, padf(Bm), padf(Cm), ssm_h0.astype(f32))
    y = y[:, front_pad:] + lp['D_skip'].astype(f32)[:, None] * xs
    y = y.reshape(b, L, SSD_INNER) * jax.nn.silu(z.astype(f32))
    yg = y.reshape(b, L, SSD_GROUPS, SSD_INNER // SSD_GROUPS)
    yg = yg * lax.rsqrt(jnp.mean(yg * yg, axis=-1, keepdims=True) + EPS)
    y_ssd = (yg.reshape(b, L, SSD_INNER) * lp['ssd_norm'].astype(f32)).astype(h.dtype)
    q = rms_norm(q.reshape(b, L, SB_HEADS, SB_HEAD_DIM), lp['q_norm'])
    k = rms_norm(k.reshape(b, L, SB_HEADS, SB_HEAD_DIM), lp['k_norm'])
    v = v.reshape(b, L, SB_HEADS, SB_HEAD_DIM)
    bias = lp['sb_bias']
    if k_past is None:
        P = front_pad + L
        pad4 = ((0, 0), (front_pad, 0), (0, 0), (0, 0))
        qp, kp, vp = jnp.pad(q, pad4), jnp.pad(k, pad4), jnp.pad(v, pad4)
        valid = jnp.arange(P) >= front_pad
        nb = P // BLOCK_Q
        q_blocks = jnp.moveaxis(qp.reshape(b, nb, BLOCK_Q, SB_HEADS, SB_HEAD_DIM), 1, 0)
        starts = jnp.arange(nb) * BLOCK_Q
        o = lax.map(lambda a: sb_block(a[0], a[1], kp, vp, valid, bias), (q_blocks, starts))
        o = jnp.moveaxis(o, 0, 1).reshape(b, P, SB_HEADS, SB_HEAD_DIM)[:, front_pad:]
    else:
        k_all = jnp.concatenate([k_past.astype(k.dtype), k], axis=1)
        v_all = jnp.concatenate([v_past.astype(v.dtype), v], axis=1)
        valid = jnp.ones((k_all.shape[1],), dtype=bool)
        o = sb_block(q, k_past.shape[1], k_all, v_all, valid, bias)
    y_sb = rms_norm(o.reshape(b, L, SB_WIDTH), lp['sb_out_norm'])
    out = jnp.concatenate([y_ssd, y_sb], axis=-1) @ lp['w_out']
    return out, new_conv, h_last, k, v


def layer(x, lp, conv_buf, ssm_h0, k_past, v_past, front_pad):
    x = x + 0.5 * swiglu(rms_norm(x, lp['norm_ffn1']), lp['ffn1_w_gu'], lp['ffn1_w_down'])
    m, new_conv, h_last, k, v = mixer(rms_norm(x, lp['norm_mix']), lp, conv_buf, ssm_h0, k_past, v_past, front_pad)
    x = x + m
    x = x + 0.5 * swiglu(rms_norm(x, lp['norm_ffn2']), lp['ffn2_w_gu'], lp['ffn2_w_down'])
    return x, new_conv, h_last, k, v


def setup_inputs(seed: int = 0) -> dict:
    key = jax.random.key(seed)
    ks = iter(jax.random.split(key, 32))
    f32 = jnp.float32
    n_pages = PAST_LEN // PAGE_SIZE
    n_phys = (5 * DEC_BATCH * n_pages + 3) // 4

    def nrm(shape, scale):
        return scale * jax.random.normal(next(ks), shape, f32)

    def gain(shape):
        return 1.0 + nrm(shape, 0.02)

    x_prompt = nrm((BATCH, SEQ, D_MODEL), 1.0)
    x_sample = nrm((DEC_BATCH, DEC_SEQ, D_MODEL), 1.0)
    cache_k = nrm((DEPTH, n_phys, PAGE_SIZE, SB_HEADS, SB_HEAD_DIM), 1.0)
    cache_v = nrm((DEPTH, n_phys, PAGE_SIZE, SB_HEADS, SB_HEAD_DIM), 1.0)
    state_ssm = nrm((DEPTH, DEC_BATCH, SSD_HEADS, SSD_HEAD_DIM, SSD_STATE), 0.1)
    state_conv = nrm((DEPTH, DEC_BATCH, SSD_CONV - 1, SSD_CONV_DIM), 1.0)
    page_table = jax.random.permutation(next(ks), n_phys)[:DEC_BATCH * n_pages].reshape(DEC_BATCH, n_pages).astype(jnp.int32)
    meta_tokens = nrm((N_META, D_MODEL), 1.0)
    norm_ffn1 = gain((DEPTH, D_MODEL))
    ffn1_w_gu = nrm((DEPTH, D_MODEL, 2 * D_FF), D_MODEL ** -0.5)
    ffn1_w_down = nrm((DEPTH, D_FF, D_MODEL), D_FF ** -0.5)
    norm_mix = gain((DEPTH, D_MODEL))
    w_in = nrm((DEPTH, D_MODEL, IN_COLS), D_MODEL ** -0.5)
    conv_w = nrm((DEPTH, SSD_CONV, SSD_CONV_DIM), SSD_CONV ** -0.5)
    conv_b = nrm((DEPTH, SSD_CONV_DIM), 0.02)
    dt0 = jnp.exp(jax.random.uniform(next(ks), (DEPTH, SSD_HEADS), f32, math.log(1e-3), math.log(1e-1)))
    dt_bias = dt0 + jnp.log(-jnp.expm1(-dt0))
    A_log = jnp.log(jax.random.uniform(next(ks), (DEPTH, SSD_HEADS), f32, 1.0, 16.0))
    D_skip = gain((DEPTH, SSD_HEADS))
    ssd_norm = gain((DEPTH, SSD_INNER))
    q_norm = gain((DEPTH, SB_HEAD_DIM))
    k_norm = gain((DEPTH, SB_HEAD_DIM))
    sb_bias = jax.random.uniform(next(ks), (DEPTH, SB_HEADS), f32, -7.0, -5.0)
    sb_out_norm = gain((DEPTH, SB_WIDTH))
    w_out = nrm((DEPTH, D_MIX, D_MODEL), (2 * DEPTH * D_MIX) ** -0.5)
    norm_ffn2 = gain((DEPTH, D_MODEL))
    ffn2_w_gu = nrm((DEPTH, D_MODEL, 2 * D_FF), D_MODEL ** -0.5)
    ffn2_w_down = nrm((DEPTH, D_FF, D_MODEL), D_FF ** -0.5)
    return {'x_prompt': x_prompt, 'x_sample': x_sample, 'cache_k': cache_k, 'cache_v': cache_v,
            'state_ssm': state_ssm, 'state_conv': state_conv, 'page_table': page_table,
            'meta_tokens': meta_tokens, 'norm_ffn1': norm_ffn1, 'ffn1_w_gu': ffn1_w_gu,
            'ffn1_w_down': ffn1_w_down, 'norm_mix': norm_mix, 'w_in': w_in, 'conv_w': conv_w,
            'conv_b': conv_b, 'dt_bias': dt_bias, 'A_log': A_log, 'D_skip': D_skip,
            'ssd_norm': ssd_norm, 'q_norm': q_norm, 'k_norm': k_norm, 'sb_bias': sb_bias,
            'sb_out_norm': sb_out_norm, 'w_out': w_out, 'norm_ffn2': norm_ffn2,
            'ffn2_w_gu': ffn2_w_gu, 'ffn2_w_down': ffn2_w_down}


def reference(x_prompt, x_sample, cache_k, cache_v, state_ssm, state_conv, page_table, meta_tokens,
              norm_ffn1, ffn1_w_gu, ffn1_w_down, norm_mix, w_in, conv_w, conv_b, dt_bias, A_log,
              D_skip, ssd_norm, q_norm, k_norm, sb_bias, sb_out_norm, w_out, norm_ffn2, ffn2_w_gu,
              ffn2_w_down):
    params = {'norm_ffn1': norm_ffn1, 'ffn1_w_gu': ffn1_w_gu, 'ffn1_w_down': ffn1_w_down,
              'norm_mix': norm_mix, 'w_in': w_in, 'conv_w': conv_w, 'conv_b': conv_b,
              'dt_bias': dt_bias, 'A_log': A_log, 'D_skip': D_skip, 'ssd_norm': ssd_norm,
              'q_norm': q_norm, 'k_norm': k_norm, 'sb_bias': sb_bias, 'sb_out_norm': sb_out_norm,
              'w_out': w_out, 'norm_ffn2': norm_ffn2, 'ffn2_w_gu': ffn2_w_gu,
              'ffn2_w_down': ffn2_w_down}
    bp = x_prompt.shape[0]
    db = x_sample.shape[0]
    meta = jnp.broadcast_to(meta_tokens[None].astype(x_prompt.dtype), (bp, N_META, D_MODEL))
    xp = jnp.concatenate([meta, x_prompt], axis=1)
    front_pad = (-xp.shape[1]) % CHUNK
    xs = x_sample
    kp_l, vp_l, sp_l, cp_l, ks_l, vs_l, ss_l, cs_l = [], [], [], [], [], [], [], []
    for l in range(DEPTH):
        lp = {name: arr[l] for name, arr in params.items()}
        conv0 = jnp.zeros((bp, SSD_CONV - 1, SSD_CONV_DIM), xp.dtype)
        ssm0 = jnp.zeros((bp, SSD_HEADS, SSD_HEAD_DIM, SSD_STATE), jnp.float32)
        xp, c_p, s_p, k_p, v_p = layer(xp, lp, conv0, ssm0, None, None, front_pad)
        k_past = cache_k[l][page_table].reshape(db, -1, SB_HEADS, SB_HEAD_DIM)
        v_past = cache_v[l][page_table].reshape(db, -1, SB_HEADS, SB_HEAD_DIM)
        xs, c_s, s_s, k_s, v_s = layer(xs, lp, state_conv[l], state_ssm[l], k_past, v_past, 0)
        kp_l.append(k_p); vp_l.append(v_p); sp_l.append(s_p); cp_l.append(c_p)
        ks_l.append(k_s); vs_l.append(v_s); ss_l.append(s_s); cs_l.append(c_s)
    y_prompt = xp[:, N_META:]
    return (y_prompt, xs, jnp.stack(kp_l), jnp.stack(vp_l), jnp.stack(sp_l), jnp.stack(cp_l),
            jnp.stack(ks_l), jnp.stack(vs_l), jnp.stack(ss_l), jnp.stack(cs_l))
```

```python
import contextlib
import numpy as np
import concourse.bass as bass
import concourse.mybir as mybir
from concourse.bass_utils import run_bass_kernel_spmd

F32, BF16, I32 = mybir.dt.float32, mybir.dt.bfloat16, mybir.dt.int32
AF = mybir.ActivationFunctionType
ALU = mybir.AluOpType
AX = mybir.AxisListType
EPS = 1e-6


class Trk:
    __slots__ = ("w", "r", "const", "excl")

    def __init__(self, excl=False):
        self.w = None
        self.r = []
        self.const = False
        self.excl = excl


class Op:
    __slots__ = ("eng", "fn", "deps", "signaled", "sig", "is_dma", "sem", "val", "prewait")

    def __init__(self, eng, fn, is_dma):
        self.eng = eng
        self.fn = fn
        self.deps = []
        self.signaled = False
        self.sig = 0
        self.is_dma = is_dma
        self.sem = None
        self.val = 0
        self.prewait = None


class Sched:
    ENGS = ("pe", "act", "dve", "pool", "sp")
    RING = {"sp": 28, "pool": 24}
    CAP = 1500

    def __init__(self, nc, stack):
        self.nc = nc
        self.stack = stack
        self.ops = {e: [] for e in self.ENGS}
        self.esem = {}
        self.dsem = {q: [stack.enter_context(nc.semaphore("%s_dma%d" % (q, i))) for i in range(n)]
                     for q, n in self.RING.items()}
        self.dcount = {q: 0 for q in self.RING}
        self.dhist = {q: [] for q in self.RING}
        self.pending = []
        self.last = {e: None for e in self.ENGS}

    def _dep(self, op, d):
        if d is None or d is op:
            return
        if (not d.is_dma) and d.eng == op.eng and op.eng == "pe" and not op.is_dma:
            return
        if not d.is_dma:
            d.signaled = True
        op.deps.append(d)

    def _dma_slot(self, op, q):
        k = self.dcount[q]
        n = self.RING[q]
        op.sem = self.dsem[q][k % n]
        op.val = 16 * (k // n + 1)
        if k >= n:
            op.prewait = self.dhist[q][k - n]
        self.dhist[q].append(op)
        self.dcount[q] = k + 1

    def op(self, eng, fn, reads=(), writes=(), dma=False):
        op = Op(eng, fn, dma)
        ex = [t for t in reads if t.excl]
        if ex:
            reads = [t for t in reads if not t.excl]
            writes = list(writes) + [t for t in ex if t not in writes]
        for t in reads:
            self._dep(op, t.w)
        for t in writes:
            self._dep(op, t.w)
            for r in t.r:
                self._dep(op, r)
        for t in reads:
            if not t.const:
                t.r.append(op)
        for t in writes:
            t.w = op
            t.r = []
        if dma:
            self._dma_slot(op, eng)
            self.pending.append(op)
        self.ops[eng].append(op)
        if not dma:
            self.last[eng] = op
        return op

    def barrier(self, scratch_a, scratch_b):
        f = Op("sp", lambda e: e.dma_start(out=scratch_a, in_=scratch_b), True)
        f.deps.extend(self.pending)
        for e in ("pe", "act", "dve", "pool"):
            if self.last[e] is not None:
                self.last[e].signaled = True
                f.deps.append(self.last[e])
        self._dma_slot(f, "sp")
        self.ops["sp"].append(f)
        self.pending = [f]
        for e in ("pe", "act", "dve", "pool"):
            w = Op(e, None, False)
            w.deps.append(f)
            self.ops[e].append(w)

    def fence_all(self):
        w = Op("sp", None, False)
        w.deps.extend(self.pending)
        for e in ("pe", "act", "dve", "pool"):
            if self.last[e] is not None:
                self.last[e].signaled = True
                w.deps.append(self.last[e])
        self.ops["sp"].append(w)

    def emit(self):
        nc = self.nc
        CAP = self.CAP
        for e in ("pe", "act", "dve", "pool"):
            c = 0
            for op in self.ops[e]:
                if op.signaled and op.fn is not None:
                    c += 1
                    op.sig = c
            self.esem[e] = [self.stack.enter_context(nc.semaphore("%s_prog%d" % (e, i))) for i in range(c // CAP + 1)]
        bname = {"pe": "tensor", "act": "scalar", "dve": "vector", "pool": "gpsimd", "sp": "sync"}
        esem = self.esem
        with nc.Block() as block:
            for e in self.ENGS:
                ops = self.ops[e]

                def body(eng, ops=ops, e=e):
                    waited = {}

                    def wait(sem, val):
                        key = id(sem)
                        if waited.get(key, 0) >= val:
                            return
                        waited[key] = val
                        eng.wait_ge(sem, val)

                    for op in ops:
                        if op.prewait is not None:
                            wait(op.prewait.sem, op.prewait.val)
                        for d in op.deps:
                            if d.is_dma:
                                wait(d.sem, d.val)
                            else:
                                wait(esem[d.eng][(d.sig - 1) // CAP], (d.sig - 1) % CAP + 1)
                        if op.fn is None:
                            continue
                        inst = op.fn(eng)
                        if op.is_dma:
                            inst.then_inc(op.sem, 16)
                        elif op.signaled:
                            inst.then_inc(esem[e][(op.sig - 1) // CAP], 1)

                getattr(block, bname[e])(body)


def build(cfg):
    SEQ, DFF, PAST, NPHYS, SEGN = cfg["SEQ"], cfg["DFF"], cfg["PAST"], cfg["NPHYS"], cfg["SEGN"]
    DM, NC8 = 1024, 8
    NPOS = SEQ + 16
    NFT = NPOS // 128
    assert NPOS == NFT * 128 + 16
    NS = 32
    NT = NPOS + NS
    NPG = PAST // 128
    FC = DFF // 128
    KCH = 512
    PGC = 4 if NPG % 4 == 0 else 2
    NCH_S = NPG // PGC
    IN_COLS = 2824

    nc = bass.Bass("TRN2", target_bir_lowering=False)

    def din(name, shape, dt=F32):
        return nc.dram_tensor(name, list(shape), dt, kind="ExternalInput").ap()

    def dout(name, shape, dt=F32):
        return nc.dram_tensor(name, list(shape), dt, kind="ExternalOutput").ap()

    xin = din("xin", [NT, DM])
    ckT = [din("ckT%d" % i, [NPHYS, 128, 512]) for i in range(2)]
    cv = [din("cv%d" % i, [NPHYS, 128, 512]) for i in range(2)]
    sssm = din("sssm", [2, 4, 8, 64, 64])
    sconv = din("sconv", [2, 4, 3, 768])
    ptab = din("ptab", [1, 4 * NPG], I32)
    W = {}
    for nm, shp in (("norm_ffn1", [2, DM]), ("ffn1_w_gu", [2, DM, 2 * DFF]), ("ffn1_w_down", [2, DFF, DM]),
                    ("norm_mix", [2, DM]), ("w_in", [2, DM, IN_COLS]), ("conv_w", [2, 4, 768]),
                    ("conv_b", [2, 768]), ("dt_bias", [2, 8]), ("A_log", [2, 8]), ("D_skip", [2, 8]),
                    ("ssd_norm", [2, 512]), ("q_norm", [2, 64]), ("k_norm", [2, 64]), ("sb_bias", [2, 8]),
                    ("sb_out_norm", [2, 512]), ("w_out", [2, DM, DM]), ("norm_ffn2", [2, DM]),
                    ("ffn2_w_gu", [2, DM, 2 * DFF]), ("ffn2_w_down", [2, DFF, DM])):
        W[nm] = din(nm, shp)
    y_o = dout("y", [NT, DM])
    kp_o = dout("kp", [2, NPOS, 512])
    vp_o = dout("vp", [2, NPOS, 512])
    ssmp_o = dout("ssmp", [2, 8, 64, 64])
    convp_o = dout("convp", [2, 3, 768])
    ks_o = dout("ks", [2, NS, 512])
    vs_o = dout("vs", [2, NS, 512])
    ssms_o = dout("ssms", [2, 4, 8, 64, 64])
    convs_o = dout("convs", [2, 4, 3, 768])

    segs = []
    c = 0
    while c < NFT * 128:
        n = min(SEGN, NFT * 128 - c)
        segs.append((c, n))
        c += n
    segs.append((NFT * 128, 16 + NS))
    half = (len(segs) - 1 + 1) // 2
    blocks = [segs[:half], segs[half:]] if len(segs) > 2 else [segs[:1], segs[1:]]
    NBMAX = max(sum(n for _, n in b) for b in blocks)

    with contextlib.ExitStack() as st:
        S = Sched(nc, st)

        _cnt = [0]

        def sb(name, shape, dt=F32, stack=st):
            _cnt[0] += 1
            return stack.enter_context(nc.sbuf_tensor("%s_%d" % (name, _cnt[0]), list(shape), dt))

        def mm(out, lhsT, rhs, start, stop, reads, writes):
            S.op("pe", lambda e: e.matmul(out, lhsT=lhsT, rhs=rhs, start=start, stop=stop), reads, writes)

        def tr(out, in_, ident, reads, writes):
            S.op("pe", lambda e: e.transpose(out, in_, ident), reads, writes)

        def act(out, in_, func, reads, writes, bias=0.0, scale=1.0):
            S.op("act", lambda e: e.activation(out=out, in_=in_, func=func, bias=bias, scale=scale), reads, writes)

        def tt(eng, out, in0, in1, op, reads, writes):
            S.op(eng, lambda e: e.tensor_tensor(out=out, in0=in0, in1=in1, op=op), reads, writes)

        def ts(eng, out, in0, s1, s2, op0, op1, reads, writes):
            if s2 is None:
                S.op(eng, lambda e: e.tensor_scalar(out=out, in0=in0, scalar1=s1, scalar2=None, op0=op0), reads, writes)
            else:
                S.op(eng, lambda e: e.tensor_scalar(out=out, in0=in0, scalar1=s1, scalar2=s2, op0=op0, op1=op1), reads, writes)

        def stt(eng, out, in0, scalar, in1, op0, op1, reads, writes):
            S.op(eng, lambda e: e.scalar_tensor_tensor(out=out, in0=in0, scalar=scalar, in1=in1, op0=op0, op1=op1), reads, writes)

        def cp(eng, out, in_, reads, writes):
            if eng == "act":
                act(out, in_, AF.Copy, reads, writes)
            else:
                S.op(eng, lambda e: e.tensor_copy(out=out, in_=in_), reads, writes)

        def mset(eng, ap, val, writes):
            S.op(eng, lambda e: e.memset(ap, val), (), writes)

        def dma(q, out, in_, reads, writes):
            return S.op(q, lambda e: e.dma_start(out=out, in_=in_), reads, writes, dma=True)

        def dma_nc(q, out, in_, reads, writes):
            return S.op(q, lambda e: e.dma_start(out=out, in_=in_, allow_slow_non_contiguous=True), reads, writes, dma=True)

        _rr = {"regs": None, "i": 0}

        def page_val(e, ap):
            if _rr["regs"] is None:
                _rr["regs"] = [e.alloc_register("pgr%d" % i) for i in range(8)]
            r = _rr["regs"][_rr["i"] % 8]
            _rr["i"] += 1
            e.reg_load(r, ap)
            return e.snap(r)

        def rsum(eng, out, in_, reads, writes):
            S.op(eng, lambda e: e.reduce_sum(out=out, in_=in_, axis=AX.X), reads, writes)

        xT = sb("xT", [128, 8, NT])
        xtrk = {}

        def XT(c, s0):
            return xtrk.setdefault((c, s0 // 128), Trk())

        def xtr(c0, n):
            return [XT(c, s) for c in range(8) for s in range((c0 // 128) * 128, c0 + n, 128)]

        identF = sb("identF", [128, 128]); identB = sb("identB", [128, 128], BF16)
        onesB = sb("onesB", [128, 128], BF16); onesF = sb("onesF", [128, 128])
        triF = sb("triF", [128, 128]); m01F = sb("m01F", [128, 128]); m01B = sb("m01B", [128, 128], BF16)
        bmask = sb("bmask", [128, 8]); onec = sb("onec", [128, 1]); epsc = sb("epsc", [128, 1])
        rhs64 = sb("rhs64", [64, 1024]); lhs64 = sb("lhs64", [64, 128]); ext = sb("ext", [128, 40])
        scr = sb("scr", [1, 8])
        gT = {nm: sb(nm + "_T", [128, 2, 8]) for nm in ("norm_ffn1", "norm_mix", "norm_ffn2")}
        convw = sb("convw", [128, 2, 4, 6]); convb = sb("convb", [128, 2, 6])
        dtb_bc = sb("dtb_bc", [128, 2, 8]); A_bc = sb("A_bc", [128, 2, 8]); D_bc = sb("D_bc", [128, 2, 8])
        sbias_bc = sb("sbias_bc", [128, 2, 8]); sbias_c = sb("sbias_c", [8, 2])
        qn_bc = sb("qn_bc", [128, 2, 64]); kn_bc = sb("kn_bc", [128, 2, 64])
        ptile = sb("ptile", [128, 4 * NPG], I32); idxT = sb("idxT", [128, 4 * NPG], I32)
        iota_c = sb("iota_c", [128, 1], I32); iota_f = sb("iota_f", [128, 1], F32)
        ST = sb("ST", [128, 512]); STm = sb("STm", [128, 512], BF16)
        mask_s = sb("mask_s", [128, 8]); bias_s = sb("bias_s", [128, 2])
        repT = sb("repT", [8, 128]); hselT = sb("hselT", [8, 128])
        CONST = Trk()
        tST, tSTm, tHist = Trk(), Trk(), Trk()
        hist = sb("hist", [128, 6, 3])
        text, tr64, tl64 = Trk(), Trk(), Trk()

        psum = [st.enter_context(nc.psum_tensor("ps%d" % i, [128, 1024] if i == 3 else [128, 512], BF16 if i == 3 else F32))
                for i in range(8)]
        ptrk = [Trk(excl=True) for _ in range(8)]

        def pbf(i):
            assert i == 3
            return psum[3][:]

        mset("pool", identF[:], 1.0, [CONST])
        S.op("pool", lambda e: e.affine_select(out=identF[:], in_=identF[:], pattern=[[-1, 128]], compare_op=ALU.is_equal,
                                               fill=0.0, base=0, channel_multiplier=1), [CONST], [CONST])
        cp("pool", identB[:], identF[:], [CONST], [CONST])
        mset("pool", onesF[:], 1.0, [CONST]); mset("pool", onesB[:], 1.0, [CONST])
        mset("pool", triF[:], 1.0, [CONST])
        S.op("pool", lambda e: e.affine_select(out=triF[:], in_=triF[:], pattern=[[1, 128]], compare_op=ALU.is_ge,
                                               fill=0.0, base=0, channel_multiplier=-1), [CONST], [CONST])
        mset("pool", m01F[:], 1.0, [CONST])
        S.op("pool", lambda e: e.affine_select(out=m01F[:], in_=m01F[:], pattern=[[-1, 128]], compare_op=ALU.is_gt,
                                               fill=0.0, base=0, channel_multiplier=1), [CONST], [CONST])
        cp("pool", m01B[:], m01F[:], [CONST], [CONST])
        mset("pool", bmask[:], 0.0, [CONST]); mset("pool", bmask[0:64, 0:4], 1.0, [CONST]); mset("pool", bmask[64:128, 4:8], 1.0, [CONST])
        mset("pool", onec[:], 1.0, [CONST]); mset("pool", epsc[:], EPS, [CONST]); mset("pool", scr[:], 0.0, [CONST])
        mset("pool", rhs64[:], 0.0, [CONST]); mset("pool", lhs64[:], 0.0, [CONST]); mset("pool", lhs64[0:8, :], 1.0, [CONST])
        mset("pool", ext[:], 0.0, [CONST])
        cp("pool", rhs64[32:40, :].rearrange("p (h t) -> p h t", h=8),
           identF[32:40, 32:40].unsqueeze(2).to_broadcast([8, 8, 128]), [CONST], [CONST])
        cp("pool", repT[:].rearrange("p (a q) -> p a q", q=8), identF[0:8, 0:8].unsqueeze(1).to_broadcast([8, 16, 8]), [CONST], [CONST])
        cp("pool", hselT[:].rearrange("p (a h q) -> p a h q", a=2, h=8),
           identF[0:8, 0:8].unsqueeze(1).unsqueeze(3).to_broadcast([8, 2, 8, 8]), [CONST], [CONST])
        for nm in gT:
            dma_nc("sp", gT[nm][:], W[nm].rearrange("l (c p) -> p l c", p=128), [], [CONST])
        for l_ in range(2):
            for j_ in range(4):
                dma_nc("sp", convw[:, l_, j_, :], W["conv_w"][l_, j_].rearrange("(c p) -> p c", p=128), [], [CONST])
        dma_nc("sp", convb[:], W["conv_b"].rearrange("l (c p) -> p l c", p=128), [], [CONST])

        def bc(dst, src, n):
            dma("sp", dst[:].rearrange("p l n -> p (l n)"), src.rearrange("l n -> (l n)").partition_broadcast(128), [], [CONST])

        bc(dtb_bc, W["dt_bias"], 8); bc(A_bc, W["A_log"], 8); bc(D_bc, W["D_skip"], 8); bc(sbias_bc, W["sb_bias"], 8)
        bc(qn_bc, W["q_norm"], 64); bc(kn_bc, W["k_norm"], 64)
        dma_nc("sp", sbias_c[:], W["sb_bias"].rearrange("l h -> h l"), [], [CONST])
        dma("sp", ptile[:], ptab.rearrange("a n -> (a n)").partition_broadcast(128), [], [CONST])
        S.op("pool", lambda e: e.iota(iota_c[:], pattern=[[0, 1]], base=0, channel_multiplier=1), [], [CONST])
        cp("dve", iota_f[:], iota_c[:], [CONST], [CONST])
        ts("dve", idxT[:], ptile[:], 128.0, iota_f[:, 0:1], ALU.mult, ALU.add, [CONST], [CONST])
        act(A_bc[:], A_bc[:], AF.Exp, [CONST], [CONST])
        ts("dve", A_bc[:], A_bc[:], -1.0, None, ALU.mult, None, [CONST], [CONST])
        mm(psum[0][:, 0:8], repT[:, :], m01F[0:8, 0:8], True, True, [CONST], [ptrk[0]])
        cp("dve", mask_s[:], psum[0][:, 0:8], [ptrk[0]], [CONST])
        mm(psum[0][:, 8:10], hselT[:, :], sbias_c[:, :], True, True, [CONST], [ptrk[0]])
        cp("dve", bias_s[:], psum[0][:, 8:10], [ptrk[0]], [CONST])
        with contextlib.ExitStack() as ph0:
            xld = [sb("xld", [128, 1024], F32, ph0) for _ in range(2)]
            txld = [Trk(), Trk()]
            for it, c0 in enumerate(range(0, NT, 128)):
                n = min(128, NT - c0)
                dma("sp", xld[it % 2][:n, :], xin[c0:c0 + n, :], [], [txld[it % 2]])
                for c in range(8):
                    tr(psum[1 + (c % 2)][:, :n], xld[it % 2][:n, c * 128:(c + 1) * 128], identF[:n, :n], [txld[it % 2], CONST],
                       [ptrk[1 + (c % 2)]])
                    cp("act" if c % 2 else "dve", xT[:, c, c0:c0 + n], psum[1 + (c % 2)][:, :n], [ptrk[1 + (c % 2)]], [XT(c, c0)])
            S.barrier(scr[0:1, 0:1], scr[0:1, 4:5])
        CONST.const = True

        def norm_block(blk, gname, l, hT, thT, tmp):
            b0 = blk[0][0]
            sq, tsq, rstd, trs = tmp
            for (s0, n) in blk:
                tt("pool", sq[:, :, :n], xT[:, :, s0:s0 + n], xT[:, :, s0:s0 + n], ALU.mult, xtr(s0, n), [tsq])
                for c in range(8):
                    mm(psum[7][:, :n], onesB[:, :], sq[:, c, :n], c == 0, c == 7, [tsq, CONST], [ptrk[7]])
                act(rstd[:, :n], psum[7][:, :n], AF.Ln, [ptrk[7], CONST], [trs], bias=epsc[:, 0:1], scale=1.0 / DM)
                act(rstd[:, :n], rstd[:, :n], AF.Exp, [trs], [trs], scale=-0.5)
                for c in range(8):
                    stt("dve", hT[:, c, s0 - b0:s0 - b0 + n], xT[:, c, s0:s0 + n], gT[gname][:, l, c:c + 1], rstd[:, :n],
                        ALU.mult, ALU.mult, xtr(s0, n) + [trs, CONST], [thT])

        def ffn(blk, l, which):
            gname = "norm_ffn%d" % which
            wgu_d = W["ffn%d_w_gu" % which][l].rearrange("(c p) (two f) -> p c two f", p=128, two=2)
            wdn_d = W["ffn%d_w_down" % which][l].rearrange("(j p) m -> p j m", p=128)
            b0 = blk[0][0]
            nb = sum(n for _, n in blk)
            with contextlib.ExitStack() as ph:
                hT = sb("hT_f", [128, 8, NBMAX], BF16, ph); thT = Trk()
                aT = sb("aT", [128, FC, NBMAX], BF16, ph); taT = [Trk() for _ in range(FC)]
                sq = sb("sq_f", [128, 8, SEGN], BF16, ph); rstd = sb("rstd_f", [128, SEGN], F32, ph)
                wgu = [sb("wgu%d" % i, [128, 8, 2, 128], BF16, ph) for i in range(3)]; twgu = [Trk() for _ in range(3)]
                wdn = [sb("wdn%d" % i, [128, FC, 128], BF16, ph) for i in range(2)]; twdn = [Trk() for _ in range(2)]
                sg = [sb("sg%d" % i, [128, SEGN], F32, ph) for i in range(2)]; tsg = [Trk(), Trk()]
                norm_block(blk, gname, l, hT, thT, (sq, Trk(), rstd, Trk()))

                def ld_gu(j):
                    for two_ in range(2):
                        dma("pool", wgu[j % 3][:, :, two_, :], wgu_d[:, :, two_, j * 128:(j + 1) * 128], [], [twgu[j % 3]])

                def ld_dn(m):
                    for j0 in range(0, FC, 8):
                        j1 = min(FC, j0 + 8)
                        dma("pool", wdn[m % 2][:, j0:j1, :], wdn_d[:, j0:j1, m * 128:(m + 1) * 128], [], [twdn[m % 2]])

                ld_gu(0)
                if FC > 1:
                    ld_gu(1)
                k = 0
                for j in range(FC):
                    if j + 2 < FC:
                        ld_gu(j + 2)
                    for (s0, n) in blk:
                        pg, pu = ((0, 1), (2, 6))[k % 2]
                        for c in range(8):
                            mm(psum[pg][:, :n], wgu[j % 3][:, c, 0, :], hT[:, c, s0 - b0:s0 - b0 + n], c == 0, c == 7,
                               [twgu[j % 3], thT], [ptrk[pg]])
                        for c in range(8):
                            mm(psum[pu][:, :n], wgu[j % 3][:, c, 1, :], hT[:, c, s0 - b0:s0 - b0 + n], c == 0, c == 7,
                               [twgu[j % 3], thT], [ptrk[pu]])
                        act(sg[k % 2][:, :n], psum[pg][:, :n], AF.Silu, [ptrk[pg]], [tsg[k % 2]])
                        tt("dve", aT[:, j, s0 - b0:s0 - b0 + n], sg[k % 2][:, :n], psum[pu][:, :n], ALU.mult,
                           [tsg[k % 2], ptrk[pu]], [taT[j]])
                        k += 1
                    if j == FC - 1:
                        ld_dn(0)
                        ld_dn(1)
                k = 0
                for m in range(8):
                    if m >= 1 and m + 1 < 8:
                        ld_dn(m + 1)
                    for (s0, n) in blk:
                        py = 4 + (k % 2)
                        for j in range(FC):
                            mm(psum[py][:, :n], wdn[m % 2][:, j, :], aT[:, j, s0 - b0:s0 - b0 + n], j == 0, j == FC - 1,
                               [twdn[m % 2], taT[j]], [ptrk[py]])
                        tl = [XT(m, s) for s in range((s0 // 128) * 128, s0 + n, 128)]
                        stt("dve", xT[:, m, s0:s0 + n], psum[py][:, :n], 0.5, xT[:, m, s0:s0 + n], ALU.mult, ALU.add,
                            [ptrk[py]] + tl, tl)
                        k += 1
                S.barrier(scr[0:1, 0:1], scr[0:1, 4:5])

        def rstd_small(dst, src, n_inv, reads, trk):
            T_ = dst.shape[0]
            act(dst, src, AF.Ln, reads + [CONST], [trk], bias=epsc[:T_, 0:1], scale=n_inv)
            act(dst, dst, AF.Exp, [trk], [trk], scale=-0.5)

        def mixer(blk, l, bi, KT_, tKT, Vb_, tVb):
            b0 = blk[0][0]
            nb = sum(n for _, n in blk)
            bend = b0 + nb
            win_d = W["w_in"][l].rearrange("(c p) f -> p c f", p=128)
            wo_d = W["w_out"][l].rearrange("(c p) m -> p c m", p=128)
            tiles = [(128, 128 * i, i) for i in range(NFT) if b0 <= 128 * i < bend]
            has_tail = b0 <= NFT * 128 < bend
            with contextlib.ExitStack() as ph:
                hT = sb("hT_m", [128, 8, NBMAX], BF16, ph); thT = Trk()
                KTs = sb("KTs", [128, 4, 32], BF16, ph); tKTs = Trk()
                Vn = sb("Vn", [8, 4, 512], BF16, ph); tVn = Trk()
                with contextlib.ExitStack() as phn:
                    sqn = sb("sq_m", [128, 8, SEGN], BF16, phn); rstdn = sb("rstd_m", [128, SEGN], F32, phn)
                    norm_block(blk, "norm_mix", l, hT, thT, (sqn, Trk(), rstdn, Trk()))
                    S.barrier(scr[0:1, 0:1], scr[0:1, 4:5])

                def mk_w(stack, ncols, c0, half_):
                    wb = sb("wb", [128, 8, ncols], BF16, stack); twb = Trk()
                    for c in range(0, 8, 2):
                        for f0 in range(0, ncols, 512):
                            fn_ = min(512, ncols - f0)
                            dma("pool", wb[:, c:c + 2, f0:f0 + fn_], win_d[:, c:c + 2, c0 + f0:c0 + f0 + fn_], [], [twb])
                    wo = None; two = None
                    if half_ is not None:
                        wo = sb("wo", [128, 4, DM], BF16, stack); two = Trk()
                        for c in range(0, 4, 2):
                            for f0 in range(0, DM, 512):
                                dma("pool", wo[:, c:c + 2, f0:f0 + 512], wo_d[:, half_ * 4 + c:half_ * 4 + c + 2, f0:f0 + 512], [], [two])
                    return wb, twb, wo, two

                def proj_tm(wb, twb, pb, T, hc, wc0, wn):
                    for c in range(8):
                        mm(psum[pb][:T, :wn], hT[:, c, hc:hc + T], wb[:, c, wc0:wc0 + wn], c == 0, c == 7, [thT, twb], [ptrk[pb]])

                def wout_add(wo, two, ym, tym, T, g0):
                    for m in range(8):
                        pb = 6 + (m // 4)
                        for j in range(4):
                            mm(psum[pb][:, (m % 4) * 128:(m % 4) * 128 + T], wo[:, j, m * 128:(m + 1) * 128], ym[:, j, :T],
                               j == 0, j == 3, [two, tym], [ptrk[pb]])
                    for hm in range(2):
                        tl = [XT(m, g0) for m in range(4 * hm, 4 * hm + 4)]
                        tt("dve", xT[:, 4 * hm:4 * hm + 4, g0:g0 + T], xT[:, 4 * hm:4 * hm + 4, g0:g0 + T],
                           psum[6 + hm][:].rearrange("p (m t) -> p m t", m=4)[:, :, :T], ALU.add, [ptrk[6 + hm]] + tl, tl)

                def load_bc(stack, name, src):
                    t_ = sb(name, [128, 512], F32, stack)
                    tk = Trk()
                    dma("sp", t_[:], src.partition_broadcast(128), [], [tk])
                    return t_, tk

                with contextlib.ExitStack() as sp1:
                    wb, twb, wo, two = mk_w(sp1, 1288, 0, 0)
                    ssdn, tssdn = load_bc(sp1, "ssdn", W["ssd_norm"][l])
                    xr = sb("xr", [128, 6, 131], F32, sp1); txr = Trk()
                    acc = sb("acc", [128, 6, 128], F32, sp1); tacc = Trk()
                    xc = sb("xc", [128, 6, 128], F32, sp1); txc = Trk()
                    zs = sb("zs", [128, 512], F32, sp1); tzs = Trk()
                    dt = sb("dt", [128, 8], F32, sp1); tdt = Trk()
                    xtm = sb("xtm", [128, 512], F32, sp1); txtm = Trk()
                    Btm = sb("Btm", [128, 128], BF16, sp1); tBtm = Trk()
                    CTb = sb("CTb", [128, 128], BF16, sp1); BTb = sb("BTb", [128, 128], BF16, sp1); tCB = Trk()
                    cs_sb = sb("cs_sb", [128, 8], F32, sp1); ecs = sb("ecs", [128, 8], F32, sp1); te = sb("te", [128, 8], F32, sp1)
                    dtw = sb("dtw", [128, 8], F32, sp1); dec = sb("dec", [128, 8], F32, sp1); tsm = Trk()
                    dmt = sb("dmt", [128, 8, 128], F32, sp1); tdm = Trk()
                    cbm = sb("cbm", [128, 2, 128], F32, sp1); tcbm = Trk()
                    sc = sb("sc", [128, 8, 128], BF16, sp1); tsc = Trk()
                    xdt = sb("xdt", [128, 512], BF16, sp1); xw = sb("xw", [128, 512], BF16, sp1); txd = Trk()
                    y1 = sb("y1", [128, 512], F32, sp1); y2 = sb("y2", [128, 512], F32, sp1); ty = Trk()
                    ssq = sb("ssq", [128, 2], F32, sp1); tssq = Trk()
                    yb = sb("yb", [128, 512], BF16, sp1); tyb = Trk()
                    ym = sb("ym", [128, 4, 128], BF16, sp1); tym = Trk()
                    tmpS = sb("tmpS", [128, 4, 128], F32, sp1); ttmpS = Trk()
                    so = sb("so", [128, 4, 64], F32, sp1); tso = Trk()
                    hst = sb("hst", [128, 3, 6], F32, sp1); thst = Trk()

                    def ssd_tile(T, g0):
                        hc = g0 - b0
                        if SUB < 1:
                            return
                        for ch in range(6):
                            pb = 0 if ch < 4 else 1
                            o = (ch % 4) * 128
                            for c in range(8):
                                mm(psum[pb][:, o:o + T], wb[:, c, 512 + ch * 128:512 + (ch + 1) * 128], hT[:, c, hc:hc + T],
                                   c == 0, c == 7, [thT, twb], [ptrk[pb]])
                        cp("act", xr[:, 0:4, 3:3 + T], psum[0][:].rearrange("p (a t) -> p a t", a=4)[:, :, :T], [ptrk[0]], [txr])
                        cp("act", xr[:, 4:6, 3:3 + T], psum[1][:].rearrange("p (a t) -> p a t", a=4)[:, 0:2, :T], [ptrk[1]], [txr])
                        cp("pool", xr[:, :, 0:3], hist[:], [tHist], [txr])
                        if SUB < 2:
                            return
                        proj_tm(wb, twb, 2, T, hc, 0, 512)
                        act(zs[:T, :], psum[2][:T, :], AF.Silu, [ptrk[2]], [tzs])
                        for c in range(8):
                            mm(psum[7][:T, 16:24], hT[:, c, hc:hc + T], wb[:, c, 1280:1288], c == 0, c == 7, [thT, twb], [ptrk[7]])
                        tt("dve", dt[:T, :], psum[7][:T, 16:24], dtb_bc[:T, l, :], ALU.add, [ptrk[7], CONST], [tdt])
                        act(dt[:T, :], dt[:T, :], AF.Exp, [tdt], [tdt])
                        act(dt[:T, :], dt[:T, :], AF.Ln, [tdt], [tdt], bias=1.0)
                        if SUB < 3:
                            return
                        for ch in range(6):
                            ts("pool", acc[:, ch, :T], xr[:, ch, 0:T], convw[:, l, 0, ch:ch + 1], convb[:, l, ch:ch + 1], ALU.mult, ALU.add,
                               [txr, CONST], [tacc])
                            for j in range(1, 4):
                                stt("dve", acc[:, ch, :T], xr[:, ch, j:j + T], convw[:, l, j, ch:ch + 1], acc[:, ch, :T], ALU.mult, ALU.add,
                                    [txr, CONST, tacc], [tacc])
                        act(xc[:, :, :T], acc[:, :, :T], AF.Silu, [tacc], [txc])
                        cp("pool", hist[:], xr[:, :, T:T + 3], [txr], [tHist])
                        if SUB < 4:
                            return
                        for ch in range(4):
                            tr(psum[4][:T, ch * 128:(ch + 1) * 128], xc[:, ch, :T], identF[:, :], [txc, CONST], [ptrk[4]])
                        tr(psum[7][:T, 160:288], xc[:, 4, :T], identF[:, :], [txc, CONST], [ptrk[7]])
                        cp("act", xtm[:T, :], psum[4][:T, :], [ptrk[4]], [txtm])
                        cp("dve", Btm[:T, :], psum[7][:T, 160:288], [ptrk[7]], [tBtm])
                        cp("pool", BTb[:, :T], xc[:, 4, :T], [txc], [tCB])
                        cp("pool", CTb[:, :T], xc[:, 5, :T], [txc], [tCB])
                        if SUB < 5:
                            return
                        tt("dve", ext[:T, 0:8], dt[:T, :], A_bc[:T, l, :], ALU.mult, [tdt, CONST], [text])
                        ts("dve", ext[:T, 32:40], ext[:T, 0:8], -1.0, None, ALU.mult, None, [text], [text])
                        mm(psum[7][:T, 0:8], triF[:T, :T], ext[:T, 0:8], True, True, [text, CONST], [ptrk[7]])
                        mm(psum[7][:, 8:16], onesF[:T, :], ext[:T, 0:8], True, True, [text, CONST], [ptrk[7]])
                        mm(psum[7][0:40, 32:32 + T], ext[:T, 0:40], triF[:T, :T], True, True, [text, CONST], [ptrk[7]])
                        if SUB < 6:
                            return
                        cp("dve", cs_sb[:T, :], psum[7][:T, 0:8], [ptrk[7]], [tsm])
                        act(ecs[:T, :], psum[7][:T, 0:8], AF.Exp, [ptrk[7]], [tsm])
                        tt("dve", te[:T, :], psum[7][:T, 8:16], cs_sb[:T, :], ALU.subtract, [ptrk[7], tsm], [tsm])
                        act(te[:T, :], te[:T, :], AF.Exp, [tsm], [tsm])
                        tt("dve", dtw[:T, :], dt[:T, :], te[:T, :], ALU.mult, [tdt, tsm], [tsm])
                        act(dec[:, :], psum[7][:, 8:16], AF.Exp, [ptrk[7]], [tsm])
                        if SUB < 7:
                            return
                        cp("act", lhs64[32:40, :T], psum[7][32:40, 32:32 + T], [ptrk[7]], [tl64])
                        tt("dve", rhs64[0:8, :].rearrange("p (h t) -> p h t", h=8)[:, :, :T],
                           psum[7][0:8, 32:32 + T].unsqueeze(1).to_broadcast([8, 8, T]),
                           identF[0:8, 0:8].unsqueeze(2).to_broadcast([8, 8, T]), ALU.mult, [ptrk[7], CONST], [tr64])
                        for h in range(8):
                            pb = h // 4
                            mm(psum[pb][:T, (h % 4) * 128:(h % 4) * 128 + T], lhs64[:, :T], rhs64[:, h * 128:h * 128 + T], True, True,
                               [tl64, tr64], [ptrk[pb]])
                        if SUB < 8:
                            return
                        for g in range(2):
                            ts("dve", dmt[:T, 4 * g:4 * g + 4, :T], psum[g][:T, :].rearrange("p (a t) -> p a t", a=4)[:, :, :T], 0.0, None,
                               ALU.min, None, [ptrk[g]], [tdm])
                        act(dmt[:T, :, :T], dmt[:T, :, :T], AF.Exp, [tdm], [tdm])
                        for g in range(2):
                            mm(psum[5][:T, g * 128:g * 128 + T], BTb[g * 64:(g + 1) * 64, :T], CTb[g * 64:(g + 1) * 64, :T], True, True,
                               [tCB], [ptrk[5]])
                        tt("dve", cbm[:T, :, :T], psum[5][:T, 0:256].rearrange("p (g t) -> p g t", g=2)[:, :, :T],
                           triF[:T, :T].unsqueeze(1).to_broadcast([T, 2, T]), ALU.mult, [ptrk[5], CONST], [tcbm])
                        for g in range(2):
                            tt("dve", sc[:T, 4 * g:4 * g + 4, :T], dmt[:T, 4 * g:4 * g + 4, :T],
                               cbm[:T, g:g + 1, :T].to_broadcast([T, 4, T]), ALU.mult, [tdm, tcbm], [tsc])
                        if SUB < 9:
                            return
                        tt("dve", xdt[:T, :].rearrange("p (h d) -> p h d", h=8), xtm[:T, :].rearrange("p (h d) -> p h d", h=8),
                           dt[:T, :].unsqueeze(2).to_broadcast([T, 8, 64]), ALU.mult, [txtm, tdt], [txd])
                        tt("dve", xw[:T, :].rearrange("p (h d) -> p h d", h=8), xtm[:T, :].rearrange("p (h d) -> p h d", h=8),
                           dtw[:T, :].unsqueeze(2).to_broadcast([T, 8, 64]), ALU.mult, [txtm, tsm], [txd])
                        for h in range(8):
                            mm(psum[2][:T, h * 64:(h + 1) * 64], sc[:T, h, :T], xdt[:T, h * 64:(h + 1) * 64], True, True, [tsc, txd], [ptrk[2]])
                        mm(psum[4][:T, :], CTb[:, :T], STm[:, :], True, True, [tCB, tSTm], [ptrk[4]])
                        tt("dve", y1[:T, :].rearrange("p (h d) -> p h d", h=8), psum[4][:T, :].rearrange("p (h d) -> p h d", h=8),
                           ecs[:T, :].unsqueeze(2).to_broadcast([T, 8, 64]), ALU.mult, [ptrk[4], tsm], [ty])
                        tt("dve", y2[:T, :], y1[:T, :], psum[2][:T, :], ALU.add, [ty, ptrk[2]], [ty])
                        if SUB < 10:
                            return
                        mm(psum[5][:, :], Btm[:T, :], xw[:T, :], True, True, [tBtm, txd], [ptrk[5]])
                        tt("dve", ST[:].rearrange("p (h d) -> p h d", h=8), ST[:].rearrange("p (h d) -> p h d", h=8),
                           dec[:, :].unsqueeze(2).to_broadcast([128, 8, 64]), ALU.mult, [tST, tsm], [tST])
                        tt("dve", ST[:], ST[:], psum[5][:, :], ALU.add, [tST, ptrk[5]], [tST])
                        tt("pool", STm[:].rearrange("p (h d) -> p h d", h=8), ST[:].rearrange("p (h d) -> p h d", h=8),
                           bmask[:, :].unsqueeze(2).to_broadcast([128, 8, 64]), ALU.mult, [tST, CONST], [tSTm])
                        if SUB < 11:
                            return
                        tt("dve", y1[:T, :].rearrange("p (h d) -> p h d", h=8), xtm[:T, :].rearrange("p (h d) -> p h d", h=8),
                           D_bc[:T, l, :].unsqueeze(2).to_broadcast([T, 8, 64]), ALU.mult, [txtm, CONST, ty], [ty])
                        tt("dve", y2[:T, :], y2[:T, :], y1[:T, :], ALU.add, [ty], [ty])
                        tt("dve", y2[:T, :], y2[:T, :], zs[:T, :], ALU.mult, [ty, tzs], [ty])
                        tt("pool", y1[:T, :], y2[:T, :], y2[:T, :], ALU.mult, [ty], [ty])
                        rsum("dve", ssq[:T, :], y1[:T, :].rearrange("p (g d) -> p g d", g=2), [ty], [tssq])
                        rstd_small(ssq[:T, :], ssq[:T, :], 1.0 / 256, [tssq], tssq)
                        tt("dve", y2[:T, :].rearrange("p (g d) -> p g d", g=2), y2[:T, :].rearrange("p (g d) -> p g d", g=2),
                           ssq[:T, :].unsqueeze(2).to_broadcast([T, 2, 256]), ALU.mult, [ty, tssq], [ty])
                        tt("dve", yb[:T, :], y2[:T, :], ssdn[:T, :], ALU.mult, [ty, tssdn], [tyb])
                        for j in range(4):
                            tr(pbf(3)[:, j * 128:j * 128 + T], yb[:T, j * 128:(j + 1) * 128], identB[:T, :T], [tyb, CONST], [ptrk[3]])
                        cp("act", ym[:, :, :T], pbf(3)[:, 0:512].rearrange("p (j t) -> p j t", j=4)[:, :, :T], [ptrk[3]], [tym])
                        wout_add(wo, two, ym, tym, T, g0)


                    def state_out(dst):
                        if "a" in DBG:
                            return
                        for j in range(4):
                            tr(psum[4][:, j * 128:(j + 1) * 128], ST[:, j * 128:(j + 1) * 128], identF[:, :], [tST, CONST], [ptrk[4]])
                        for j in range(4):
                            g = j // 2
                            cp("act", so[:, j, :], psum[4][:, j * 128 + g * 64:j * 128 + g * 64 + 64], [ptrk[4]], [tso])
                        dma("sp", dst.rearrange("(j hh) p n -> (hh p) j n", hh=2), so[:], [tso], [])

                    def conv_out(dst):
                        if "b" in DBG:
                            return
                        for j_ in range(3):
                            dma_nc("sp", dst[j_].rearrange("(c p) -> p c", p=128), hist[:, :, j_], [tHist], [])

                    if bi == 0:
                        mset("pool", ST[:], 0.0, [tST]); mset("pool", STm[:], 0.0, [tSTm]); mset("pool", hist[:], 0.0, [tHist])
                    for (T, g0, i) in tiles:
                        ssd_tile(T, g0)
                    if has_tail:
                        ssd_tile(16, NFT * 128)
                        state_out(ssmp_o[l])
                        conv_out(convp_o[l])
                        for b in range(0 if "c" in DBG else 4):
                            if "d" not in DBG:
                                mset("pool", tmpS[:], 0.0, [ttmpS])
                                for j in range(4):
                                    g = j // 2
                                    for hh in range(2):
                                        dma("sp", tmpS[hh * 64:(hh + 1) * 64, j, g * 64:(g + 1) * 64], sssm[l, b, 2 * j + hh], [], [ttmpS])
                                if "f" not in DBG:
                                    for j in range(4):
                                        tr(psum[5][:, j * 128:(j + 1) * 128], tmpS[:, j, :], identF[:, :], [ttmpS, CONST], [ptrk[5]])
                                    cp("dve", ST[:], psum[5][:, :], [ptrk[5]], [tST])
                                    if "g" not in DBG:
                                        cp("act", STm[:], psum[5][:, :], [ptrk[5]], [tSTm])
                            if "e" not in DBG:
                                for j_ in range(3):
                                    dma_nc("sp", hst[:, j_, :], sconv[l, b, j_].rearrange("(c p) -> p c", p=128), [], [thst])
                                cp("pool", hist[:].rearrange("p c j -> p j c"), hst[:], [thst], [tHist])
                            ssd_tile(8, NPOS + 8 * b)
                            state_out(ssms_o[l, b])
                            conv_out(convs_o[l, b])
                    S.barrier(scr[0:1, 0:1], scr[0:1, 4:5])

                def mk_norm_bufs(stack):
                    d = {}
                    d["raw"] = sb("raw", [128, 512], F32, stack); d["traw"] = Trk()
                    d["sq2"] = sb("sq2", [128, 512], F32, stack); d["tsq2"] = Trk()
                    d["s8"] = sb("s8", [128, 8], F32, stack); d["ts8"] = Trk()
                    d["nrm"] = sb("nrm", [128, 512], F32, stack); d["tnrm"] = Trk()
                    d["nb16"] = sb("nb16", [128, 512], BF16, stack); d["tnb"] = Trk()
                    return d

                def qk_norm(d, pb, T, gbc):
                    raw, traw, sq2, tsq2, s8, ts8, nrm, tnrm, nb16, tnb = (d[k_] for k_ in
                        ("raw", "traw", "sq2", "tsq2", "s8", "ts8", "nrm", "tnrm", "nb16", "tnb"))
                    cp("act", raw[:T, :], psum[pb][:T, :], [ptrk[pb]], [traw])
                    tt("pool", sq2[:T, :], raw[:T, :], raw[:T, :], ALU.mult, [traw], [tsq2])
                    rsum("dve", s8[:T, :], sq2[:T, :].rearrange("p (h d) -> p h d", h=8), [tsq2], [ts8])
                    rstd_small(s8[:T, :], s8[:T, :], 1.0 / 64, [ts8], ts8)
                    tt("dve", nrm[:T, :].rearrange("p (h d) -> p h d", h=8), raw[:T, :].rearrange("p (h d) -> p h d", h=8),
                       s8[:T, :].unsqueeze(2).to_broadcast([T, 8, 64]), ALU.mult, [traw, ts8], [tnrm])
                    tt("dve", nrm[:T, :].rearrange("p (h d) -> p h d", h=8), nrm[:T, :].rearrange("p (h d) -> p h d", h=8),
                       gbc[:T, l, :].unsqueeze(1).to_broadcast([T, 8, 64]), ALU.mult, [tnrm, CONST], [tnrm])
                    cp("pool", nb16[:T, :], nrm[:T, :], [tnrm], [tnb])

                def to_fm(d, dst, tdst, T, c0):
                    nb16, tnb = d["nb16"], d["tnb"]
                    for j in range(4):
                        tr(pbf(3)[:, j * 128:j * 128 + T], nb16[:T, j * 128:(j + 1) * 128], identB[:T, :T], [tnb, CONST], [ptrk[3]])
                    cp("act", dst[:, :, c0:c0 + T], pbf(3)[:, 0:512].rearrange("p (j t) -> p j t", j=4)[:, :, :T], [ptrk[3]], [tdst])

                if not allow(l, 3):
                    return
                with contextlib.ExitStack() as sp2:
                    wb, twb, _, _ = mk_w(sp2, 1024, 1800, None)
                    d = mk_norm_bufs(sp2)
                    vraw = sb("vraw", [128, 512], F32, sp2); tvraw = Trk()

                    def kv_tile(T, g0, kt_idx, sample):
                        hc = g0 - b0
                        proj_tm(wb, twb, 0, T, hc, 0, 512)
                        qk_norm(d, 0, T, kn_bc)
                        if sample:
                            dma("sp", ks_o[l, :, :], d["nrm"][:T, :], [d["tnrm"]], [])
                            to_fm(d, KTs, tKTs, T, 0)
                            for b in range(4):
                                for c in range(8):
                                    mm(psum[1][0:8, :], hT[:, c, hc + 8 * b:hc + 8 * b + 8], wb[:, c, 512:1024], c == 0, c == 7,
                                       [thT, twb], [ptrk[1]])
                                cp("act", vraw[0:8, :], psum[1][0:8, :], [ptrk[1]], [tvraw])
                                dma("sp", vs_o[l, 8 * b:8 * b + 8, :], vraw[0:8, :], [tvraw], [])
                                cp("pool", Vn[0:8, b, :], vraw[0:8, :], [tvraw], [tVn])
                        else:
                            dma("sp", kp_o[l, g0:g0 + T, :], d["nrm"][:T, :], [d["tnrm"]], [])
                            to_fm(d, KT_, tKT[kt_idx], T, g0)
                            proj_tm(wb, twb, 1, T, hc, 512, 512)
                            cp("act", vraw[:T, :], psum[1][:T, :], [ptrk[1]], [tvraw])
                            dma("sp", vp_o[l, g0:g0 + T, :], vraw[:T, :], [tvraw], [])
                            cp("pool", Vb_[:T, kt_idx, :], vraw[:T, :], [tvraw], [tVb[kt_idx]])

                    for (T, g0, i) in tiles:
                        kv_tile(T, g0, i, False)
                    if has_tail:
                        kv_tile(16, NFT * 128, NFT, False)
                        kv_tile(32, NPOS, None, True)
                    S.barrier(scr[0:1, 0:1], scr[0:1, 4:5])

                if not allow(l, 4):
                    return
                with contextlib.ExitStack() as sp3:
                    wb, twb, wo, two = mk_w(sp3, 512, 1288, 1)
                    sbon, tsbon = load_bc(sp3, "sbon", W["sb_out_norm"][l])
                    d = mk_norm_bufs(sp3)
                    raw, traw, sq2, tsq2, s8, ts8, nb16, tnb = (d[k_] for k_ in ("raw", "traw", "sq2", "tsq2", "s8", "ts8", "nb16", "tnb"))
                    QT = sb("QT", [128, 4, 128], BF16, sp3); tQT = Trk()
                    eb = sb("eb", [128, 513], F32, sp3); Sb = sb("Sb", [128, 513], F32, sp3); Fx = sb("Fx", [128, 513], F32, sp3)
                    te_, tS_, tF_ = Trk(), Trk(), Trk()
                    ncar = sb("ncar", [128, 1], F32, sp3); tnc = Trk()
                    wq = sb("wq", [128, 512], BF16, sp3); twq = Trk()
                    wT = [sb("wT%d" % i_, [128, 4, 128], BF16, sp3) for i_ in range(2)]; twT = [Trk(), Trk()]
                    ym2 = sb("ym2", [128, 4, 128], BF16, sp3); tym2 = Trk()

                    def sb_chunk(R, Wn, pz, bias_ap, diag_mask, first):
                        act(eb[:R, :Wn], psum[pz][:R, :Wn], AF.Exp, [ptrk[pz], CONST], [te_], bias=bias_ap, scale=0.125)
                        if first:
                            mset("pool", ncar[:R, :], 0.0, [tnc])
                        mset("pool", Sb[:R, 0:1], 0.0, [tS_])
                        act(Sb[:R, 1:Wn + 1], eb[:R, :Wn], AF.Ln, [te_, tS_], [tS_], bias=1.0)
                        if diag_mask is not None:
                            dm_ap, dw = diag_mask
                            tt("pool", Sb[:R, 1 + Wn - dw:1 + Wn], Sb[:R, 1 + Wn - dw:1 + Wn], dm_ap, ALU.mult, [tS_, CONST], [tS_])
                        S.op("dve", lambda e: e.tensor_tensor_scan(out=Fx[:R, :Wn + 1], data0=onec[:R, 0:1].to_broadcast([R, Wn + 1]),
                                                                   data1=Sb[:R, :Wn + 1], initial=0.0, op0=ALU.mult, op1=ALU.add),
                             [tS_, CONST], [tF_])
                        tt("dve", ncar[:R, :], ncar[:R, :], Fx[:R, Wn:Wn + 1], ALU.subtract, [tnc, tF_], [tnc])
                        act(Sb[:R, :Wn], Fx[:R, :Wn], AF.Exp, [tF_, tnc, tS_], [tS_], bias=ncar[:R, 0:1])
                        tt("dve", wq[:R, :Wn], eb[:R, :Wn], Sb[:R, :Wn], ALU.mult, [te_, tS_], [twq])
                        if diag_mask is not None:
                            dm_ap, dw = diag_mask
                            tt("pool", wq[:R, Wn - dw:Wn], wq[:R, Wn - dw:Wn], dm_ap, ALU.mult, [twq, CONST], [twq])

                    def attn_tile(T, g0, i):
                        hc = g0 - b0
                        proj_tm(wb, twb, 0, T, hc, 0, 512)
                        qk_norm(d, 0, T, qn_bc)
                        to_fm(d, QT, tQT, T, 0)
                        nk = g0 + T
                        nck = (nk + KCH - 1) // KCH
                        nt_total = (nk + 127) // 128
                        k = 0
                        for h in range(8):
                            hp, hh = h // 2, h % 2
                            done = 0
                            for c in reversed(range(nck)):
                                k0 = c * KCH
                                Wn = min(KCH, nk - k0)
                                pz = 1 + (k % 2)
                                wts = wT[k % 2]; twts = twT[k % 2]
                                k += 1
                                ktl = [tKT[t_] for t_ in range(k0 // 128, (k0 + Wn + 127) // 128)]
                                mm(psum[pz][:T, :Wn], QT[hh * 64:(hh + 1) * 64, hp, :T], KT_[hh * 64:(hh + 1) * 64, hp, k0:k0 + Wn], True, True,
                                   [tQT] + ktl, [ptrk[pz]])
                                dmk = (m01F[:T, :T], T) if c == nck - 1 else None
                                sb_chunk(T, Wn, pz, sbias_bc[:T, l, h:h + 1], dmk, c == nck - 1)
                                njt = (Wn + 127) // 128
                                for j in range(njt):
                                    ksz = min(128, Wn - 128 * j)
                                    tr(pbf(3)[:ksz, j * 128:j * 128 + T], wq[:T, 128 * j:128 * j + ksz], identB[:T, :T], [twq, CONST], [ptrk[3]])
                                for j in range(njt):
                                    ksz = min(128, Wn - 128 * j)
                                    cp("act" if j % 2 else "dve", wts[:ksz, j, :T], pbf(3)[:ksz, j * 128:j * 128 + T], [ptrk[3]], [twts])
                                for j in range(njt):
                                    ksz = min(128, Wn - 128 * j)
                                    kt = k0 // 128 + j
                                    done += 1
                                    mm(psum[4][:T, h * 64:(h + 1) * 64], wts[:ksz, j, :T], Vb_[:ksz, kt, h * 64:(h + 1) * 64],
                                       done == 1, done == nt_total, [twts, tVb[kt]], [ptrk[4]])
                        cp("act", raw[:T, :], psum[4][:T, :], [ptrk[4]], [traw])
                        tt("pool", sq2[:T, :], raw[:T, :], raw[:T, :], ALU.mult, [traw], [tsq2])
                        rsum("dve", s8[:T, 0:1], sq2[:T, :], [tsq2], [ts8])
                        rstd_small(s8[:T, 0:1], s8[:T, 0:1], 1.0 / 512, [ts8], ts8)
                        stt("dve", nb16[:T, :], raw[:T, :], s8[:T, 0:1], sbon[:T, :], ALU.mult, ALU.mult, [traw, ts8, tsbon], [tnb])
                        to_fm(d, ym2, tym2, T, 0)
                        wout_add(wo, two, ym2, tym2, T, g0)

                    def sample_attn():
                        with contextlib.ExitStack() as sp4:
                            Lq = sb("Lq", [128, 4, 4, 128], BF16, sp4); tLq = Trk()
                            osb = sb("osb", [8, 2, 512], F32, sp4); tosb = Trk()
                            Kbf = sb("Kbf", [128, 2, PGC, 512], BF16, sp4); tKbf = Trk()
                            Vbs = sb("Vbs", [128, 2, PGC, 512], BF16, sp4); tVbs = Trk()
                            hc = NPOS - b0
                            proj_tm(wb, twb, 0, 32, hc, 0, 512)
                            qk_norm(d, 0, 32, qn_bc)
                            to_fm(d, QT, tQT, 32, 0)
                            mset("pool", Lq[:], 0.0, [tLq])
                            for b in range(4):
                                bb = b % 2
                                for hp in range(4):
                                    for hh in range(2):
                                        h = 2 * hp + hh
                                        cp("pool", Lq[hh * 64:(hh + 1) * 64, b, hp, bb * 64 + h * 8:bb * 64 + h * 8 + 8],
                                           QT[hh * 64:(hh + 1) * 64, hp, 8 * b:8 * b + 8], [tQT, tLq], [tLq])
                            for pr in range(2):
                                bs = (2 * pr, 2 * pr + 1)
                                n = 0
                                for bb, b in enumerate(bs):
                                    for hp in range(4):
                                        mm(psum[1][:, 0:8], Lq[:, b, hp, :], KTs[:, hp, 8 * b:8 * b + 8], n == 0, n == 7, [tLq, tKTs], [ptrk[1]])
                                        n += 1
                                sb_chunk(128, 8, 1, bias_s[:, l:l + 1], (mask_s[:, :], 8), True)
                                if "w" in DBG and pr == 0 and l == 0:
                                    dbgt = sb("dbgt", [128, 16], F32, sp4); tdbg = Trk()
                                    cp("dve", dbgt[:, 0:8], wq[:, 0:8], [twq], [tdbg])
                                    cp("dve", dbgt[:, 8:16], eb[:, 0:8], [te_, tdbg], [tdbg])
                                    dma("sp", ks_o[1].rearrange("t (a c) -> (t a) c", c=16)[0:128, :], dbgt[:, :], [tdbg], [])
                                tr(pbf(3)[0:8, 0:128], wq[:, 0:8], identB[:, :], [twq, CONST], [ptrk[3]])
                                cp("dve", wT[0][0:8, 0, :], pbf(3)[0:8, 0:128], [ptrk[3]], [twT[0]])
                                for bb, b in enumerate(bs):
                                    for h in range(8):
                                        mm(psum[4 + bb][0:8, h * 64:(h + 1) * 64], wT[0][0:8, 0, bb * 64 + h * 8:bb * 64 + h * 8 + 8],
                                           Vn[0:8, b, h * 64:(h + 1) * 64], h == 0, False, [twT[0], tVn], [ptrk[4 + bb]])
                                k = 1
                                for cch in reversed(range(NCH_S)):
                                    for bb, b in enumerate(bs):
                                        for j in range(PGC):
                                            col = b * NPG + cch * PGC + j
                                            S.op("pool", lambda e, bb=bb, j=j, col=col: e.indirect_dma_start(
                                                out=Kbf[:, bb, j, :], out_offset=None, in_=ckT[l].rearrange("n p f -> (n p) f"),
                                                in_offset=bass.IndirectOffsetOnAxis(ap=idxT[:, col:col + 1], axis=0)), [CONST], [tKbf], dma=True)
                                            S.op("pool", lambda e, bb=bb, j=j, col=col: e.indirect_dma_start(
                                                out=Vbs[:, bb, j, :], out_offset=None, in_=cv[l].rearrange("n p f -> (n p) f"),
                                                in_offset=bass.IndirectOffsetOnAxis(ap=idxT[:, col:col + 1], axis=0)), [CONST], [tVbs], dma=True)
                                    Wn = PGC * 128
                                    pz = 1 + (k % 2)
                                    wts = wT[k % 2]; twts = twT[k % 2]
                                    k += 1
                                    for j in range(PGC):
                                        n = 0
                                        for bb, b in enumerate(bs):
                                            for hp in range(4):
                                                mm(psum[pz][:, j * 128:(j + 1) * 128], Lq[:, b, hp, :], Kbf[:, bb, j, hp * 128:(hp + 1) * 128],
                                                   n == 0, n == 7, [tLq, tKbf], [ptrk[pz]])
                                                n += 1
                                    sb_chunk(128, Wn, pz, bias_s[:, l:l + 1], None, False)
                                    for j in range(PGC):
                                        tr(pbf(3)[:, j * 128:(j + 1) * 128], wq[:, 128 * j:128 * (j + 1)], identB[:, :], [twq, CONST], [ptrk[3]])
                                    for j in range(PGC):
                                        cp("act" if j % 2 else "dve", wts[:, j, :], pbf(3)[:, j * 128:(j + 1) * 128], [ptrk[3]], [twts])
                                    for j in range(PGC):
                                        lastmm = (cch == 0 and j == PGC - 1)
                                        for bb, b in enumerate(bs):
                                            for h in range(8):
                                                mm(psum[4 + bb][0:8, h * 64:(h + 1) * 64], wts[:, j, bb * 64 + h * 8:bb * 64 + h * 8 + 8],
                                                   Vbs[:, bb, j, h * 64:(h + 1) * 64], False, lastmm, [twts, tVbs], [ptrk[4 + bb]])
                                for bb, b in enumerate(bs):
                                    cp("act", osb[0:8, bb, :], psum[4 + bb][0:8, :], [ptrk[4 + bb]], [tosb])
                                if "o" in DBG and pr == 0 and l == 0:
                                    dma("sp", vs_o[1].rearrange("(t a) c -> t (a c)", a=2)[0:8, :], osb[0:8, :, :].rearrange("p a c -> p (a c)"), [tosb], [])
                                for bb, b in enumerate(bs):
                                    tt("pool", sq2[0:8, :], osb[0:8, bb, :], osb[0:8, bb, :], ALU.mult, [tosb, tsq2], [tsq2])
                                    rsum("dve", s8[0:8, bb:bb + 1], sq2[0:8, :], [tsq2, ts8], [ts8])
                                rstd_small(s8[0:8, 0:2], s8[0:8, 0:2], 1.0 / 512, [ts8], ts8)
                                for bb, b in enumerate(bs):
                                    stt("dve", nb16[0:8, :], osb[0:8, bb, :], s8[0:8, bb:bb + 1], sbon[0:8, :], ALU.mult, ALU.mult,
                                        [tosb, ts8, tsbon, tnb], [tnb])
                                    for j in range(4):
                                        tr(pbf(3)[:, j * 128 + 8 * bb:j * 128 + 8 * bb + 8], nb16[0:8, j * 128:(j + 1) * 128], identB[0:8, 0:8],
                                           [tnb, CONST], [ptrk[3]])
                                cp("act", ym2[:, :, 16 * pr:16 * pr + 16], pbf(3)[:, 0:512].rearrange("p (j t) -> p j t", j=4)[:, :, 0:16],
                                   [ptrk[3]], [tym2])
                            wout_add(wo, two, ym2, tym2, 32, NPOS)
                            S.barrier(scr[0:1, 0:1], scr[0:1, 4:5])

                    for (T, g0, i) in tiles:
                        attn_tile(T, g0, i)
                    if has_tail:
                        attn_tile(16, NFT * 128, NFT)
                        S.barrier(scr[0:1, 0:1], scr[0:1, 4:5])
                        if allow(l, 5):
                            sample_attn()
                    S.barrier(scr[0:1, 0:1], scr[0:1, 4:5])

        STOP = cfg.get("stop", 99)
        SUB = cfg.get("sub", 99)
        DBG = cfg.get("dbg", "")

        def allow(l, code):
            return 10 * l + code <= STOP

        for l in range(2):
            for blk in blocks:
                if allow(l, 1):
                    ffn(blk, l, 1)
            with contextlib.ExitStack() as lay:
                KTt = sb("KT", [128, 4, NPOS], BF16, lay)
                Vbt = sb("Vb", [128, NFT + 1, 512], BF16, lay)
                tKT = [Trk() for _ in range(NFT + 1)]
                tVb = [Trk() for _ in range(NFT + 1)]
                for bi, blk in enumerate(blocks):
                    if allow(l, 2):
                        mixer(blk, l, bi, KTt, tKT, Vbt, tVb)
            for blk in blocks:
                if allow(l, 6):
                    ffn(blk, l, 2)
        if "m" in DBG:
            dma("sp", ks_o[1].rearrange("t (a c) -> (t a) c", c=8)[0:128, :], mask_s[:, :], [CONST], [])
            dma("sp", vs_o[1].rearrange("t (a c) -> (t a) c", c=2)[0:128, :], bias_s[:, :], [CONST], [])
        with contextlib.ExitStack() as phy:
            yst = [sb("yst%d" % i, [128, 1024], F32, phy) for i in range(2)]
            tyst = [Trk(), Trk()]
            k = 0
            for c0 in range(0, NT, 128):
                n = min(128, NT - c0)
                ys, tys = yst[k % 2], tyst[k % 2]
                for c in range(8):
                    pb = c % 2
                    tr(psum[pb][:n, 0:128], xT[:, c, c0:c0 + n], identF[:, :], [XT(c, c0), CONST], [ptrk[pb]])
                    cp("act" if c % 2 else "dve", ys[:n, c * 128:(c + 1) * 128], psum[pb][:n, 0:128], [ptrk[pb]], [tys])
                dma("sp", y_o[c0:c0 + n, :], ys[:n, :], [tys], [])
                k += 1
            S.fence_all()
            S.emit()
    return nc


_WNAMES = ("norm_ffn1", "ffn1_w_gu", "ffn1_w_down", "norm_mix", "w_in", "conv_w", "conv_b", "dt_bias", "A_log", "D_skip",
           "ssd_norm", "q_norm", "k_norm", "sb_bias", "sb_out_norm", "w_out", "norm_ffn2", "ffn2_w_gu", "ffn2_w_down")


def kernel(**inp):
    x_prompt = np.asarray(inp["x_prompt"], np.float32)
    x_sample = np.asarray(inp["x_sample"], np.float32)
    B, SEQ, DM = x_prompt.shape
    cache_k = np.asarray(inp["cache_k"], np.float32)
    cache_v = np.asarray(inp["cache_v"], np.float32)
    NPHYS = cache_k.shape[1]
    page_table = np.asarray(inp["page_table"], np.int32)
    NPG = page_table.shape[1]
    DFF = inp["ffn1_w_down"].shape[1]
    cfg = dict(SEQ=SEQ, DFF=DFF, PAST=NPG * 128, NPHYS=NPHYS, SEGN=512 if SEQ >= 1024 else 128)
    if "_stop" in inp:
        cfg["stop"] = int(inp["_stop"])
    if "_sub" in inp:
        cfg["sub"] = int(inp["_sub"])
    if "_dbg" in inp:
        cfg["dbg"] = inp["_dbg"]
    nc = build(cfg)
    NPOS = SEQ + 16
    meta = np.asarray(inp["meta_tokens"], np.float32)
    ckT = [np.ascontiguousarray(cache_k[i].reshape(NPHYS, 128, 4, 128).transpose(0, 3, 2, 1)).reshape(NPHYS, 128, 512) for i in range(2)]
    cvr = [np.ascontiguousarray(cache_v[i].reshape(NPHYS, 128, 512)) for i in range(2)]
    wts = {k: np.ascontiguousarray(np.asarray(inp[k], np.float32)) for k in _WNAMES}
    in_maps = []
    for c in range(8):
        xin = np.concatenate([meta, x_prompt[c], x_sample[4 * c:4 * c + 4].reshape(32, DM)], axis=0)
        m = dict(xin=np.ascontiguousarray(xin), ckT0=ckT[0], ckT1=ckT[1], cv0=cvr[0], cv1=cvr[1],
                 sssm=np.ascontiguousarray(np.asarray(inp["state_ssm"], np.float32)[:, 4 * c:4 * c + 4]),
                 sconv=np.ascontiguousarray(np.asarray(inp["state_conv"], np.float32)[:, 4 * c:4 * c + 4]),
                 ptab=np.ascontiguousarray(page_table[4 * c:4 * c + 4].reshape(1, 4 * NPG)))
        m.update(wts)
        in_maps.append(m)
    res = run_bass_kernel_spmd(nc, in_maps, core_ids=list(range(8))).results
    y = np.stack([r["y"] for r in res])
    y_prompt = np.ascontiguousarray(y[:, 16:NPOS])
    y_sample = np.ascontiguousarray(y[:, NPOS:].reshape(32, 8, DM))

    def st(name, shape_tail):
        return np.ascontiguousarray(np.stack([r[name] for r in res], axis=1).reshape((2, -1) + shape_tail))

    k_prompt = np.stack([r["kp"] for r in res], axis=1).reshape(2, 8, NPOS, 8, 64)
    v_prompt = np.stack([r["vp"] for r in res], axis=1).reshape(2, 8, NPOS, 8, 64)
    ssm_prompt = np.stack([r["ssmp"] for r in res], axis=1)
    conv_prompt = np.stack([r["convp"] for r in res], axis=1)
    k_sample = np.stack([r["ks"] for r in res], axis=1).reshape(2, 32, 8, 8, 64)
    v_sample = np.stack([r["vs"] for r in res], axis=1).reshape(2, 32, 8, 8, 64)
    ssm_sample = np.stack([r["ssms"] for r in res], axis=1).reshape(2, 32, 8, 64, 64)
    conv_sample = np.stack([r["convs"] for r in res], axis=1).reshape(2, 32, 3, 768)
    return tuple(np.ascontiguousarray(a, dtype=np.float32) for a in
                 (y_prompt, y_sample, k_prompt, v_prompt, ssm_prompt, conv_prompt, k_sample, v_sample, ssm_sample, conv_sample))
```

```python
import contextlib
import numpy as np
import concourse.bass as bass
import concourse.mybir as mybir
from concourse.bass_utils import run_bass_kernel_spmd

F32, BF16, I32 = mybir.dt.float32, mybir.dt.bfloat16, mybir.dt.int32
AF = mybir.ActivationFunctionType
ALU = mybir.AluOpType
AX = mybir.AxisListType
EPS = 1e-6


class Trk:
    __slots__ = ("w", "r", "const", "excl")

    def __init__(self, excl=False):
        self.w = None
        self.r = []
        self.const = False
        self.excl = excl


class Op:
    __slots__ = ("eng", "fn", "deps", "signaled", "sig", "is_dma", "sem", "val", "prewait")

    def __init__(self, eng, fn, is_dma):
        self.eng = eng
        self.fn = fn
        self.deps = []
        self.signaled = False
        self.sig = 0
        self.is_dma = is_dma
        self.sem = None
        self.val = 0
        self.prewait = None


class Sched:
    ENGS = ("pe", "act", "dve", "pool", "sp")
    RING = {"sp": 28, "pool": 24}
    CAP = 1500

    def __init__(self, nc, stack):
        self.nc = nc
        self.stack = stack
        self.ops = {e: [] for e in self.ENGS}
        self.esem = {}
        self.dsem = {q: [stack.enter_context(nc.semaphore("%s_dma%d" % (q, i))) for i in range(n)]
                     for q, n in self.RING.items()}
        self.dcount = {q: 0 for q in self.RING}
        self.dhist = {q: [] for q in self.RING}
        self.pending = []
        self.last = {e: None for e in self.ENGS}

    def _dep(self, op, d):
        if d is None or d is op:
            return
        if (not d.is_dma) and d.eng == op.eng and op.eng == "pe" and not op.is_dma:
            return
        if not d.is_dma:
            d.signaled = True
        op.deps.append(d)

    def _dma_slot(self, op, q):
        k = self.dcount[q]
        n = self.RING[q]
        op.sem = self.dsem[q][k % n]
        op.val = 16 * (k // n + 1)
        if k >= n:
            op.prewait = self.dhist[q][k - n]
        self.dhist[q].append(op)
        self.dcount[q] = k + 1

    def op(self, eng, fn, reads=(), writes=(), dma=False):
        op = Op(eng, fn, dma)
        ex = [t for t in reads if t.excl]
        if ex:
            reads = [t for t in reads if not t.excl]
            writes = list(writes) + [t for t in ex if t not in writes]
        for t in reads:
            self._dep(op, t.w)
        for t in writes:
            self._dep(op, t.w)
            for r in t.r:
                self._dep(op, r)
        for t in reads:
            if not t.const:
                t.r.append(op)
        for t in writes:
            t.w = op
            t.r = []
        if dma:
            self._dma_slot(op, eng)
            self.pending.append(op)
        self.ops[eng].append(op)
        if not dma:
            self.last[eng] = op
        return op

    def barrier(self, scratch_a, scratch_b):
        f = Op("sp", lambda e: e.dma_start(out=scratch_a, in_=scratch_b), True)
        f.deps.extend(self.pending)
        for e in ("pe", "act", "dve", "pool"):
            if self.last[e] is not None:
                self.last[e].signaled = True
                f.deps.append(self.last[e])
        self._dma_slot(f, "sp")
        self.ops["sp"].append(f)
        self.pending = [f]
        for e in ("pe", "act", "dve", "pool"):
            w = Op(e, None, False)
            w.deps.append(f)
            self.ops[e].append(w)

    def fence_all(self):
        w = Op("sp", None, False)
        w.deps.extend(self.pending)
        for e in ("pe", "act", "dve", "pool"):
            if self.last[e] is not None:
                self.last[e].signaled = True
                w.deps.append(self.last[e])
        self.ops["sp"].append(w)

    def emit(self):
        nc = self.nc
        CAP = self.CAP
        for e in ("pe", "act", "dve", "pool"):
            c = 0
            for op in self.ops[e]:
                if op.signaled and op.fn is not None:
                    c += 1
                    op.sig = c
            self.esem[e] = [self.stack.enter_context(nc.semaphore("%s_prog%d" % (e, i))) for i in range(c // CAP + 1)]
        bname = {"pe": "tensor", "act": "scalar", "dve": "vector", "pool": "gpsimd", "sp": "sync"}
        esem = self.esem
        with nc.Block() as block:
            for e in self.ENGS:
                ops = self.ops[e]

                def body(eng, ops=ops, e=e):
                    waited = {}

                    def wait(sem, val):
                        key = id(sem)
                        if waited.get(key, 0) >= val:
                            return
                        waited[key] = val
                        eng.wait_ge(sem, val)

                    for op in ops:
                        if op.prewait is not None:
                            wait(op.prewait.sem, op.prewait.val)
                        for d in op.deps:
                            if d.is_dma:
                                wait(d.sem, d.val)
                            else:
                                wait(esem[d.eng][(d.sig - 1) // CAP], (d.sig - 1) % CAP + 1)
                        if op.fn is None:
                            continue
                        inst = op.fn(eng)
                        if op.is_dma:
                            inst.then_inc(op.sem, 16)
                        elif op.signaled:
                            inst.then_inc(esem[e][(op.sig - 1) // CAP], 1)

                getattr(block, bname[e])(body)


def build(cfg):
    SEQ, DFF, PAST, NPHYS, SEGN = cfg["SEQ"], cfg["DFF"], cfg["PAST"], cfg["NPHYS"], cfg["SEGN"]
    DM, NC8 = 1024, 8
    NPOS = SEQ + 16
    NFT = NPOS // 128
    assert NPOS == NFT * 128 + 16
    NS = 32
    NT = NPOS + NS
    NPG = PAST // 128
    FC = DFF // 128
    KCH = 512
    PGC = 4 if NPG % 4 == 0 else 2
    NCH_S = NPG // PGC
    IN_COLS = 2824

    nc = bass.Bass("TRN2", target_bir_lowering=False)

    def din(name, shape, dt=F32):
        return nc.dram_tensor(name, list(shape), dt, kind="ExternalInput").ap()

    def dout(name, shape, dt=F32):
        return nc.dram_tensor(name, list(shape), dt, kind="ExternalOutput").ap()

    xin = din("xin", [NT, DM])
    ckT = [din("ckT%d" % i, [NPHYS, 128, 512]) for i in range(2)]
    cv = [din("cv%d" % i, [NPHYS, 128, 512]) for i in range(2)]
    sssm = din("sssm", [2, 4, 8, 64, 64])
    sconv = din("sconv", [2, 4, 3, 768])
    ptab = din("ptab", [1, 4 * NPG], I32)
    W = {}
    for nm, shp in (("norm_ffn1", [2, DM]), ("ffn1_w_gu", [2, DM, 2 * DFF]), ("ffn1_w_down", [2, DFF, DM]),
                    ("norm_mix", [2, DM]), ("w_in", [2, DM, IN_COLS]), ("conv_w", [2, 4, 768]),
                    ("conv_b", [2, 768]), ("dt_bias", [2, 8]), ("A_log", [2, 8]), ("D_skip", [2, 8]),
                    ("ssd_norm", [2, 512]), ("q_norm", [2, 64]), ("k_norm", [2, 64]), ("sb_bias", [2, 8]),
                    ("sb_out_norm", [2, 512]), ("w_out", [2, DM, DM]), ("norm_ffn2", [2, DM]),
                    ("ffn2_w_gu", [2, DM, 2 * DFF]), ("ffn2_w_down", [2, DFF, DM])):
        W[nm] = din(nm, shp)
    y_o = dout("y", [NT, DM])
    kp_o = dout("kp", [2, NPOS, 512])
    vp_o = dout("vp", [2, NPOS, 512])
    ssmp_o = dout("ssmp", [2, 8, 64, 64])
    convp_o = dout("convp", [2, 3, 768])
    ks_o = dout("ks", [2, NS, 512])
    vs_o = dout("vs", [2, NS, 512])
    ssms_o = dout("ssms", [2, 4, 8, 64, 64])
    convs_o = dout("convs", [2, 4, 3, 768])

    segs = []
    c = 0
    while c < NFT * 128:
        n = min(SEGN, NFT * 128 - c)
        segs.append((c, n))
        c += n
    segs.append((NFT * 128, 16 + NS))
    half = (len(segs) - 1 + 1) // 2
    blocks = [segs[:half], segs[half:]] if len(segs) > 2 else [segs[:1], segs[1:]]
    NBMAX = max(sum(n for _, n in b) for b in blocks)

    with contextlib.ExitStack() as st:
        S = Sched(nc, st)

        _cnt = [0]

        def sb(name, shape, dt=F32, stack=st):
            _cnt[0] += 1
            return stack.enter_context(nc.sbuf_tensor("%s_%d" % (name, _cnt[0]), list(shape), dt))

        def mm(out, lhsT, rhs, start, stop, reads, writes):
            S.op("pe", lambda e: e.matmul(out, lhsT=lhsT, rhs=rhs, start=start, stop=stop), reads, writes)

        def tr(out, in_, ident, reads, writes):
            S.op("pe", lambda e: e.transpose(out, in_, ident), reads, writes)

        def act(out, in_, func, reads, writes, bias=0.0, scale=1.0):
            S.op("act", lambda e: e.activation(out=out, in_=in_, func=func, bias=bias, scale=scale), reads, writes)

        def tt(eng, out, in0, in1, op, reads, writes):
            S.op(eng, lambda e: e.tensor_tensor(out=out, in0=in0, in1=in1, op=op), reads, writes)

        def ts(eng, out, in0, s1, s2, op0, op1, reads, writes):
            if s2 is None:
                S.op(eng, lambda e: e.tensor_scalar(out=out, in0=in0, scalar1=s1, scalar2=None, op0=op0), reads, writes)
            else:
                S.op(eng, lambda e: e.tensor_scalar(out=out, in0=in0, scalar1=s1, scalar2=s2, op0=op0, op1=op1), reads, writes)

        def stt(eng, out, in0, scalar, in1, op0, op1, reads, writes):
            S.op(eng, lambda e: e.scalar_tensor_tensor(out=out, in0=in0, scalar=scalar, in1=in1, op0=op0, op1=op1), reads, writes)

        def cp(eng, out, in_, reads, writes):
            if eng == "act":
                act(out, in_, AF.Copy, reads, writes)
            else:
                S.op(eng, lambda e: e.tensor_copy(out=out, in_=in_), reads, writes)

        def mset(eng, ap, val, writes):
            S.op(eng, lambda e: e.memset(ap, val), (), writes)

        def dma(q, out, in_, reads, writes):
            return S.op(q, lambda e: e.dma_start(out=out, in_=in_), reads, writes, dma=True)

        def dma_nc(q, out, in_, reads, writes):
            return S.op(q, lambda e: e.dma_start(out=out, in_=in_, allow_slow_non_contiguous=True), reads, writes, dma=True)

        _rr = {"regs": None, "i": 0}

        def page_val(e, ap):
            if _rr["regs"] is None:
                _rr["regs"] = [e.alloc_register("pgr%d" % i) for i in range(8)]
            r = _rr["regs"][_rr["i"] % 8]
            _rr["i"] += 1
            e.reg_load(r, ap)
            return e.snap(r)

        def rsum(eng, out, in_, reads, writes):
            S.op(eng, lambda e: e.reduce_sum(out=out, in_=in_, axis=AX.X), reads, writes)

        xT = sb("xT", [128, 8, NT])
        xtrk = {}

        def XT(c, s0):
            return xtrk.setdefault((c, s0 // 128), Trk())

        def xtr(c0, n):
            return [XT(c, s) for c in range(8) for s in range((c0 // 128) * 128, c0 + n, 128)]

        identF = sb("identF", [128, 128]); identB = sb("identB", [128, 128], BF16)
        onesB = sb("onesB", [128, 128], BF16); onesF = sb("onesF", [128, 128])
        triF = sb("triF", [128, 128]); m01F = sb("m01F", [128, 128]); m01B = sb("m01B", [128, 128], BF16)
        bmask = sb("bmask", [128, 8]); onec = sb("onec", [128, 1]); epsc = sb("epsc", [128, 1])
        rhs64 = sb("rhs64", [64, 1024]); lhs64 = sb("lhs64", [64, 128]); ext = sb("ext", [128, 40])
        scr = sb("scr", [1, 8])
        gT = {nm: sb(nm + "_T", [128, 2, 8]) for nm in ("norm_ffn1", "norm_mix", "norm_ffn2")}
        convw = sb("convw", [128, 2, 4, 6]); convb = sb("convb", [128, 2, 6])
        dtb_bc = sb("dtb_bc", [128, 2, 8]); A_bc = sb("A_bc", [128, 2, 8]); D_bc = sb("D_bc", [128, 2, 8])
        sbias_bc = sb("sbias_bc", [128, 2, 8]); sbias_c = sb("sbias_c", [8, 2])
        qn_bc = sb("qn_bc", [128, 2, 64]); kn_bc = sb("kn_bc", [128, 2, 64])
        ptile = sb("ptile", [128, 4 * NPG], I32); idxT = sb("idxT", [128, 4 * NPG], I32)
        iota_c = sb("iota_c", [128, 1], I32); iota_f = sb("iota_f", [128, 1], F32)
        ST = sb("ST", [128, 512]); STm = sb("STm", [128, 512], BF16)
        mask_s = sb("mask_s", [128, 8]); bias_s = sb("bias_s", [128, 2])
        repT = sb("repT", [8, 128]); hselT = sb("hselT", [8, 128])
        CONST = Trk()
        tST, tSTm, tHist = Trk(), Trk(), Trk()
        hist = sb("hist", [128, 6, 3])
        text, tr64, tl64 = Trk(), Trk(), Trk()

        psum = [st.enter_context(nc.psum_tensor("ps%d" % i, [128, 1024] if i == 3 else [128, 512], BF16 if i == 3 else F32))
                for i in range(8)]
        ptrk = [Trk(excl=True) for _ in range(8)]

        def pbf(i):
            assert i == 3
            return psum[3][:]

        mset("pool", identF[:], 1.0, [CONST])
        S.op("pool", lambda e: e.affine_select(out=identF[:], in_=identF[:], pattern=[[-1, 128]], compare_op=ALU.is_equal,
                                               fill=0.0, base=0, channel_multiplier=1), [CONST], [CONST])
        cp("pool", identB[:], identF[:], [CONST], [CONST])
        mset("pool", onesF[:], 1.0, [CONST]); mset("pool", onesB[:], 1.0, [CONST])
        mset("pool", triF[:], 1.0, [CONST])
        S.op("pool", lambda e: e.affine_select(out=triF[:], in_=triF[:], pattern=[[1, 128]], compare_op=ALU.is_ge,
                                               fill=0.0, base=0, channel_multiplier=-1), [CONST], [CONST])
        mset("pool", m01F[:], 1.0, [CONST])
        S.op("pool", lambda e: e.affine_select(out=m01F[:], in_=m01F[:], pattern=[[-1, 128]], compare_op=ALU.is_gt,
                                               fill=0.0, base=0, channel_multiplier=1), [CONST], [CONST])
        cp("pool", m01B[:], m01F[:], [CONST], [CONST])
        mset("pool", bmask[:], 0.0, [CONST]); mset("pool", bmask[0:64, 0:4], 1.0, [CONST]); mset("pool", bmask[64:128, 4:8], 1.0, [CONST])
        mset("pool", onec[:], 1.0, [CONST]); mset("pool", epsc[:], EPS, [CONST]); mset("pool", scr[:], 0.0, [CONST])
        mset("pool", rhs64[:], 0.0, [CONST]); mset("pool", lhs64[:], 0.0, [CONST]); mset("pool", lhs64[0:8, :], 1.0, [CONST])
        mset("pool", ext[:], 0.0, [CONST])
        cp("pool", rhs64[32:40, :].rearrange("p (h t) -> p h t", h=8),
           identF[32:40, 32:40].unsqueeze(2).to_broadcast([8, 8, 128]), [CONST], [CONST])
        cp("pool", repT[:].rearrange("p (a q) -> p a q", q=8), identF[0:8, 0:8].unsqueeze(1).to_broadcast([8, 16, 8]), [CONST], [CONST])
        cp("pool", hselT[:].rearrange("p (a h q) -> p a h q", a=2, h=8),
           identF[0:8, 0:8].unsqueeze(1).unsqueeze(3).to_broadcast([8, 2, 8, 8]), [CONST], [CONST])
        for nm in gT:
            dma_nc("sp", gT[nm][:], W[nm].rearrange("l (c p) -> p l c", p=128), [], [CONST])
        for l_ in range(2):
            for j_ in range(4):
                dma_nc("sp", convw[:, l_, j_, :], W["conv_w"][l_, j_].rearrange("(c p) -> p c", p=128), [], [CONST])
        dma_nc("sp", convb[:], W["conv_b"].rearrange("l (c p) -> p l c", p=128), [], [CONST])

        def bc(dst, src, n):
            dma("sp", dst[:].rearrange("p l n -> p (l n)"), src.rearrange("l n -> (l n)").partition_broadcast(128), [], [CONST])

        bc(dtb_bc, W["dt_bias"], 8); bc(A_bc, W["A_log"], 8); bc(D_bc, W["D_skip"], 8); bc(sbias_bc, W["sb_bias"], 8)
        bc(qn_bc, W["q_norm"], 64); bc(kn_bc, W["k_norm"], 64)
        dma_nc("sp", sbias_c[:], W["sb_bias"].rearrange("l h -> h l"), [], [CONST])
        dma("sp", ptile[:], ptab.rearrange("a n -> (a n)").partition_broadcast(128), [], [CONST])
        S.op("pool", lambda e: e.iota(iota_c[:], pattern=[[0, 1]], base=0, channel_multiplier=1), [], [CONST])
        cp("dve", iota_f[:], iota_c[:], [CONST], [CONST])
        ts("dve", idxT[:], ptile[:], 128.0, iota_f[:, 0:1], ALU.mult, ALU.add, [CONST], [CONST])
        act(A_bc[:], A_bc[:], AF.Exp, [CONST], [CONST])
        ts("dve", A_bc[:], A_bc[:], -1.0, None, ALU.mult, None, [CONST], [CONST])
        mm(psum[0][:, 0:8], repT[:, :], m01F[0:8, 0:8], True, True, [CONST], [ptrk[0]])
        cp("dve", mask_s[:], psum[0][:, 0:8], [ptrk[0]], [CONST])
        mm(psum[0][:, 8:10], hselT[:, :], sbias_c[:, :], True, True, [CONST], [ptrk[0]])
        cp("dve", bias_s[:], psum[0][:, 8:10], [ptrk[0]], [CONST])
        with contextlib.ExitStack() as ph0:
            xld = [sb("xld", [128, 1024], F32, ph0) for _ in range(2)]
            txld = [Trk(), Trk()]
            for it, c0 in enumerate(range(0, NT, 128)):
                n = min(128, NT - c0)
                dma("sp", xld[it % 2][:n, :], xin[c0:c0 + n, :], [], [txld[it % 2]])
                for c in range(8):
                    tr(psum[1 + (c % 2)][:, :n], xld[it % 2][:n, c * 128:(c + 1) * 128], identF[:n, :n], [txld[it % 2], CONST],
                       [ptrk[1 + (c % 2)]])
                    cp("act" if c % 2 else "dve", xT[:, c, c0:c0 + n], psum[1 + (c % 2)][:, :n], [ptrk[1 + (c % 2)]], [XT(c, c0)])
            S.barrier(scr[0:1, 0:1], scr[0:1, 4:5])
        CONST.const = True

        def norm_block(blk, gname, l, hT, thT, tmp):
            b0 = blk[0][0]
            sq, tsq, rstd, trs = tmp
            for (s0, n) in blk:
                tt("pool", sq[:, :, :n], xT[:, :, s0:s0 + n], xT[:, :, s0:s0 + n], ALU.mult, xtr(s0, n), [tsq])
                for c in range(8):
                    mm(psum[7][:, :n], onesB[:, :], sq[:, c, :n], c == 0, c == 7, [tsq, CONST], [ptrk[7]])
                act(rstd[:, :n], psum[7][:, :n], AF.Ln, [ptrk[7], CONST], [trs], bias=epsc[:, 0:1], scale=1.0 / DM)
                act(rstd[:, :n], rstd[:, :n], AF.Exp, [trs], [trs], scale=-0.5)
                for c in range(8):
                    stt("dve", hT[:, c, s0 - b0:s0 - b0 + n], xT[:, c, s0:s0 + n], gT[gname][:, l, c:c + 1], rstd[:, :n],
                        ALU.mult, ALU.mult, xtr(s0, n) + [trs, CONST], [thT])

        def ffn(blk, l, which):
            gname = "norm_ffn%d" % which
            wgu_d = W["ffn%d_w_gu" % which][l].rearrange("(c p) (two f) -> p c two f", p=128, two=2)
            wdn_d = W["ffn%d_w_down" % which][l].rearrange("(j p) m -> p j m", p=128)
            b0 = blk[0][0]
            nb = sum(n for _, n in blk)
            with contextlib.ExitStack() as ph:
                hT = sb("hT_f", [128, 8, NBMAX], BF16, ph); thT = Trk()
                aT = sb("aT", [128, FC, NBMAX], BF16, ph); taT = [Trk() for _ in range(FC)]
                sq = sb("sq_f", [128, 8, SEGN], BF16, ph); rstd = sb("rstd_f", [128, SEGN], F32, ph)
                wgu = [sb("wgu%d" % i, [128, 8, 2, 128], BF16, ph) for i in range(3)]; twgu = [Trk() for _ in range(3)]
                wdn = [sb("wdn%d" % i, [128, FC, 128], BF16, ph) for i in range(2)]; twdn = [Trk() for _ in range(2)]
                sg = [sb("sg%d" % i, [128, SEGN], F32, ph) for i in range(2)]; tsg = [Trk(), Trk()]
                norm_block(blk, gname, l, hT, thT, (sq, Trk(), rstd, Trk()))

                def ld_gu(j):
                    for two_ in range(2):
                        dma("pool", wgu[j % 3][:, :, two_, :], wgu_d[:, :, two_, j * 128:(j + 1) * 128], [], [twgu[j % 3]])

                def ld_dn(m):
                    for j0 in range(0, FC, 8):
                        j1 = min(FC, j0 + 8)
                        dma("pool", wdn[m % 2][:, j0:j1, :], wdn_d[:, j0:j1, m * 128:(m + 1) * 128], [], [twdn[m % 2]])

                ld_gu(0)
                if FC > 1:
                    ld_gu(1)
                k = 0
                for j in range(FC):
                    if j + 2 < FC:
                        ld_gu(j + 2)
                    for (s0, n) in blk:
                        pg, pu = ((0, 1), (2, 6))[k % 2]
                        for c in range(8):
                            mm(psum[pg][:, :n], wgu[j % 3][:, c, 0, :], hT[:, c, s0 - b0:s0 - b0 + n], c == 0, c == 7,
                               [twgu[j % 3], thT], [ptrk[pg]])
                        for c in range(8):
                            mm(psum[pu][:, :n], wgu[j % 3][:, c, 1, :], hT[:, c, s0 - b0:s0 - b0 + n], c == 0, c == 7,
                               [twgu[j % 3], thT], [ptrk[pu]])
                        act(sg[k % 2][:, :n], psum[pg][:, :n], AF.Silu, [ptrk[pg]], [tsg[k % 2]])
                        tt("dve", aT[:, j, s0 - b0:s0 - b0 + n], sg[k % 2][:, :n], psum[pu][:, :n], ALU.mult,
                           [tsg[k % 2], ptrk[pu]], [taT[j]])
                        k += 1
                    if j == FC - 1:
                        ld_dn(0)
                        ld_dn(1)
                k = 0
                for m in range(8):
                    if m >= 1 and m + 1 < 8:
                        ld_dn(m + 1)
                    for (s0, n) in blk:
                        py = 4 + (k % 2)
                        for j in range(FC):
                            mm(psum[py][:, :n], wdn[m % 2][:, j, :], aT[:, j, s0 - b0:s0 - b0 + n], j == 0, j == FC - 1,
                               [twdn[m % 2], taT[j]], [ptrk[py]])
                        tl = [XT(m, s) for s in range((s0 // 128) * 128, s0 + n, 128)]
                        stt("dve", xT[:, m, s0:s0 + n], psum[py][:, :n], 0.5, xT[:, m, s0:s0 + n], ALU.mult, ALU.add,
                            [ptrk[py]] + tl, tl)
                        k += 1
                S.barrier(scr[0:1, 0:1], scr[0:1, 4:5])

        def rstd_small(dst, src, n_inv, reads, trk):
            T_ = dst.shape[0]
            act(dst, src, AF.Ln, reads + [CONST], [trk], bias=epsc[:T_, 0:1], scale=n_inv)
            act(dst, dst, AF.Exp, [trk], [trk], scale=-0.5)

        def mixer(blk, l, bi, KT_, tKT, Vb_, tVb):
            b0 = blk[0][0]
            nb = sum(n for _, n in blk)
            bend = b0 + nb
            win_d = W["w_in"][l].rearrange("(c p) f -> p c f", p=128)
            wo_d = W["w_out"][l].rearrange("(c p) m -> p c m", p=128)
            tiles = [(128, 128 * i, i) for i in range(NFT) if b0 <= 128 * i < bend]
            has_tail = b0 <= NFT * 128 < bend
            with contextlib.ExitStack() as ph:
                hT = sb("hT_m", [128, 8, NBMAX], BF16, ph); thT = Trk()
                KTs = sb("KTs", [128, 4, 32], BF16, ph); tKTs = Trk()
                Vn = sb("Vn", [8, 4, 512], BF16, ph); tVn = Trk()
                with contextlib.ExitStack() as phn:
                    sqn = sb("sq_m", [128, 8, SEGN], BF16, phn); rstdn = sb("rstd_m", [128, SEGN], F32, phn)
                    norm_block(blk, "norm_mix", l, hT, thT, (sqn, Trk(), rstdn, Trk()))
                    S.barrier(scr[0:1, 0:1], scr[0:1, 4:5])

                def mk_w(stack, ncols, c0, half_):
                    wb = sb("wb", [128, 8, ncols], BF16, stack); twb = Trk()
                    for c in range(0, 8, 2):
                        for f0 in range(0, ncols, 512):
                            fn_ = min(512, ncols - f0)
                            dma("pool", wb[:, c:c + 2, f0:f0 + fn_], win_d[:, c:c + 2, c0 + f0:c0 + f0 + fn_], [], [twb])
                    wo = None; two = None
                    if half_ is not None:
                        wo = sb("wo", [128, 4, DM], BF16, stack); two = Trk()
                        for c in range(0, 4, 2):
                            for f0 in range(0, DM, 512):
                                dma("pool", wo[:, c:c + 2, f0:f0 + 512], wo_d[:, half_ * 4 + c:half_ * 4 + c + 2, f0:f0 + 512], [], [two])
                    return wb, twb, wo, two

                def proj_tm(wb, twb, pb, T, hc, wc0, wn):
                    for c in range(8):
                        mm(psum[pb][:T, :wn], hT[:, c, hc:hc + T], wb[:, c, wc0:wc0 + wn], c == 0, c == 7, [thT, twb], [ptrk[pb]])

                def wout_add(wo, two, ym, tym, T, g0):
                    for m in range(8):
                        pb = 6 + (m // 4)
                        for j in range(4):
                            mm(psum[pb][:, (m % 4) * 128:(m % 4) * 128 + T], wo[:, j, m * 128:(m + 1) * 128], ym[:, j, :T],
                               j == 0, j == 3, [two, tym], [ptrk[pb]])
                    for hm in range(2):
                        tl = [XT(m, g0) for m in range(4 * hm, 4 * hm + 4)]
                        tt("dve", xT[:, 4 * hm:4 * hm + 4, g0:g0 + T], xT[:, 4 * hm:4 * hm + 4, g0:g0 + T],
                           psum[6 + hm][:].rearrange("p (m t) -> p m t", m=4)[:, :, :T], ALU.add, [ptrk[6 + hm]] + tl, tl)

                def load_bc(stack, name, src):
                    t_ = sb(name, [128, 512], F32, stack)
                    tk = Trk()
                    dma("sp", t_[:], src.partition_broadcast(128), [], [tk])
                    return t_, tk

                with contextlib.ExitStack() as sp1:
                    wb, twb, wo, two = mk_w(sp1, 1288, 0, 0)
                    ssdn, tssdn = load_bc(sp1, "ssdn", W["ssd_norm"][l])
                    xr = sb("xr", [128, 6, 131], F32, sp1); txr = Trk()
                    acc = sb("acc", [128, 6, 128], F32, sp1); tacc = Trk()
                    xc = sb("xc", [128, 6, 128], F32, sp1); txc = Trk()
                    zs = sb("zs", [128, 512], F32, sp1); tzs = Trk()
                    dt = sb("dt", [128, 8], F32, sp1); tdt = Trk()
                    xtm = sb("xtm", [128, 512], F32, sp1); txtm = Trk()
                    Btm = sb("Btm", [128, 128], BF16, sp1); tBtm = Trk()
                    CTb = sb("CTb", [128, 128], BF16, sp1); BTb = sb("BTb", [128, 128], BF16, sp1); tCB = Trk()
                    cs_sb = sb("cs_sb", [128, 8], F32, sp1); ecs = sb("ecs", [128, 8], F32, sp1); te = sb("te", [128, 8], F32, sp1)
                    dtw = sb("dtw", [128, 8], F32, sp1); dec = sb("dec", [128, 8], F32, sp1); tsm = Trk()
                    dmt = sb("dmt", [128, 8, 128], F32, sp1); tdm = Trk()
                    cbm = sb("cbm", [128, 2, 128], F32, sp1); tcbm = Trk()
                    sc = sb("sc", [128, 8, 128], BF16, sp1); tsc = Trk()
                    xdt = sb("xdt", [128, 512], BF16, sp1); xw = sb("xw", [128, 512], BF16, sp1); txd = Trk()
                    y1 = sb("y1", [128, 512], F32, sp1); y2 = sb("y2", [128, 512], F32, sp1); ty = Trk()
                    ssq = sb("ssq", [128, 2], F32, sp1); tssq = Trk()
                    yb = sb("yb", [128, 512], BF16, sp1); tyb = Trk()
                    ym = sb("ym", [128, 4, 128], BF16, sp1); tym = Trk()
                    tmpS = sb("tmpS", [128, 4, 128], F32, sp1); ttmpS = Trk()
                    so = sb("so", [128, 4, 64], F32, sp1); tso = Trk()
                    hst = sb("hst", [128, 3, 6], F32, sp1); thst = Trk()

                    def ssd_tile(T, g0):
                        hc = g0 - b0
                        if SUB < 1:
                            return
                        for ch in range(6):
                            pb = 0 if ch < 4 else 1
                            o = (ch % 4) * 128
                            for c in range(8):
                                mm(psum[pb][:, o:o + T], wb[:, c, 512 + ch * 128:512 + (ch + 1) * 128], hT[:, c, hc:hc + T],
                                   c == 0, c == 7, [thT, twb], [ptrk[pb]])
                        cp("act", xr[:, 0:4, 3:3 + T], psum[0][:].rearrange("p (a t) -> p a t", a=4)[:, :, :T], [ptrk[0]], [txr])
                        cp("act", xr[:, 4:6, 3:3 + T], psum[1][:].rearrange("p (a t) -> p a t", a=4)[:, 0:2, :T], [ptrk[1]], [txr])
                        cp("pool", xr[:, :, 0:3], hist[:], [tHist], [txr])
                        if SUB < 2:
                            return
                        proj_tm(wb, twb, 2, T, hc, 0, 512)
                        act(zs[:T, :], psum[2][:T, :], AF.Silu, [ptrk[2]], [tzs])
                        for c in range(8):
                            mm(psum[7][:T, 16:24], hT[:, c, hc:hc + T], wb[:, c, 1280:1288], c == 0, c == 7, [thT, twb], [ptrk[7]])
                        tt("dve", dt[:T, :], psum[7][:T, 16:24], dtb_bc[:T, l, :], ALU.add, [ptrk[7], CONST], [tdt])
                        act(dt[:T, :], dt[:T, :], AF.Exp, [tdt], [tdt])
                        act(dt[:T, :], dt[:T, :], AF.Ln, [tdt], [tdt], bias=1.0)
                        if SUB < 3:
                            return
                        for ch in range(6):
                            ts("pool", acc[:, ch, :T], xr[:, ch, 0:T], convw[:, l, 0, ch:ch + 1], convb[:, l, ch:ch + 1], ALU.mult, ALU.add,
                               [txr, CONST], [tacc])
                            for j in range(1, 4):
                                stt("dve", acc[:, ch, :T], xr[:, ch, j:j + T], convw[:, l, j, ch:ch + 1], acc[:, ch, :T], ALU.mult, ALU.add,
                                    [txr, CONST, tacc], [tacc])
                        act(xc[:, :, :T], acc[:, :, :T], AF.Silu, [tacc], [txc])
                        cp("pool", hist[:], xr[:, :, T:T + 3], [txr], [tHist])
                        if SUB < 4:
                            return
                        for ch in range(4):
                            tr(psum[4][:T, ch * 128:(ch + 1) * 128], xc[:, ch, :T], identF[:, :], [txc, CONST], [ptrk[4]])
                        tr(psum[7][:T, 160:288], xc[:, 4, :T], identF[:, :], [txc, CONST], [ptrk[7]])
                        cp("act", xtm[:T, :], psum[4][:T, :], [ptrk[4]], [txtm])
                        cp("dve", Btm[:T, :], psum[7][:T, 160:288], [ptrk[7]], [tBtm])
                        cp("pool", BTb[:, :T], xc[:, 4, :T], [txc], [tCB])
                        cp("pool", CTb[:, :T], xc[:, 5, :T], [txc], [tCB])
                        if SUB < 5:
                            return
                        tt("dve", ext[:T, 0:8], dt[:T, :], A_bc[:T, l, :], ALU.mult, [tdt, CONST], [text])
                        ts("dve", ext[:T, 32:40], ext[:T, 0:8], -1.0, None, ALU.mult, None, [text], [text])
                        mm(psum[7][:T, 0:8], triF[:T, :T], ext[:T, 0:8], True, True, [text, CONST], [ptrk[7]])
                        mm(psum[7][:, 8:16], onesF[:T, :], ext[:T, 0:8], True, True, [text, CONST], [ptrk[7]])
                        mm(psum[7][0:40, 32:32 + T], ext[:T, 0:40], triF[:T, :T], True, True, [text, CONST], [ptrk[7]])
                        if SUB < 6:
                            return
                        cp("dve", cs_sb[:T, :], psum[7][:T, 0:8], [ptrk[7]], [tsm])
                        act(ecs[:T, :], psum[7][:T, 0:8], AF.Exp, [ptrk[7]], [tsm])
                        tt("dve", te[:T, :], psum[7][:T, 8:16], cs_sb[:T, :], ALU.subtract, [ptrk[7], tsm], [tsm])
                        act(te[:T, :], te[:T, :], AF.Exp, [tsm], [tsm])
                        tt("dve", dtw[:T, :], dt[:T, :], te[:T, :], ALU.mult, [tdt, tsm], [tsm])
                        act(dec[:, :], psum[7][:, 8:16], AF.Exp, [ptrk[7]], [tsm])
                        if SUB < 7:
                            return
                        cp("act", lhs64[32:40, :T], psum[7][32:40, 32:32 + T], [ptrk[7]], [tl64])
                        tt("dve", rhs64[0:8, :].rearrange("p (h t) -> p h t", h=8)[:, :, :T],
                           psum[7][0:8, 32:32 + T].unsqueeze(1).to_broadcast([8, 8, T]),
                           identF[0:8, 0:8].unsqueeze(2).to_broadcast([8, 8, T]), ALU.mult, [ptrk[7], CONST], [tr64])
                        for h in range(8):
                            pb = h // 4
                            mm(psum[pb][:T, (h % 4) * 128:(h % 4) * 128 + T], lhs64[:, :T], rhs64[:, h * 128:h * 128 + T], True, True,
                               [tl64, tr64], [ptrk[pb]])
                        if SUB < 8:
                            return
                        for g in range(2):
                            ts("dve", dmt[:T, 4 * g:4 * g + 4, :T], psum[g][:T, :].rearrange("p (a t) -> p a t", a=4)[:, :, :T], 0.0, None,
                               ALU.min, None, [ptrk[g]], [tdm])
                        act(dmt[:T, :, :T], dmt[:T, :, :T], AF.Exp, [tdm], [tdm])
                        for g in range(2):
                            mm(psum[5][:T, g * 128:g * 128 + T], BTb[g * 64:(g + 1) * 64, :T], CTb[g * 64:(g + 1) * 64, :T], True, True,
                               [tCB], [ptrk[5]])
                        tt("dve", cbm[:T, :, :T], psum[5][:T, 0:256].rearrange("p (g t) -> p g t", g=2)[:, :, :T],
                           triF[:T, :T].unsqueeze(1).to_broadcast([T, 2, T]), ALU.mult, [ptrk[5], CONST], [tcbm])
                        for g in range(2):
                            tt("dve", sc[:T, 4 * g:4 * g + 4, :T], dmt[:T, 4 * g:4 * g + 4, :T],
                               cbm[:T, g:g + 1, :T].to_broadcast([T, 4, T]), ALU.mult, [tdm, tcbm], [tsc])
                        if SUB < 9:
                            return
                        tt("dve", xdt[:T, :].rearrange("p (h d) -> p h d", h=8), xtm[:T, :].rearrange("p (h d) -> p h d", h=8),
                           dt[:T, :].unsqueeze(2).to_broadcast([T, 8, 64]), ALU.mult, [txtm, tdt], [txd])
                        tt("dve", xw[:T, :].rearrange("p (h d) -> p h d", h=8), xtm[:T, :].rearrange("p (h d) -> p h d", h=8),
                           dtw[:T, :].unsqueeze(2).to_broadcast([T, 8, 64]), ALU.mult, [txtm, tsm], [txd])
                        for h in range(8):
                            mm(psum[2][:T, h * 64:(h + 1) * 64], sc[:T, h, :T], xdt[:T, h * 64:(h + 1) * 64], True, True, [tsc, txd], [ptrk[2]])
                        mm(psum[4][:T, :], CTb[:, :T], STm[:, :], True, True, [tCB, tSTm], [ptrk[4]])
                        tt("dve", y1[:T, :].rearrange("p (h d) -> p h d", h=8), psum[4][:T, :].rearrange("p (h d) -> p h d", h=8),
                           ecs[:T, :].unsqueeze(2).to_broadcast([T, 8, 64]), ALU.mult, [ptrk[4], tsm], [ty])
                        tt("dve", y2[:T, :], y1[:T, :], psum[2][:T, :], ALU.add, [ty, ptrk[2]], [ty])
                        if SUB < 10:
                            return
                        mm(psum[5][:, :], Btm[:T, :], xw[:T, :], True, True, [tBtm, txd], [ptrk[5]])
                        tt("dve", ST[:].rearrange("p (h d) -> p h d", h=8), ST[:].rearrange("p (h d) -> p h d", h=8),
                           dec[:, :].unsqueeze(2).to_broadcast([128, 8, 64]), ALU.mult, [tST, tsm], [tST])
                        tt("dve", ST[:], ST[:], psum[5][:, :], ALU.add, [tST, ptrk[5]], [tST])
                        tt("pool", STm[:].rearrange("p (h d) -> p h d", h=8), ST[:].rearrange("p (h d) -> p h d", h=8),
                           bmask[:, :].unsqueeze(2).to_broadcast([128, 8, 64]), ALU.mult, [tST, CONST], [tSTm])
                        if SUB < 11:
                            return
                        tt("dve", y1[:T, :].rearrange("p (h d) -> p h d", h=8), xtm[:T, :].rearrange("p (h d) -> p h d", h=8),
                           D_bc[:T, l, :].unsqueeze(2).to_broadcast([T, 8, 64]), ALU.mult, [txtm, CONST, ty], [ty])
                        tt("dve", y2[:T, :], y2[:T, :], y1[:T, :], ALU.add, [ty], [ty])
                        tt("dve", y2[:T, :], y2[:T, :], zs[:T, :], ALU.mult, [ty, tzs], [ty])
                        tt("pool", y1[:T, :], y2[:T, :], y2[:T, :], ALU.mult, [ty], [ty])
                        rsum("dve", ssq[:T, :], y1[:T, :].rearrange("p (g d) -> p g d", g=2), [ty], [tssq])
                        rstd_small(ssq[:T, :], ssq[:T, :], 1.0 / 256, [tssq], tssq)
                        tt("dve", y2[:T, :].rearrange("p (g d) -> p g d", g=2), y2[:T, :].rearrange("p (g d) -> p g d", g=2),
                           ssq[:T, :].unsqueeze(2).to_broadcast([T, 2, 256]), ALU.mult, [ty, tssq], [ty])
                        tt("dve", yb[:T, :], y2[:T, :], ssdn[:T, :], ALU.mult, [ty, tssdn], [tyb])
                        for j in range(4):
                            tr(pbf(3)[:, j * 128:j * 128 + T], yb[:T, j * 128:(j + 1) * 128], identB[:T, :T], [tyb, CONST], [ptrk[3]])
                        cp("act", ym[:, :, :T], pbf(3)[:, 0:512].rearrange("p (j t) -> p j t", j=4)[:, :, :T], [ptrk[3]], [tym])
                        wout_add(wo, two, ym, tym, T, g0)


                    def state_out(dst):
                        if "a" in DBG:
                            return
                        for j in range(4):
                            tr(psum[4][:, j * 128:(j + 1) * 128], ST[:, j * 128:(j + 1) * 128], identF[:, :], [tST, CONST], [ptrk[4]])
                        for j in range(4):
                            g = j // 2
                            cp("act", so[:, j, :], psum[4][:, j * 128 + g * 64:j * 128 + g * 64 + 64], [ptrk[4]], [tso])
                        dma("sp", dst.rearrange("(j hh) p n -> (hh p) j n", hh=2), so[:], [tso], [])

                    def conv_out(dst):
                        if "b" in DBG:
                            return
                        for j_ in range(3):
                            dma_nc("sp", dst[j_].rearrange("(c p) -> p c", p=128), hist[:, :, j_], [tHist], [])

                    if bi == 0:
                        mset("pool", ST[:], 0.0, [tST]); mset("pool", STm[:], 0.0, [tSTm]); mset("pool", hist[:], 0.0, [tHist])
                    for (T, g0, i) in tiles:
                        ssd_tile(T, g0)
                    if has_tail:
                        ssd_tile(16, NFT * 128)
                        state_out(ssmp_o[l])
                        conv_out(convp_o[l])
                        for b in range(0 if "c" in DBG else 4):
                            if "d" not in DBG:
                                mset("pool", tmpS[:], 0.0, [ttmpS])
                                for j in range(4):
                                    g = j // 2
                                    for hh in range(2):
                                        dma("sp", tmpS[hh * 64:(hh + 1) * 64, j, g * 64:(g + 1) * 64], sssm[l, b, 2 * j + hh], [], [ttmpS])
                                if "f" not in DBG:
                                    for j in range(4):
                                        tr(psum[5][:, j * 128:(j + 1) * 128], tmpS[:, j, :], identF[:, :], [ttmpS, CONST], [ptrk[5]])
                                    cp("dve", ST[:], psum[5][:, :], [ptrk[5]], [tST])
                                    if "g" not in DBG:
                                        cp("act", STm[:], psum[5][:, :], [ptrk[5]], [tSTm])
                            if "e" not in DBG:
                                for j_ in range(3):
                                    dma_nc("sp", hst[:, j_, :], sconv[l, b, j_].rearrange("(c p) -> p c", p=128), [], [thst])
                                cp("pool", hist[:].rearrange("p c j -> p j c"), hst[:], [thst], [tHist])
                            ssd_tile(8, NPOS + 8 * b)
                            state_out(ssms_o[l, b])
                            conv_out(convs_o[l, b])
                    S.barrier(scr[0:1, 0:1], scr[0:1, 4:5])

                def mk_norm_bufs(stack):
                    d = {}
                    d["raw"] = sb("raw", [128, 512], F32, stack); d["traw"] = Trk()
                    d["sq2"] = sb("sq2", [128, 512], F32, stack); d["tsq2"] = Trk()
                    d["s8"] = sb("s8", [128, 8], F32, stack); d["ts8"] = Trk()
                    d["nrm"] = sb("nrm", [128, 512], F32, stack); d["tnrm"] = Trk()
                    d["nb16"] = sb("nb16", [128, 512], BF16, stack); d["tnb"] = Trk()
                    return d

                def qk_norm(d, pb, T, gbc):
                    raw, traw, sq2, tsq2, s8, ts8, nrm, tnrm, nb16, tnb = (d[k_] for k_ in
                        ("raw", "traw", "sq2", "tsq2", "s8", "ts8", "nrm", "tnrm", "nb16", "tnb"))
                    cp("act", raw[:T, :], psum[pb][:T, :], [ptrk[pb]], [traw])
                    tt("pool", sq2[:T, :], raw[:T, :], raw[:T, :], ALU.mult, [traw], [tsq2])
                    rsum("dve", s8[:T, :], sq2[:T, :].rearrange("p (h d) -> p h d", h=8), [tsq2], [ts8])
                    rstd_small(s8[:T, :], s8[:T, :], 1.0 / 64, [ts8], ts8)
                    tt("dve", nrm[:T, :].rearrange("p (h d) -> p h d", h=8), raw[:T, :].rearrange("p (h d) -> p h d", h=8),
                       s8[:T, :].unsqueeze(2).to_broadcast([T, 8, 64]), ALU.mult, [traw, ts8], [tnrm])
                    tt("dve", nrm[:T, :].rearrange("p (h d) -> p h d", h=8), nrm[:T, :].rearrange("p (h d) -> p h d", h=8),
                       gbc[:T, l, :].unsqueeze(1).to_broadcast([T, 8, 64]), ALU.mult, [tnrm, CONST], [tnrm])
                    cp("pool", nb16[:T, :], nrm[:T, :], [tnrm], [tnb])

                def to_fm(d, dst, tdst, T, c0):
                    nb16, tnb = d["nb16"], d["tnb"]
                    for j in range(4):
                        tr(pbf(3)[:, j * 128:j * 128 + T], nb16[:T, j * 128:(j + 1) * 128], identB[:T, :T], [tnb, CONST], [ptrk[3]])
                    cp("act", dst[:, :, c0:c0 + T], pbf(3)[:, 0:512].rearrange("p (j t) -> p j t", j=4)[:, :, :T], [ptrk[3]], [tdst])

                if not allow(l, 3):
                    return
                with contextlib.ExitStack() as sp2:
                    wb, twb, _, _ = mk_w(sp2, 1024, 1800, None)
                    d = mk_norm_bufs(sp2)
                    vraw = sb("vraw", [128, 512], F32, sp2); tvraw = Trk()

                    def kv_tile(T, g0, kt_idx, sample):
                        hc = g0 - b0
                        proj_tm(wb, twb, 0, T, hc, 0, 512)
                        qk_norm(d, 0, T, kn_bc)
                        if sample:
                            dma("sp", ks_o[l, :, :], d["nrm"][:T, :], [d["tnrm"]], [])
                            to_fm(d, KTs, tKTs, T, 0)
                            for b in range(4):
                                for c in range(8):
                                    mm(psum[1][0:8, :], hT[:, c, hc + 8 * b:hc + 8 * b + 8], wb[:, c, 512:1024], c == 0, c == 7,
                                       [thT, twb], [ptrk[1]])
                                cp("act", vraw[0:8, :], psum[1][0:8, :], [ptrk[1]], [tvraw])
                                dma("sp", vs_o[l, 8 * b:8 * b + 8, :], vraw[0:8, :], [tvraw], [])
                                cp("pool", Vn[0:8, b, :], vraw[0:8, :], [tvraw], [tVn])
                        else:
                            dma("sp", kp_o[l, g0:g0 + T, :], d["nrm"][:T, :], [d["tnrm"]], [])
                            to_fm(d, KT_, tKT[kt_idx], T, g0)
                            proj_tm(wb, twb, 1, T, hc, 512, 512)
                            cp("act", vraw[:T, :], psum[1][:T, :], [ptrk[1]], [tvraw])
                            dma("sp", vp_o[l, g0:g0 + T, :], vraw[:T, :], [tvraw], [])
                            cp("pool", Vb_[:T, kt_idx, :], vraw[:T, :], [tvraw], [tVb[kt_idx]])

                    for (T, g0, i) in tiles:
                        kv_tile(T, g0, i, False)
                    if has_tail:
                        kv_tile(16, NFT * 128, NFT, False)
                        kv_tile(32, NPOS, None, True)
                    S.barrier(scr[0:1, 0:1], scr[0:1, 4:5])

                if not allow(l, 4):
                    return
                with contextlib.ExitStack() as sp3:
                    wb, twb, wo, two = mk_w(sp3, 512, 1288, 1)
                    sbon, tsbon = load_bc(sp3, "sbon", W["sb_out_norm"][l])
                    d = mk_norm_bufs(sp3)
                    raw, traw, sq2, tsq2, s8, ts8, nb16, tnb = (d[k_] for k_ in ("raw", "traw", "sq2", "tsq2", "s8", "ts8", "nb16", "tnb"))
                    QT = sb("QT", [128, 4, 128], BF16, sp3); tQT = Trk()
                    NSET = 2
                    csets = [dict(eb=sb("eb", [128, 513], F32, sp3), Sb=sb("Sb", [128, 513], F32, sp3), Fx=sb("Fx", [128, 513], F32, sp3),
                                  wq=sb("wq", [128, 512], BF16, sp3), te=Trk(), tS=Trk(), tF=Trk(), twq=Trk()) for _ in range(NSET)]
                    ncars = [(sb("ncar", [128, 1], F32, sp3), Trk()) for _ in range(2)]
                    cctr = [0]
                    wT = [sb("wT%d" % i_, [128, 4, 128], BF16, sp3) for i_ in range(2)]; twT = [Trk(), Trk()]
                    ym2 = sb("ym2", [128, 4, 128], BF16, sp3); tym2 = Trk()

                    def sb_chunk(R, Wn, pz, bias_ap, diag_mask, first, nci=0):
                        cs_ = csets[cctr[0] % NSET]
                        cctr[0] += 1
                        eb, Sb, Fx, wq, te_, tS_, tF_, twq = (cs_[k_] for k_ in ("eb", "Sb", "Fx", "wq", "te", "tS", "tF", "twq"))
                        ncar, tnc = ncars[nci]
                        act(eb[:R, :Wn], psum[pz][:R, :Wn], AF.Exp, [ptrk[pz], CONST], [te_], bias=bias_ap, scale=0.125)
                        if first:
                            mset("pool", ncar[:R, :], 0.0, [tnc])
                        mset("pool", Sb[:R, 0:1], 0.0, [tS_])
                        act(Sb[:R, 1:Wn + 1], eb[:R, :Wn], AF.Ln, [te_, tS_], [tS_], bias=1.0)
                        if diag_mask is not None:
                            dm_ap, dw = diag_mask
                            tt("pool", Sb[:R, 1 + Wn - dw:1 + Wn], Sb[:R, 1 + Wn - dw:1 + Wn], dm_ap, ALU.mult, [tS_, CONST], [tS_])
                        S.op("dve", lambda e: e.tensor_tensor_scan(out=Fx[:R, :Wn + 1], data0=onec[:R, 0:1].to_broadcast([R, Wn + 1]),
                                                                   data1=Sb[:R, :Wn + 1], initial=0.0, op0=ALU.mult, op1=ALU.add),
                             [tS_, CONST], [tF_])
                        tt("dve", ncar[:R, :], ncar[:R, :], Fx[:R, Wn:Wn + 1], ALU.subtract, [tnc, tF_], [tnc])
                        act(Sb[:R, :Wn], Fx[:R, :Wn], AF.Exp, [tF_, tnc, tS_], [tS_], bias=ncar[:R, 0:1])
                        tt("dve", wq[:R, :Wn], eb[:R, :Wn], Sb[:R, :Wn], ALU.mult, [te_, tS_], [twq])
                        if diag_mask is not None:
                            dm_ap, dw = diag_mask
                            tt("pool", wq[:R, Wn - dw:Wn], wq[:R, Wn - dw:Wn], dm_ap, ALU.mult, [twq, CONST], [twq])
                        return wq, twq

                    def attn_tile(T, g0, i):
                        hc = g0 - b0
                        proj_tm(wb, twb, 0, T, hc, 0, 512)
                        qk_norm(d, 0, T, qn_bc)
                        to_fm(d, QT, tQT, T, 0)
                        nk = g0 + T
                        nck = (nk + KCH - 1) // KCH
                        nt_total = (nk + 127) // 128
                        k = 0
                        for h in range(8):
                            hp, hh = h // 2, h % 2
                            done = 0
                            for c in reversed(range(nck)):
                                k0 = c * KCH
                                Wn = min(KCH, nk - k0)
                                pz = 1 + (k % 2)
                                wts = wT[k % 2]; twts = twT[k % 2]
                                k += 1
                                ktl = [tKT[t_] for t_ in range(k0 // 128, (k0 + Wn + 127) // 128)]
                                mm(psum[pz][:T, :Wn], QT[hh * 64:(hh + 1) * 64, hp, :T], KT_[hh * 64:(hh + 1) * 64, hp, k0:k0 + Wn], True, True,
                                   [tQT] + ktl, [ptrk[pz]])
                                dmk = (m01F[:T, :T], T) if c == nck - 1 else None
                                wq, twq = sb_chunk(T, Wn, pz, sbias_bc[:T, l, h:h + 1], dmk, c == nck - 1, h % 2)
                                njt = (Wn + 127) // 128
                                for j in range(njt):
                                    ksz = min(128, Wn - 128 * j)
                                    tr(pbf(3)[:ksz, j * 128:j * 128 + T], wq[:T, 128 * j:128 * j + ksz], identB[:T, :T], [twq, CONST], [ptrk[3]])
                                for j in range(njt):
                                    ksz = min(128, Wn - 128 * j)
                                    cp("act" if j % 2 else "dve", wts[:ksz, j, :T], pbf(3)[:ksz, j * 128:j * 128 + T], [ptrk[3]], [twts])
                                for j in range(njt):
                                    ksz = min(128, Wn - 128 * j)
                                    kt = k0 // 128 + j
                                    done += 1
                                    mm(psum[4][:T, h * 64:(h + 1) * 64], wts[:ksz, j, :T], Vb_[:ksz, kt, h * 64:(h + 1) * 64],
                                       done == 1, done == nt_total, [twts, tVb[kt]], [ptrk[4]])
                        cp("act", raw[:T, :], psum[4][:T, :], [ptrk[4]], [traw])
                        tt("pool", sq2[:T, :], raw[:T, :], raw[:T, :], ALU.mult, [traw], [tsq2])
                        rsum("dve", s8[:T, 0:1], sq2[:T, :], [tsq2], [ts8])
                        rstd_small(s8[:T, 0:1], s8[:T, 0:1], 1.0 / 512, [ts8], ts8)
                        stt("dve", nb16[:T, :], raw[:T, :], s8[:T, 0:1], sbon[:T, :], ALU.mult, ALU.mult, [traw, ts8, tsbon], [tnb])
                        to_fm(d, ym2, tym2, T, 0)
                        wout_add(wo, two, ym2, tym2, T, g0)

                    def sample_attn():
                        with contextlib.ExitStack() as sp4:
                            Lq = sb("Lq", [128, 4, 4, 128], BF16, sp4); tLq = Trk()
                            osb = sb("osb", [8, 2, 512], F32, sp4); tosb = Trk()
                            Kbf = sb("Kbf", [128, 2, PGC, 512], BF16, sp4); tKbf = Trk()
                            Vbs = sb("Vbs", [128, 2, PGC, 512], BF16, sp4); tVbs = Trk()
                            hc = NPOS - b0
                            proj_tm(wb, twb, 0, 32, hc, 0, 512)
                            qk_norm(d, 0, 32, qn_bc)
                            to_fm(d, QT, tQT, 32, 0)
                            mset("pool", Lq[:], 0.0, [tLq])
                            for b in range(4):
                                bb = b % 2
                                for hp in range(4):
                                    for hh in range(2):
                                        h = 2 * hp + hh
                                        cp("pool", Lq[hh * 64:(hh + 1) * 64, b, hp, bb * 64 + h * 8:bb * 64 + h * 8 + 8],
                                           QT[hh * 64:(hh + 1) * 64, hp, 8 * b:8 * b + 8], [tQT, tLq], [tLq])
                            for pr in range(2):
                                bs = (2 * pr, 2 * pr + 1)
                                n = 0
                                for bb, b in enumerate(bs):
                                    for hp in range(4):
                                        mm(psum[1][:, 0:8], Lq[:, b, hp, :], KTs[:, hp, 8 * b:8 * b + 8], n == 0, n == 7, [tLq, tKTs], [ptrk[1]])
                                        n += 1
                                wq, twq = sb_chunk(128, 8, 1, bias_s[:, l:l + 1], (mask_s[:, :], 8), True)
                                if "w" in DBG and pr == 0 and l == 0:
                                    dbgt = sb("dbgt", [128, 16], F32, sp4); tdbg = Trk()
                                    cp("dve", dbgt[:, 0:8], wq[:, 0:8], [twq], [tdbg])
                                    cp("dve", dbgt[:, 8:16], eb[:, 0:8], [te_, tdbg], [tdbg])
                                    dma("sp", ks_o[1].rearrange("t (a c) -> (t a) c", c=16)[0:128, :], dbgt[:, :], [tdbg], [])
                                tr(pbf(3)[0:8, 0:128], wq[:, 0:8], identB[:, :], [twq, CONST], [ptrk[3]])
                                cp("dve", wT[0][0:8, 0, :], pbf(3)[0:8, 0:128], [ptrk[3]], [twT[0]])
                                for bb, b in enumerate(bs):
                                    for h in range(8):
                                        mm(psum[4 + bb][0:8, h * 64:(h + 1) * 64], wT[0][0:8, 0, bb * 64 + h * 8:bb * 64 + h * 8 + 8],
                                           Vn[0:8, b, h * 64:(h + 1) * 64], h == 0, False, [twT[0], tVn], [ptrk[4 + bb]])
                                k = 1
                                for cch in reversed(range(NCH_S)):
                                    for bb, b in enumerate(bs):
                                        for j in range(PGC):
                                            col = b * NPG + cch * PGC + j
                                            S.op("pool", lambda e, bb=bb, j=j, col=col: e.indirect_dma_start(
                                                out=Kbf[:, bb, j, :], out_offset=None, in_=ckT[l].rearrange("n p f -> (n p) f"),
                                                in_offset=bass.IndirectOffsetOnAxis(ap=idxT[:, col:col + 1], axis=0)), [CONST], [tKbf], dma=True)
                                            S.op("pool", lambda e, bb=bb, j=j, col=col: e.indirect_dma_start(
                                                out=Vbs[:, bb, j, :], out_offset=None, in_=cv[l].rearrange("n p f -> (n p) f"),
                                                in_offset=bass.IndirectOffsetOnAxis(ap=idxT[:, col:col + 1], axis=0)), [CONST], [tVbs], dma=True)
                                    Wn = PGC * 128
                                    pz = 1 + (k % 2)
                                    wts = wT[k % 2]; twts = twT[k % 2]
                                    k += 1
                                    for j in range(PGC):
                                        n = 0
                                        for bb, b in enumerate(bs):
                                            for hp in range(4):
                                                mm(psum[pz][:, j * 128:(j + 1) * 128], Lq[:, b, hp, :], Kbf[:, bb, j, hp * 128:(hp + 1) * 128],
                                                   n == 0, n == 7, [tLq, tKbf], [ptrk[pz]])
                                                n += 1
                                    wq, twq = sb_chunk(128, Wn, pz, bias_s[:, l:l + 1], None, False)
                                    for j in range(PGC):
                                        tr(pbf(3)[:, j * 128:(j + 1) * 128], wq[:, 128 * j:128 * (j + 1)], identB[:, :], [twq, CONST], [ptrk[3]])
                                    for j in range(PGC):
                                        cp("act" if j % 2 else "dve", wts[:, j, :], pbf(3)[:, j * 128:(j + 1) * 128], [ptrk[3]], [twts])
                                    for j in range(PGC):
                                        lastmm = (cch == 0 and j == PGC - 1)
                                        for bb, b in enumerate(bs):
                                            for h in range(8):
                                                mm(psum[4 + bb][0:8, h * 64:(h + 1) * 64], wts[:, j, bb * 64 + h * 8:bb * 64 + h * 8 + 8],
                                                   Vbs[:, bb, j, h * 64:(h + 1) * 64], False, lastmm, [twts, tVbs], [ptrk[4 + bb]])
                                for bb, b in enumerate(bs):
                                    cp("act", osb[0:8, bb, :], psum[4 + bb][0:8, :], [ptrk[4 + bb]], [tosb])
                                if "o" in DBG and pr == 0 and l == 0:
                                    dma("sp", vs_o[1].rearrange("(t a) c -> t (a c)", a=2)[0:8, :], osb[0:8, :, :].rearrange("p a c -> p (a c)"), [tosb], [])
                                for bb, b in enumerate(bs):
                                    tt("pool", sq2[0:8, :], osb[0:8, bb, :], osb[0:8, bb, :], ALU.mult, [tosb, tsq2], [tsq2])
                                    rsum("dve", s8[0:8, bb:bb + 1], sq2[0:8, :], [tsq2, ts8], [ts8])
                                rstd_small(s8[0:8, 0:2], s8[0:8, 0:2], 1.0 / 512, [ts8], ts8)
                                for bb, b in enumerate(bs):
                                    stt("dve", nb16[0:8, :], osb[0:8, bb, :], s8[0:8, bb:bb + 1], sbon[0:8, :], ALU.mult, ALU.mult,
                                        [tosb, ts8, tsbon, tnb], [tnb])
                                    for j in range(4):
                                        tr(pbf(3)[:, j * 128 + 8 * bb:j * 128 + 8 * bb + 8], nb16[0:8, j * 128:(j + 1) * 128], identB[0:8, 0:8],
                                           [tnb, CONST], [ptrk[3]])
                                cp("act", ym2[:, :, 16 * pr:16 * pr + 16], pbf(3)[:, 0:512].rearrange("p (j t) -> p j t", j=4)[:, :, 0:16],
                                   [ptrk[3]], [tym2])
                            wout_add(wo, two, ym2, tym2, 32, NPOS)
                            S.barrier(scr[0:1, 0:1], scr[0:1, 4:5])

                    for (T, g0, i) in tiles:
                        attn_tile(T, g0, i)
                    if has_tail:
                        attn_tile(16, NFT * 128, NFT)
                        S.barrier(scr[0:1, 0:1], scr[0:1, 4:5])
                        if allow(l, 5):
                            sample_attn()
                    S.barrier(scr[0:1, 0:1], scr[0:1, 4:5])

        STOP = cfg.get("stop", 99)
        SUB = cfg.get("sub", 99)
        DBG = cfg.get("dbg", "")

        def allow(l, code):
            return 10 * l + code <= STOP

        for l in range(2):
            for blk in blocks:
                if allow(l, 1):
                    ffn(blk, l, 1)
            with contextlib.ExitStack() as lay:
                KTt = sb("KT", [128, 4, NPOS], BF16, lay)
                Vbt = sb("Vb", [128, NFT + 1, 512], BF16, lay)
                tKT = [Trk() for _ in range(NFT + 1)]
                tVb = [Trk() for _ in range(NFT + 1)]
                for bi, blk in enumerate(blocks):
                    if allow(l, 2):
                        mixer(blk, l, bi, KTt, tKT, Vbt, tVb)
            for blk in blocks:
                if allow(l, 6):
                    ffn(blk, l, 2)
        if "m" in DBG:
            dma("sp", ks_o[1].rearrange("t (a c) -> (t a) c", c=8)[0:128, :], mask_s[:, :], [CONST], [])
            dma("sp", vs_o[1].rearrange("t (a c) -> (t a) c", c=2)[0:128, :], bias_s[:, :], [CONST], [])
        with contextlib.ExitStack() as phy:
            yst = [sb("yst%d" % i, [128, 1024], F32, phy) for i in range(2)]
            tyst = [Trk(), Trk()]
            k = 0
            for c0 in range(0, NT, 128):
                n = min(128, NT - c0)
                ys, tys = yst[k % 2], tyst[k % 2]
                for c in range(8):
                    pb = c % 2
                    tr(psum[pb][:n, 0:128], xT[:, c, c0:c0 + n], identF[:, :], [XT(c, c0), CONST], [ptrk[pb]])
                    cp("act" if c % 2 else "dve", ys[:n, c * 128:(c + 1) * 128], psum[pb][:n, 0:128], [ptrk[pb]], [tys])
                dma("sp", y_o[c0:c0 + n, :], ys[:n, :], [tys], [])
                k += 1
            S.fence_all()
            S.emit()
    return nc


_WNAMES = ("norm_ffn1", "ffn1_w_gu", "ffn1_w_down", "norm_mix", "w_in", "conv_w", "conv_b", "dt_bias", "A_log", "D_skip",
           "ssd_norm", "q_norm", "k_norm", "sb_bias", "sb_out_norm", "w_out", "norm_ffn2", "ffn2_w_gu", "ffn2_w_down")


def kernel(**inp):
    x_prompt = np.asarray(inp["x_prompt"], np.float32)
    x_sample = np.asarray(inp["x_sample"], np.float32)
    B, SEQ, DM = x_prompt.shape
    cache_k = np.asarray(inp["cache_k"], np.float32)
    cache_v = np.asarray(inp["cache_v"], np.float32)
    NPHYS = cache_k.shape[1]
    page_table = np.asarray(inp["page_table"], np.int32)
    NPG = page_table.shape[1]
    DFF = inp["ffn1_w_down"].shape[1]
    cfg = dict(SEQ=SEQ, DFF=DFF, PAST=NPG * 128, NPHYS=NPHYS, SEGN=512 if SEQ >= 1024 else 128)
    if "_stop" in inp:
        cfg["stop"] = int(inp["_stop"])
    if "_sub" in inp:
        cfg["sub"] = int(inp["_sub"])
    if "_dbg" in inp:
        cfg["dbg"] = inp["_dbg"]
    nc = build(cfg)
    NPOS = SEQ + 16
    meta = np.asarray(inp["meta_tokens"], np.float32)
    ckT = [np.ascontiguousarray(cache_k[i].reshape(NPHYS, 128, 4, 128).transpose(0, 3, 2, 1)).reshape(NPHYS, 128, 512) for i in range(2)]
    cvr = [np.ascontiguousarray(cache_v[i].reshape(NPHYS, 128, 512)) for i in range(2)]
    wts = {k: np.ascontiguousarray(np.asarray(inp[k], np.float32)) for k in _WNAMES}
    in_maps = []
    for c in range(8):
        xin = np.concatenate([meta, x_prompt[c], x_sample[4 * c:4 * c + 4].reshape(32, DM)], axis=0)
        m = dict(xin=np.ascontiguousarray(xin), ckT0=ckT[0], ckT1=ckT[1], cv0=cvr[0], cv1=cvr[1],
                 sssm=np.ascontiguousarray(np.asarray(inp["state_ssm"], np.float32)[:, 4 * c:4 * c + 4]),
                 sconv=np.ascontiguousarray(np.asarray(inp["state_conv"], np.float32)[:, 4 * c:4 * c + 4]),
                 ptab=np.ascontiguousarray(page_table[4 * c:4 * c + 4].reshape(1, 4 * NPG)))
        m.update(wts)
        in_maps.append(m)
    res = run_bass_kernel_spmd(nc, in_maps, core_ids=list(range(8))).results
    y = np.stack([r["y"] for r in res])
    y_prompt = np.ascontiguousarray(y[:, 16:NPOS])
    y_sample = np.ascontiguousarray(y[:, NPOS:].reshape(32, 8, DM))

    def st(name, shape_tail):
        return np.ascontiguousarray(np.stack([r[name] for r in res], axis=1).reshape((2, -1) + shape_tail))

    k_prompt = np.stack([r["kp"] for r in res], axis=1).reshape(2, 8, NPOS, 8, 64)
    v_prompt = np.stack([r["vp"] for r in res], axis=1).reshape(2, 8, NPOS, 8, 64)
    ssm_prompt = np.stack([r["ssmp"] for r in res], axis=1)
    conv_prompt = np.stack([r["convp"] for r in res], axis=1)
    k_sample = np.stack([r["ks"] for r in res], axis=1).reshape(2, 32, 8, 8, 64)
    v_sample = np.stack([r["vs"] for r in res], axis=1).reshape(2, 32, 8, 8, 64)
    ssm_sample = np.stack([r["ssms"] for r in res], axis=1).reshape(2, 32, 8, 64, 64)
    conv_sample = np.stack([r["convs"] for r in res], axis=1).reshape(2, 32, 3, 768)
    return tuple(np.ascontiguousarray(a, dtype=np.float32) for a in
                 (y_prompt, y_sample, k_prompt, v_prompt, ssm_prompt, conv_prompt, k_sample, v_sample, ssm_sample, conv_sample))
```

```python
import contextlib
import numpy as np
import concourse.bass as bass
import concourse.mybir as mybir
from concourse.bass_utils import run_bass_kernel_spmd

F32, BF16, I32 = mybir.dt.float32, mybir.dt.bfloat16, mybir.dt.int32
AF = mybir.ActivationFunctionType
ALU = mybir.AluOpType
AX = mybir.AxisListType
EPS = 1e-6


class Trk:
    __slots__ = ("w", "r", "const", "excl")

    def __init__(self, excl=False):
        self.w = None
        self.r = []
        self.const = False
        self.excl = excl


class Op:
    __slots__ = ("eng", "fn", "deps", "signaled", "sig", "is_dma", "sem", "val", "prewait")

    def __init__(self, eng, fn, is_dma):
        self.eng = eng
        self.fn = fn
        self.deps = []
        self.signaled = False
        self.sig = 0
        self.is_dma = is_dma
        self.sem = None
        self.val = 0
        self.prewait = None


class Sched:
    ENGS = ("pe", "act", "dve", "pool", "sp")
    RING = {"sp": 28, "pool": 24}
    CAP = 1500

    def __init__(self, nc, stack):
        self.nc = nc
        self.stack = stack
        self.ops = {e: [] for e in self.ENGS}
        self.esem = {}
        self.dsem = {q: [stack.enter_context(nc.semaphore("%s_dma%d" % (q, i))) for i in range(n)]
                     for q, n in self.RING.items()}
        self.dcount = {q: 0 for q in self.RING}
        self.dhist = {q: [] for q in self.RING}
        self.pending = []
        self.last = {e: None for e in self.ENGS}

    def _dep(self, op, d):
        if d is None or d is op:
            return
        if (not d.is_dma) and d.eng == op.eng and op.eng == "pe" and not op.is_dma:
            return
        if not d.is_dma:
            d.signaled = True
        op.deps.append(d)

    def _dma_slot(self, op, q):
        k = self.dcount[q]
        n = self.RING[q]
        op.sem = self.dsem[q][k % n]
        op.val = 16 * (k // n + 1)
        if k >= n:
            op.prewait = self.dhist[q][k - n]
        self.dhist[q].append(op)
        self.dcount[q] = k + 1

    def op(self, eng, fn, reads=(), writes=(), dma=False):
        op = Op(eng, fn, dma)
        ex = [t for t in reads if t.excl]
        if ex:
            reads = [t for t in reads if not t.excl]
            writes = list(writes) + [t for t in ex if t not in writes]
        for t in reads:
            self._dep(op, t.w)
        for t in writes:
            self._dep(op, t.w)
            for r in t.r:
                self._dep(op, r)
        for t in reads:
            if not t.const:
                t.r.append(op)
        for t in writes:
            t.w = op
            t.r = []
        if dma:
            self._dma_slot(op, eng)
            self.pending.append(op)
        self.ops[eng].append(op)
        if not dma:
            self.last[eng] = op
        return op

    def barrier(self, scratch_a, scratch_b):
        f = Op("sp", lambda e: e.dma_start(out=scratch_a, in_=scratch_b), True)
        f.deps.extend(self.pending)
        for e in ("pe", "act", "dve", "pool"):
            if self.last[e] is not None:
                self.last[e].signaled = True
                f.deps.append(self.last[e])
        self._dma_slot(f, "sp")
        self.ops["sp"].append(f)
        self.pending = [f]
        for e in ("pe", "act", "dve", "pool"):
            w = Op(e, None, False)
            w.deps.append(f)
            self.ops[e].append(w)

    def fence_all(self):
        w = Op("sp", None, False)
        w.deps.extend(self.pending)
        for e in ("pe", "act", "dve", "pool"):
            if self.last[e] is not None:
                self.last[e].signaled = True
                w.deps.append(self.last[e])
        self.ops["sp"].append(w)

    def emit(self):
        nc = self.nc
        CAP = self.CAP
        for e in ("pe", "act", "dve", "pool"):
            c = 0
            for op in self.ops[e]:
                if op.signaled and op.fn is not None:
                    c += 1
                    op.sig = c
            self.esem[e] = [self.stack.enter_context(nc.semaphore("%s_prog%d" % (e, i))) for i in range(c // CAP + 1)]
        bname = {"pe": "tensor", "act": "scalar", "dve": "vector", "pool": "gpsimd", "sp": "sync"}
        esem = self.esem
        with nc.Block() as block:
            for e in self.ENGS:
                ops = self.ops[e]

                def body(eng, ops=ops, e=e):
                    waited = {}

                    def wait(sem, val):
                        key = id(sem)
                        if waited.get(key, 0) >= val:
                            return
                        waited[key] = val
                        eng.wait_ge(sem, val)

                    for op in ops:
                        if op.prewait is not None:
                            wait(op.prewait.sem, op.prewait.val)
                        for d in op.deps:
                            if d.is_dma:
                                wait(d.sem, d.val)
                            else:
                                wait(esem[d.eng][(d.sig - 1) // CAP], (d.sig - 1) % CAP + 1)
                        if op.fn is None:
                            continue
                        inst = op.fn(eng)
                        if op.is_dma:
                            inst.then_inc(op.sem, 16)
                        elif op.signaled:
                            inst.then_inc(esem[e][(op.sig - 1) // CAP], 1)

                getattr(block, bname[e])(body)


def build(cfg):
    SEQ, DFF, PAST, NPHYS, SEGN = cfg["SEQ"], cfg["DFF"], cfg["PAST"], cfg["NPHYS"], cfg["SEGN"]
    DM, NC8 = 1024, 8
    NPOS = SEQ + 16
    NFT = NPOS // 128
    assert NPOS == NFT * 128 + 16
    NS = 32
    NT = NPOS + NS
    NPG = PAST // 128
    FC = DFF // 128
    KCH = 512
    PGC = 4 if NPG % 4 == 0 else 2
    NCH_S = NPG // PGC
    IN_COLS = 2824

    nc = bass.Bass("TRN2", target_bir_lowering=False)

    def din(name, shape, dt=F32):
        return nc.dram_tensor(name, list(shape), dt, kind="ExternalInput").ap()

    def dout(name, shape, dt=F32):
        return nc.dram_tensor(name, list(shape), dt, kind="ExternalOutput").ap()

    xin = din("xin", [NT, DM])
    ckT = [din("ckT%d" % i, [NPHYS, 128, 512]) for i in range(2)]
    cv = [din("cv%d" % i, [NPHYS, 128, 512]) for i in range(2)]
    sssm = din("sssm", [2, 4, 8, 64, 64])
    sconv = din("sconv", [2, 4, 3, 768])
    ptab = din("ptab", [1, 4 * NPG], I32)
    W = {}
    for nm, shp in (("norm_ffn1", [2, DM]), ("ffn1_w_gu", [2, DM, 2 * DFF]), ("ffn1_w_down", [2, DFF, DM]),
                    ("norm_mix", [2, DM]), ("w_in", [2, DM, IN_COLS]), ("conv_w", [2, 4, 768]),
                    ("conv_b", [2, 768]), ("dt_bias", [2, 8]), ("A_log", [2, 8]), ("D_skip", [2, 8]),
                    ("ssd_norm", [2, 512]), ("q_norm", [2, 64]), ("k_norm", [2, 64]), ("sb_bias", [2, 8]),
                    ("sb_out_norm", [2, 512]), ("w_out", [2, DM, DM]), ("norm_ffn2", [2, DM]),
                    ("ffn2_w_gu", [2, DM, 2 * DFF]), ("ffn2_w_down", [2, DFF, DM])):
        W[nm] = din(nm, shp)
    y_o = dout("y", [NT, DM])
    kp_o = dout("kp", [2, NPOS, 512])
    vp_o = dout("vp", [2, NPOS, 512])
    ssmp_o = dout("ssmp", [2, 8, 64, 64])
    convp_o = dout("convp", [2, 3, 768])
    ks_o = dout("ks", [2, NS, 512])
    vs_o = dout("vs", [2, NS, 512])
    ssms_o = dout("ssms", [2, 4, 8, 64, 64])
    convs_o = dout("convs", [2, 4, 3, 768])

    segs = []
    c = 0
    while c < NFT * 128:
        n = min(SEGN, NFT * 128 - c)
        segs.append((c, n))
        c += n
    segs.append((NFT * 128, 16 + NS))
    half = (len(segs) - 1 + 1) // 2
    blocks = [segs[:half], segs[half:]] if len(segs) > 2 else [segs[:1], segs[1:]]
    NBMAX = max(sum(n for _, n in b) for b in blocks)

    with contextlib.ExitStack() as st:
        S = Sched(nc, st)

        _cnt = [0]

        def sb(name, shape, dt=F32, stack=st):
            _cnt[0] += 1
            return stack.enter_context(nc.sbuf_tensor("%s_%d" % (name, _cnt[0]), list(shape), dt))

        def mm(out, lhsT, rhs, start, stop, reads, writes):
            S.op("pe", lambda e: e.matmul(out, lhsT=lhsT, rhs=rhs, start=start, stop=stop), reads, writes)

        def tr(out, in_, ident, reads, writes):
            S.op("pe", lambda e: e.transpose(out, in_, ident), reads, writes)

        def act(out, in_, func, reads, writes, bias=0.0, scale=1.0):
            S.op("act", lambda e: e.activation(out=out, in_=in_, func=func, bias=bias, scale=scale), reads, writes)

        def tt(eng, out, in0, in1, op, reads, writes):
            S.op(eng, lambda e: e.tensor_tensor(out=out, in0=in0, in1=in1, op=op), reads, writes)

        def ts(eng, out, in0, s1, s2, op0, op1, reads, writes):
            if s2 is None:
                S.op(eng, lambda e: e.tensor_scalar(out=out, in0=in0, scalar1=s1, scalar2=None, op0=op0), reads, writes)
            else:
                S.op(eng, lambda e: e.tensor_scalar(out=out, in0=in0, scalar1=s1, scalar2=s2, op0=op0, op1=op1), reads, writes)

        def stt(eng, out, in0, scalar, in1, op0, op1, reads, writes):
            S.op(eng, lambda e: e.scalar_tensor_tensor(out=out, in0=in0, scalar=scalar, in1=in1, op0=op0, op1=op1), reads, writes)

        def cp(eng, out, in_, reads, writes):
            if eng == "act":
                act(out, in_, AF.Copy, reads, writes)
            else:
                S.op(eng, lambda e: e.tensor_copy(out=out, in_=in_), reads, writes)

        def mset(eng, ap, val, writes):
            S.op(eng, lambda e: e.memset(ap, val), (), writes)

        def dma(q, out, in_, reads, writes):
            return S.op(q, lambda e: e.dma_start(out=out, in_=in_), reads, writes, dma=True)

        def dma_nc(q, out, in_, reads, writes):
            return S.op(q, lambda e: e.dma_start(out=out, in_=in_, allow_slow_non_contiguous=True), reads, writes, dma=True)

        _rr = {"regs": None, "i": 0}

        def page_val(e, ap):
            if _rr["regs"] is None:
                _rr["regs"] = [e.alloc_register("pgr%d" % i) for i in range(8)]
            r = _rr["regs"][_rr["i"] % 8]
            _rr["i"] += 1
            e.reg_load(r, ap)
            return e.snap(r)

        def rsum(eng, out, in_, reads, writes):
            S.op(eng, lambda e: e.reduce_sum(out=out, in_=in_, axis=AX.X), reads, writes)

        xT = sb("xT", [128, 8, NT])
        xtrk = {}

        def XT(c, s0):
            return xtrk.setdefault((c, s0 // 128), Trk())

        def xtr(c0, n):
            return [XT(c, s) for c in range(8) for s in range((c0 // 128) * 128, c0 + n, 128)]

        identF = sb("identF", [128, 128]); identB = sb("identB", [128, 128], BF16)
        onesB = sb("onesB", [128, 128], BF16); onesF = sb("onesF", [128, 128])
        triF = sb("triF", [128, 128]); m01F = sb("m01F", [128, 128]); m01B = sb("m01B", [128, 128], BF16)
        bmask = sb("bmask", [128, 8]); onec = sb("onec", [128, 1]); epsc = sb("epsc", [128, 1])
        rhs64 = sb("rhs64", [64, 1024]); lhs64 = sb("lhs64", [64, 128]); ext = sb("ext", [128, 40])
        scr = sb("scr", [1, 8])
        gT = {nm: sb(nm + "_T", [128, 2, 8]) for nm in ("norm_ffn1", "norm_mix", "norm_ffn2")}
        convw = sb("convw", [128, 2, 4, 6]); convb = sb("convb", [128, 2, 6])
        dtb_bc = sb("dtb_bc", [128, 2, 8]); A_bc = sb("A_bc", [128, 2, 8]); D_bc = sb("D_bc", [128, 2, 8])
        sbias_bc = sb("sbias_bc", [128, 2, 8]); sbias_c = sb("sbias_c", [8, 2])
        qn_bc = sb("qn_bc", [128, 2, 64]); kn_bc = sb("kn_bc", [128, 2, 64])
        ptile = sb("ptile", [128, 4 * NPG], I32); idxT = sb("idxT", [128, 4 * NPG], I32)
        iota_c = sb("iota_c", [128, 1], I32); iota_f = sb("iota_f", [128, 1], F32)
        ST = sb("ST", [128, 512]); STm = sb("STm", [128, 512], BF16)
        mask_s = sb("mask_s", [128, 8]); bias_s = sb("bias_s", [128, 2])
        repT = sb("repT", [8, 128]); hselT = sb("hselT", [8, 128])
        CONST = Trk()
        tST, tSTm, tHist = Trk(), Trk(), Trk()
        hist = sb("hist", [128, 6, 3])
        text, tr64, tl64 = Trk(), Trk(), Trk()

        psum = [st.enter_context(nc.psum_tensor("ps%d" % i, [128, 1024] if i == 3 else [128, 512], BF16 if i == 3 else F32))
                for i in range(8)]
        ptrk = [Trk(excl=True) for _ in range(8)]

        def pbf(i):
            assert i == 3
            return psum[3][:]

        mset("pool", identF[:], 1.0, [CONST])
        S.op("pool", lambda e: e.affine_select(out=identF[:], in_=identF[:], pattern=[[-1, 128]], compare_op=ALU.is_equal,
                                               fill=0.0, base=0, channel_multiplier=1), [CONST], [CONST])
        cp("pool", identB[:], identF[:], [CONST], [CONST])
        mset("pool", onesF[:], 1.0, [CONST]); mset("pool", onesB[:], 1.0, [CONST])
        mset("pool", triF[:], 1.0, [CONST])
        S.op("pool", lambda e: e.affine_select(out=triF[:], in_=triF[:], pattern=[[1, 128]], compare_op=ALU.is_ge,
                                               fill=0.0, base=0, channel_multiplier=-1), [CONST], [CONST])
        mset("pool", m01F[:], 1.0, [CONST])
        S.op("pool", lambda e: e.affine_select(out=m01F[:], in_=m01F[:], pattern=[[-1, 128]], compare_op=ALU.is_gt,
                                               fill=0.0, base=0, channel_multiplier=1), [CONST], [CONST])
        cp("pool", m01B[:], m01F[:], [CONST], [CONST])
        mset("pool", bmask[:], 0.0, [CONST]); mset("pool", bmask[0:64, 0:4], 1.0, [CONST]); mset("pool", bmask[64:128, 4:8], 1.0, [CONST])
        mset("pool", onec[:], 1.0, [CONST]); mset("pool", epsc[:], EPS, [CONST]); mset("pool", scr[:], 0.0, [CONST])
        mset("pool", rhs64[:], 0.0, [CONST]); mset("pool", lhs64[:], 0.0, [CONST]); mset("pool", lhs64[0:8, :], 1.0, [CONST])
        mset("pool", ext[:], 0.0, [CONST])
        cp("pool", rhs64[32:40, :].rearrange("p (h t) -> p h t", h=8),
           identF[32:40, 32:40].unsqueeze(2).to_broadcast([8, 8, 128]), [CONST], [CONST])
        cp("pool", repT[:].rearrange("p (a q) -> p a q", q=8), identF[0:8, 0:8].unsqueeze(1).to_broadcast([8, 16, 8]), [CONST], [CONST])
        cp("pool", hselT[:].rearrange("p (a h q) -> p a h q", a=2, h=8),
           identF[0:8, 0:8].unsqueeze(1).unsqueeze(3).to_broadcast([8, 2, 8, 8]), [CONST], [CONST])
        for nm in gT:
            dma_nc("sp", gT[nm][:], W[nm].rearrange("l (c p) -> p l c", p=128), [], [CONST])
        for l_ in range(2):
            for j_ in range(4):
                dma_nc("sp", convw[:, l_, j_, :], W["conv_w"][l_, j_].rearrange("(c p) -> p c", p=128), [], [CONST])
        dma_nc("sp", convb[:], W["conv_b"].rearrange("l (c p) -> p l c", p=128), [], [CONST])

        def bc(dst, src, n):
            dma("sp", dst[:].rearrange("p l n -> p (l n)"), src.rearrange("l n -> (l n)").partition_broadcast(128), [], [CONST])

        bc(dtb_bc, W["dt_bias"], 8); bc(A_bc, W["A_log"], 8); bc(D_bc, W["D_skip"], 8); bc(sbias_bc, W["sb_bias"], 8)
        bc(qn_bc, W["q_norm"], 64); bc(kn_bc, W["k_norm"], 64)
        dma_nc("sp", sbias_c[:], W["sb_bias"].rearrange("l h -> h l"), [], [CONST])
        dma("sp", ptile[:], ptab.rearrange("a n -> (a n)").partition_broadcast(128), [], [CONST])
        S.op("pool", lambda e: e.iota(iota_c[:], pattern=[[0, 1]], base=0, channel_multiplier=1), [], [CONST])
        cp("dve", iota_f[:], iota_c[:], [CONST], [CONST])
        ts("dve", idxT[:], ptile[:], 128.0, iota_f[:, 0:1], ALU.mult, ALU.add, [CONST], [CONST])
        act(A_bc[:], A_bc[:], AF.Exp, [CONST], [CONST])
        ts("dve", A_bc[:], A_bc[:], -1.0, None, ALU.mult, None, [CONST], [CONST])
        mm(psum[0][:, 0:8], repT[:, :], m01F[0:8, 0:8], True, True, [CONST], [ptrk[0]])
        cp("dve", mask_s[:], psum[0][:, 0:8], [ptrk[0]], [CONST])
        mm(psum[0][:, 8:10], hselT[:, :], sbias_c[:, :], True, True, [CONST], [ptrk[0]])
        cp("dve", bias_s[:], psum[0][:, 8:10], [ptrk[0]], [CONST])
        with contextlib.ExitStack() as ph0:
            xld = [sb("xld", [128, 1024], F32, ph0) for _ in range(2)]
            txld = [Trk(), Trk()]
            for it, c0 in enumerate(range(0, NT, 128)):
                n = min(128, NT - c0)
                dma("sp", xld[it % 2][:n, :], xin[c0:c0 + n, :], [], [txld[it % 2]])
                for c in range(8):
                    tr(psum[1 + (c % 2)][:, :n], xld[it % 2][:n, c * 128:(c + 1) * 128], identF[:n, :n], [txld[it % 2], CONST],
                       [ptrk[1 + (c % 2)]])
                    cp("act" if c % 2 else "dve", xT[:, c, c0:c0 + n], psum[1 + (c % 2)][:, :n], [ptrk[1 + (c % 2)]], [XT(c, c0)])
            S.barrier(scr[0:1, 0:1], scr[0:1, 4:5])
        CONST.const = True

        def norm_block(blk, gname, l, hT, thT, tmp):
            b0 = blk[0][0]
            sq, tsq, rstd, trs = tmp
            for (s0, n) in blk:
                tt("pool", sq[:, :, :n], xT[:, :, s0:s0 + n], xT[:, :, s0:s0 + n], ALU.mult, xtr(s0, n), [tsq])
                for c in range(8):
                    mm(psum[7][:, :n], onesB[:, :], sq[:, c, :n], c == 0, c == 7, [tsq, CONST], [ptrk[7]])
                act(rstd[:, :n], psum[7][:, :n], AF.Ln, [ptrk[7], CONST], [trs], bias=epsc[:, 0:1], scale=1.0 / DM)
                act(rstd[:, :n], rstd[:, :n], AF.Exp, [trs], [trs], scale=-0.5)
                for c in range(8):
                    stt("dve", hT[:, c, s0 - b0:s0 - b0 + n], xT[:, c, s0:s0 + n], gT[gname][:, l, c:c + 1], rstd[:, :n],
                        ALU.mult, ALU.mult, xtr(s0, n) + [trs, CONST], [thT])

        def ffn(blk, l, which):
            gname = "norm_ffn%d" % which
            wgu_d = W["ffn%d_w_gu" % which][l].rearrange("(c p) (two f) -> p c two f", p=128, two=2)
            wdn_d = W["ffn%d_w_down" % which][l].rearrange("(j p) m -> p j m", p=128)
            b0 = blk[0][0]
            nb = sum(n for _, n in blk)
            with contextlib.ExitStack() as ph:
                hT = sb("hT_f", [128, 8, NBMAX], BF16, ph); thT = Trk()
                aT = sb("aT", [128, FC, NBMAX], BF16, ph); taT = [Trk() for _ in range(FC)]
                sq = sb("sq_f", [128, 8, SEGN], BF16, ph); rstd = sb("rstd_f", [128, SEGN], F32, ph)
                wgu = [sb("wgu%d" % i, [128, 8, 2, 128], BF16, ph) for i in range(3)]; twgu = [Trk() for _ in range(3)]
                wdn = [sb("wdn%d" % i, [128, FC, 128], BF16, ph) for i in range(2)]; twdn = [Trk() for _ in range(2)]
                sg = [sb("sg%d" % i, [128, SEGN], F32, ph) for i in range(2)]; tsg = [Trk(), Trk()]
                norm_block(blk, gname, l, hT, thT, (sq, Trk(), rstd, Trk()))

                def ld_gu(j):
                    for two_ in range(2):
                        dma("pool", wgu[j % 3][:, :, two_, :], wgu_d[:, :, two_, j * 128:(j + 1) * 128], [], [twgu[j % 3]])

                def ld_dn(m):
                    for j0 in range(0, FC, 8):
                        j1 = min(FC, j0 + 8)
                        dma("pool", wdn[m % 2][:, j0:j1, :], wdn_d[:, j0:j1, m * 128:(m + 1) * 128], [], [twdn[m % 2]])

                ld_gu(0)
                if FC > 1:
                    ld_gu(1)
                k = 0
                for j in range(FC):
                    if j + 2 < FC:
                        ld_gu(j + 2)
                    for (s0, n) in blk:
                        pg, pu = ((0, 1), (2, 6))[k % 2]
                        for c in range(8):
                            mm(psum[pg][:, :n], wgu[j % 3][:, c, 0, :], hT[:, c, s0 - b0:s0 - b0 + n], c == 0, c == 7,
                               [twgu[j % 3], thT], [ptrk[pg]])
                        for c in range(8):
                            mm(psum[pu][:, :n], wgu[j % 3][:, c, 1, :], hT[:, c, s0 - b0:s0 - b0 + n], c == 0, c == 7,
                               [twgu[j % 3], thT], [ptrk[pu]])
                        act(sg[k % 2][:, :n], psum[pg][:, :n], AF.Silu, [ptrk[pg]], [tsg[k % 2]])
                        tt("dve", aT[:, j, s0 - b0:s0 - b0 + n], sg[k % 2][:, :n], psum[pu][:, :n], ALU.mult,
                           [tsg[k % 2], ptrk[pu]], [taT[j]])
                        k += 1
                    if j == FC - 1:
                        ld_dn(0)
                        ld_dn(1)
                k = 0
                for m in range(8):
                    if m >= 1 and m + 1 < 8:
                        ld_dn(m + 1)
                    for (s0, n) in blk:
                        py = 4 + (k % 2)
                        for j in range(FC):
                            mm(psum[py][:, :n], wdn[m % 2][:, j, :], aT[:, j, s0 - b0:s0 - b0 + n], j == 0, j == FC - 1,
                               [twdn[m % 2], taT[j]], [ptrk[py]])
                        tl = [XT(m, s) for s in range((s0 // 128) * 128, s0 + n, 128)]
                        stt("dve", xT[:, m, s0:s0 + n], psum[py][:, :n], 0.5, xT[:, m, s0:s0 + n], ALU.mult, ALU.add,
                            [ptrk[py]] + tl, tl)
                        k += 1
                S.barrier(scr[0:1, 0:1], scr[0:1, 4:5])

        def rstd_small(dst, src, n_inv, reads, trk):
            T_ = dst.shape[0]
            act(dst, src, AF.Ln, reads + [CONST], [trk], bias=epsc[:T_, 0:1], scale=n_inv)
            act(dst, dst, AF.Exp, [trk], [trk], scale=-0.5)

        def mixer(blk, l, bi, KT_, tKT, Vb_, tVb):
            b0 = blk[0][0]
            nb = sum(n for _, n in blk)
            bend = b0 + nb
            win_d = W["w_in"][l].rearrange("(c p) f -> p c f", p=128)
            wo_d = W["w_out"][l].rearrange("(c p) m -> p c m", p=128)
            tiles = [(128, 128 * i, i) for i in range(NFT) if b0 <= 128 * i < bend]
            has_tail = b0 <= NFT * 128 < bend
            with contextlib.ExitStack() as ph:
                hT = sb("hT_m", [128, 8, NBMAX], BF16, ph); thT = Trk()
                KTs = sb("KTs", [128, 4, 32], BF16, ph); tKTs = Trk()
                Vn = sb("Vn", [8, 4, 512], BF16, ph); tVn = Trk()
                with contextlib.ExitStack() as phn:
                    sqn = sb("sq_m", [128, 8, SEGN], BF16, phn); rstdn = sb("rstd_m", [128, SEGN], F32, phn)
                    norm_block(blk, "norm_mix", l, hT, thT, (sqn, Trk(), rstdn, Trk()))
                    S.barrier(scr[0:1, 0:1], scr[0:1, 4:5])

                def mk_w(stack, ncols, c0, half_):
                    wb = sb("wb", [128, 8, ncols], BF16, stack); twb = Trk()
                    for c in range(0, 8, 2):
                        for f0 in range(0, ncols, 512):
                            fn_ = min(512, ncols - f0)
                            dma("pool", wb[:, c:c + 2, f0:f0 + fn_], win_d[:, c:c + 2, c0 + f0:c0 + f0 + fn_], [], [twb])
                    wo = None; two = None
                    if half_ is not None:
                        wo = sb("wo", [128, 4, DM], BF16, stack); two = Trk()
                        for c in range(0, 4, 2):
                            for f0 in range(0, DM, 512):
                                dma("pool", wo[:, c:c + 2, f0:f0 + 512], wo_d[:, half_ * 4 + c:half_ * 4 + c + 2, f0:f0 + 512], [], [two])
                    return wb, twb, wo, two

                def proj_tm(wb, twb, pb, T, hc, wc0, wn):
                    for c in range(8):
                        mm(psum[pb][:T, :wn], hT[:, c, hc:hc + T], wb[:, c, wc0:wc0 + wn], c == 0, c == 7, [thT, twb], [ptrk[pb]])

                def wout_add(wo, two, ym, tym, T, g0):
                    for m in range(8):
                        pb = 6 + (m // 4)
                        for j in range(4):
                            mm(psum[pb][:, (m % 4) * 128:(m % 4) * 128 + T], wo[:, j, m * 128:(m + 1) * 128], ym[:, j, :T],
                               j == 0, j == 3, [two, tym], [ptrk[pb]])
                    for hm in range(2):
                        tl = [XT(m, g0) for m in range(4 * hm, 4 * hm + 4)]
                        tt("dve", xT[:, 4 * hm:4 * hm + 4, g0:g0 + T], xT[:, 4 * hm:4 * hm + 4, g0:g0 + T],
                           psum[6 + hm][:].rearrange("p (m t) -> p m t", m=4)[:, :, :T], ALU.add, [ptrk[6 + hm]] + tl, tl)

                def load_bc(stack, name, src):
                    t_ = sb(name, [128, 512], F32, stack)
                    tk = Trk()
                    dma("sp", t_[:], src.partition_broadcast(128), [], [tk])
                    return t_, tk

                with contextlib.ExitStack() as sp1:
                    wb, twb, wo, two = mk_w(sp1, 1288, 0, 0)
                    ssdn, tssdn = load_bc(sp1, "ssdn", W["ssd_norm"][l])
                    xr = sb("xr", [128, 6, 131], F32, sp1); txr = Trk()
                    acc = sb("acc", [128, 6, 128], F32, sp1); tacc = Trk()
                    xc = sb("xc", [128, 6, 128], F32, sp1); txc = Trk()
                    zs = sb("zs", [128, 512], F32, sp1); tzs = Trk()
                    dt = sb("dt", [128, 8], F32, sp1); tdt = Trk()
                    xtm = sb("xtm", [128, 512], F32, sp1); txtm = Trk()
                    Btm = sb("Btm", [128, 128], BF16, sp1); tBtm = Trk()
                    CTb = sb("CTb", [128, 128], BF16, sp1); BTb = sb("BTb", [128, 128], BF16, sp1); tCB = Trk()
                    cs_sb = sb("cs_sb", [128, 8], F32, sp1); ecs = sb("ecs", [128, 8], F32, sp1); te = sb("te", [128, 8], F32, sp1)
                    dtw = sb("dtw", [128, 8], F32, sp1); dec = sb("dec", [128, 8], F32, sp1); tsm = Trk()
                    dmt = sb("dmt", [128, 8, 128], F32, sp1); tdm = Trk()
                    cbm = sb("cbm", [128, 2, 128], F32, sp1); tcbm = Trk()
                    sc = sb("sc", [128, 8, 128], BF16, sp1); tsc = Trk()
                    xdt = sb("xdt", [128, 512], BF16, sp1); xw = sb("xw", [128, 512], BF16, sp1); txd = Trk()
                    y1 = sb("y1", [128, 512], F32, sp1); y2 = sb("y2", [128, 512], F32, sp1); ty = Trk()
                    ssq = sb("ssq", [128, 2], F32, sp1); tssq = Trk()
                    yb = sb("yb", [128, 512], BF16, sp1); tyb = Trk()
                    ym = sb("ym", [128, 4, 128], BF16, sp1); tym = Trk()
                    tmpS = sb("tmpS", [128, 4, 128], F32, sp1); ttmpS = Trk()
                    so = sb("so", [128, 4, 64], F32, sp1); tso = Trk()
                    hst = sb("hst", [128, 3, 6], F32, sp1); thst = Trk()

                    def ssd_tile(T, g0):
                        hc = g0 - b0
                        if SUB < 1:
                            return
                        for ch in range(6):
                            pb = 0 if ch < 4 else 1
                            o = (ch % 4) * 128
                            for c in range(8):
                                mm(psum[pb][:, o:o + T], wb[:, c, 512 + ch * 128:512 + (ch + 1) * 128], hT[:, c, hc:hc + T],
                                   c == 0, c == 7, [thT, twb], [ptrk[pb]])
                        cp("act", xr[:, 0:4, 3:3 + T], psum[0][:].rearrange("p (a t) -> p a t", a=4)[:, :, :T], [ptrk[0]], [txr])
                        cp("act", xr[:, 4:6, 3:3 + T], psum[1][:].rearrange("p (a t) -> p a t", a=4)[:, 0:2, :T], [ptrk[1]], [txr])
                        cp("pool", xr[:, :, 0:3], hist[:], [tHist], [txr])
                        if SUB < 2:
                            return
                        proj_tm(wb, twb, 2, T, hc, 0, 512)
                        act(zs[:T, :], psum[2][:T, :], AF.Silu, [ptrk[2]], [tzs])
                        for c in range(8):
                            mm(psum[7][:T, 16:24], hT[:, c, hc:hc + T], wb[:, c, 1280:1288], c == 0, c == 7, [thT, twb], [ptrk[7]])
                        tt("dve", dt[:T, :], psum[7][:T, 16:24], dtb_bc[:T, l, :], ALU.add, [ptrk[7], CONST], [tdt])
                        act(dt[:T, :], dt[:T, :], AF.Exp, [tdt], [tdt])
                        act(dt[:T, :], dt[:T, :], AF.Ln, [tdt], [tdt], bias=1.0)
                        if SUB < 3:
                            return
                        for ch in range(6):
                            ts("pool", acc[:, ch, :T], xr[:, ch, 0:T], convw[:, l, 0, ch:ch + 1], convb[:, l, ch:ch + 1], ALU.mult, ALU.add,
                               [txr, CONST], [tacc])
                            for j in range(1, 4):
                                stt("dve", acc[:, ch, :T], xr[:, ch, j:j + T], convw[:, l, j, ch:ch + 1], acc[:, ch, :T], ALU.mult, ALU.add,
                                    [txr, CONST, tacc], [tacc])
                        act(xc[:, :, :T], acc[:, :, :T], AF.Silu, [tacc], [txc])
                        cp("pool", hist[:], xr[:, :, T:T + 3], [txr], [tHist])
                        if SUB < 4:
                            return
                        for ch in range(4):
                            tr(psum[4][:T, ch * 128:(ch + 1) * 128], xc[:, ch, :T], identF[:, :], [txc, CONST], [ptrk[4]])
                        tr(psum[7][:T, 160:288], xc[:, 4, :T], identF[:, :], [txc, CONST], [ptrk[7]])
                        cp("act", xtm[:T, :], psum[4][:T, :], [ptrk[4]], [txtm])
                        cp("dve", Btm[:T, :], psum[7][:T, 160:288], [ptrk[7]], [tBtm])
                        cp("pool", BTb[:, :T], xc[:, 4, :T], [txc], [tCB])
                        cp("pool", CTb[:, :T], xc[:, 5, :T], [txc], [tCB])
                        if SUB < 5:
                            return
                        tt("dve", ext[:T, 0:8], dt[:T, :], A_bc[:T, l, :], ALU.mult, [tdt, CONST], [text])
                        ts("dve", ext[:T, 32:40], ext[:T, 0:8], -1.0, None, ALU.mult, None, [text], [text])
                        mm(psum[7][:T, 0:8], triF[:T, :T], ext[:T, 0:8], True, True, [text, CONST], [ptrk[7]])
                        mm(psum[7][:, 8:16], onesF[:T, :], ext[:T, 0:8], True, True, [text, CONST], [ptrk[7]])
                        mm(psum[7][0:40, 32:32 + T], ext[:T, 0:40], triF[:T, :T], True, True, [text, CONST], [ptrk[7]])
                        if SUB < 6:
                            return
                        cp("dve", cs_sb[:T, :], psum[7][:T, 0:8], [ptrk[7]], [tsm])
                        act(ecs[:T, :], psum[7][:T, 0:8], AF.Exp, [ptrk[7]], [tsm])
                        tt("dve", te[:T, :], psum[7][:T, 8:16], cs_sb[:T, :], ALU.subtract, [ptrk[7], tsm], [tsm])
                        act(te[:T, :], te[:T, :], AF.Exp, [tsm], [tsm])
                        tt("dve", dtw[:T, :], dt[:T, :], te[:T, :], ALU.mult, [tdt, tsm], [tsm])
                        act(dec[:, :], psum[7][:, 8:16], AF.Exp, [ptrk[7]], [tsm])
                        if SUB < 7:
                            return
                        cp("act", lhs64[32:40, :T], psum[7][32:40, 32:32 + T], [ptrk[7]], [tl64])
                        tt("dve", rhs64[0:8, :].rearrange("p (h t) -> p h t", h=8)[:, :, :T],
                           psum[7][0:8, 32:32 + T].unsqueeze(1).to_broadcast([8, 8, T]),
                           identF[0:8, 0:8].unsqueeze(2).to_broadcast([8, 8, T]), ALU.mult, [ptrk[7], CONST], [tr64])
                        for h in range(8):
                            pb = h // 4
                            mm(psum[pb][:T, (h % 4) * 128:(h % 4) * 128 + T], lhs64[:, :T], rhs64[:, h * 128:h * 128 + T], True, True,
                               [tl64, tr64], [ptrk[pb]])
                        if SUB < 8:
                            return
                        for g in range(2):
                            ts("dve", dmt[:T, 4 * g:4 * g + 4, :T], psum[g][:T, :].rearrange("p (a t) -> p a t", a=4)[:, :, :T], 0.0, None,
                               ALU.min, None, [ptrk[g]], [tdm])
                        act(dmt[:T, :, :T], dmt[:T, :, :T], AF.Exp, [tdm], [tdm])
                        for g in range(2):
                            mm(psum[5][:T, g * 128:g * 128 + T], BTb[g * 64:(g + 1) * 64, :T], CTb[g * 64:(g + 1) * 64, :T], True, True,
                               [tCB], [ptrk[5]])
                        tt("dve", cbm[:T, :, :T], psum[5][:T, 0:256].rearrange("p (g t) -> p g t", g=2)[:, :, :T],
                           triF[:T, :T].unsqueeze(1).to_broadcast([T, 2, T]), ALU.mult, [ptrk[5], CONST], [tcbm])
                        for g in range(2):
                            tt("dve", sc[:T, 4 * g:4 * g + 4, :T], dmt[:T, 4 * g:4 * g + 4, :T],
                               cbm[:T, g:g + 1, :T].to_broadcast([T, 4, T]), ALU.mult, [tdm, tcbm], [tsc])
                        if SUB < 9:
                            return
                        tt("dve", xdt[:T, :].rearrange("p (h d) -> p h d", h=8), xtm[:T, :].rearrange("p (h d) -> p h d", h=8),
                           dt[:T, :].unsqueeze(2).to_broadcast([T, 8, 64]), ALU.mult, [txtm, tdt], [txd])
                        tt("dve", xw[:T, :].rearrange("p (h d) -> p h d", h=8), xtm[:T, :].rearrange("p (h d) -> p h d", h=8),
                           dtw[:T, :].unsqueeze(2).to_broadcast([T, 8, 64]), ALU.mult, [txtm, tsm], [txd])
                        for h in range(8):
                            mm(psum[2][:T, h * 64:(h + 1) * 64], sc[:T, h, :T], xdt[:T, h * 64:(h + 1) * 64], True, True, [tsc, txd], [ptrk[2]])
                        mm(psum[4][:T, :], CTb[:, :T], STm[:, :], True, True, [tCB, tSTm], [ptrk[4]])
                        tt("dve", y1[:T, :].rearrange("p (h d) -> p h d", h=8), psum[4][:T, :].rearrange("p (h d) -> p h d", h=8),
                           ecs[:T, :].unsqueeze(2).to_broadcast([T, 8, 64]), ALU.mult, [ptrk[4], tsm], [ty])
                        tt("dve", y2[:T, :], y1[:T, :], psum[2][:T, :], ALU.add, [ty, ptrk[2]], [ty])
                        if SUB < 10:
                            return
                        mm(psum[5][:, :], Btm[:T, :], xw[:T, :], True, True, [tBtm, txd], [ptrk[5]])
                        tt("dve", ST[:].rearrange("p (h d) -> p h d", h=8), ST[:].rearrange("p (h d) -> p h d", h=8),
                           dec[:, :].unsqueeze(2).to_broadcast([128, 8, 64]), ALU.mult, [tST, tsm], [tST])
                        tt("dve", ST[:], ST[:], psum[5][:, :], ALU.add, [tST, ptrk[5]], [tST])
                        tt("pool", STm[:].rearrange("p (h d) -> p h d", h=8), ST[:].rearrange("p (h d) -> p h d", h=8),
                           bmask[:, :].unsqueeze(2).to_broadcast([128, 8, 64]), ALU.mult, [tST, CONST], [tSTm])
                        if SUB < 11:
                            return
                        tt("dve", y1[:T, :].rearrange("p (h d) -> p h d", h=8), xtm[:T, :].rearrange("p (h d) -> p h d", h=8),
                           D_bc[:T, l, :].unsqueeze(2).to_broadcast([T, 8, 64]), ALU.mult, [txtm, CONST, ty], [ty])
                        tt("dve", y2[:T, :], y2[:T, :], y1[:T, :], ALU.add, [ty], [ty])
                        tt("dve", y2[:T, :], y2[:T, :], zs[:T, :], ALU.mult, [ty, tzs], [ty])
                        tt("pool", y1[:T, :], y2[:T, :], y2[:T, :], ALU.mult, [ty], [ty])
                        rsum("dve", ssq[:T, :], y1[:T, :].rearrange("p (g d) -> p g d", g=2), [ty], [tssq])
                        rstd_small(ssq[:T, :], ssq[:T, :], 1.0 / 256, [tssq], tssq)
                        tt("dve", y2[:T, :].rearrange("p (g d) -> p g d", g=2), y2[:T, :].rearrange("p (g d) -> p g d", g=2),
                           ssq[:T, :].unsqueeze(2).to_broadcast([T, 2, 256]), ALU.mult, [ty, tssq], [ty])
                        tt("dve", yb[:T, :], y2[:T, :], ssdn[:T, :], ALU.mult, [ty, tssdn], [tyb])
                        for j in range(4):
                            tr(pbf(3)[:, j * 128:j * 128 + T], yb[:T, j * 128:(j + 1) * 128], identB[:T, :T], [tyb, CONST], [ptrk[3]])
                        cp("act", ym[:, :, :T], pbf(3)[:, 0:512].rearrange("p (j t) -> p j t", j=4)[:, :, :T], [ptrk[3]], [tym])
                        wout_add(wo, two, ym, tym, T, g0)


                    def state_out(dst):
                        if "a" in DBG:
                            return
                        for j in range(4):
                            tr(psum[4][:, j * 128:(j + 1) * 128], ST[:, j * 128:(j + 1) * 128], identF[:, :], [tST, CONST], [ptrk[4]])
                        for j in range(4):
                            g = j // 2
                            cp("act", so[:, j, :], psum[4][:, j * 128 + g * 64:j * 128 + g * 64 + 64], [ptrk[4]], [tso])
                        dma("sp", dst.rearrange("(j hh) p n -> (hh p) j n", hh=2), so[:], [tso], [])

                    def conv_out(dst):
                        if "b" in DBG:
                            return
                        for j_ in range(3):
                            dma_nc("sp", dst[j_].rearrange("(c p) -> p c", p=128), hist[:, :, j_], [tHist], [])

                    if bi == 0:
                        mset("pool", ST[:], 0.0, [tST]); mset("pool", STm[:], 0.0, [tSTm]); mset("pool", hist[:], 0.0, [tHist])
                    for (T, g0, i) in tiles:
                        ssd_tile(T, g0)
                    if has_tail:
                        ssd_tile(16, NFT * 128)
                        state_out(ssmp_o[l])
                        conv_out(convp_o[l])
                        for b in range(0 if "c" in DBG else 4):
                            if "d" not in DBG:
                                mset("pool", tmpS[:], 0.0, [ttmpS])
                                for j in range(4):
                                    g = j // 2
                                    for hh in range(2):
                                        dma("sp", tmpS[hh * 64:(hh + 1) * 64, j, g * 64:(g + 1) * 64], sssm[l, b, 2 * j + hh], [], [ttmpS])
                                if "f" not in DBG:
                                    for j in range(4):
                                        tr(psum[5][:, j * 128:(j + 1) * 128], tmpS[:, j, :], identF[:, :], [ttmpS, CONST], [ptrk[5]])
                                    cp("dve", ST[:], psum[5][:, :], [ptrk[5]], [tST])
                                    if "g" not in DBG:
                                        cp("act", STm[:], psum[5][:, :], [ptrk[5]], [tSTm])
                            if "e" not in DBG:
                                for j_ in range(3):
                                    dma_nc("sp", hst[:, j_, :], sconv[l, b, j_].rearrange("(c p) -> p c", p=128), [], [thst])
                                cp("pool", hist[:].rearrange("p c j -> p j c"), hst[:], [thst], [tHist])
                            ssd_tile(8, NPOS + 8 * b)
                            state_out(ssms_o[l, b])
                            conv_out(convs_o[l, b])
                    S.barrier(scr[0:1, 0:1], scr[0:1, 4:5])

                def mk_norm_bufs(stack):
                    d = {}
                    d["raw"] = sb("raw", [128, 512], F32, stack); d["traw"] = Trk()
                    d["sq2"] = sb("sq2", [128, 512], F32, stack); d["tsq2"] = Trk()
                    d["s8"] = sb("s8", [128, 8], F32, stack); d["ts8"] = Trk()
                    d["nrm"] = sb("nrm", [128, 512], F32, stack); d["tnrm"] = Trk()
                    d["nb16"] = sb("nb16", [128, 512], BF16, stack); d["tnb"] = Trk()
                    return d

                def qk_norm(d, pb, T, gbc):
                    raw, traw, sq2, tsq2, s8, ts8, nrm, tnrm, nb16, tnb = (d[k_] for k_ in
                        ("raw", "traw", "sq2", "tsq2", "s8", "ts8", "nrm", "tnrm", "nb16", "tnb"))
                    cp("act", raw[:T, :], psum[pb][:T, :], [ptrk[pb]], [traw])
                    tt("pool", sq2[:T, :], raw[:T, :], raw[:T, :], ALU.mult, [traw], [tsq2])
                    rsum("dve", s8[:T, :], sq2[:T, :].rearrange("p (h d) -> p h d", h=8), [tsq2], [ts8])
                    rstd_small(s8[:T, :], s8[:T, :], 1.0 / 64, [ts8], ts8)
                    tt("dve", nrm[:T, :].rearrange("p (h d) -> p h d", h=8), raw[:T, :].rearrange("p (h d) -> p h d", h=8),
                       s8[:T, :].unsqueeze(2).to_broadcast([T, 8, 64]), ALU.mult, [traw, ts8], [tnrm])
                    tt("dve", nrm[:T, :].rearrange("p (h d) -> p h d", h=8), nrm[:T, :].rearrange("p (h d) -> p h d", h=8),
                       gbc[:T, l, :].unsqueeze(1).to_broadcast([T, 8, 64]), ALU.mult, [tnrm, CONST], [tnrm])
                    cp("pool", nb16[:T, :], nrm[:T, :], [tnrm], [tnb])

                def to_fm(d, dst, tdst, T, c0):
                    nb16, tnb = d["nb16"], d["tnb"]
                    for j in range(4):
                        tr(pbf(3)[:, j * 128:j * 128 + T], nb16[:T, j * 128:(j + 1) * 128], identB[:T, :T], [tnb, CONST], [ptrk[3]])
                    cp("act", dst[:, :, c0:c0 + T], pbf(3)[:, 0:512].rearrange("p (j t) -> p j t", j=4)[:, :, :T], [ptrk[3]], [tdst])

                if not allow(l, 3):
                    return
                with contextlib.ExitStack() as sp2:
                    wb, twb, _, _ = mk_w(sp2, 1024, 1800, None)
                    d = mk_norm_bufs(sp2)
                    vraw = sb("vraw", [128, 512], F32, sp2); tvraw = Trk()

                    def kv_tile(T, g0, kt_idx, sample):
                        hc = g0 - b0
                        proj_tm(wb, twb, 0, T, hc, 0, 512)
                        qk_norm(d, 0, T, kn_bc)
                        if sample:
                            dma("sp", ks_o[l, :, :], d["nrm"][:T, :], [d["tnrm"]], [])
                            to_fm(d, KTs, tKTs, T, 0)
                            for b in range(4):
                                for c in range(8):
                                    mm(psum[1][0:8, :], hT[:, c, hc + 8 * b:hc + 8 * b + 8], wb[:, c, 512:1024], c == 0, c == 7,
                                       [thT, twb], [ptrk[1]])
                                cp("act", vraw[0:8, :], psum[1][0:8, :], [ptrk[1]], [tvraw])
                                dma("sp", vs_o[l, 8 * b:8 * b + 8, :], vraw[0:8, :], [tvraw], [])
                                cp("pool", Vn[0:8, b, :], vraw[0:8, :], [tvraw], [tVn])
                        else:
                            dma("sp", kp_o[l, g0:g0 + T, :], d["nrm"][:T, :], [d["tnrm"]], [])
                            to_fm(d, KT_, tKT[kt_idx], T, g0)
                            proj_tm(wb, twb, 1, T, hc, 512, 512)
                            cp("act", vraw[:T, :], psum[1][:T, :], [ptrk[1]], [tvraw])
                            dma("sp", vp_o[l, g0:g0 + T, :], vraw[:T, :], [tvraw], [])
                            cp("pool", Vb_[:T, kt_idx, :], vraw[:T, :], [tvraw], [tVb[kt_idx]])

                    for (T, g0, i) in tiles:
                        kv_tile(T, g0, i, False)
                    if has_tail:
                        kv_tile(16, NFT * 128, NFT, False)
                        kv_tile(32, NPOS, None, True)
                    S.barrier(scr[0:1, 0:1], scr[0:1, 4:5])

                if not allow(l, 4):
                    return
                with contextlib.ExitStack() as sp3:
                    wb, twb, wo, two = mk_w(sp3, 512, 1288, 1)
                    sbon, tsbon = load_bc(sp3, "sbon", W["sb_out_norm"][l])
                    d = mk_norm_bufs(sp3)
                    raw, traw, sq2, tsq2, s8, ts8, nb16, tnb = (d[k_] for k_ in ("raw", "traw", "sq2", "tsq2", "s8", "ts8", "nb16", "tnb"))
                    QT = sb("QT", [128, 4, 128], BF16, sp3); tQT = Trk()
                    NSET = 2
                    csets = [dict(eb=sb("eb", [128, 513], F32, sp3), Sb=sb("Sb", [128, 513], F32, sp3), Fx=sb("Fx", [128, 513], F32, sp3),
                                  wq=sb("wq", [128, 512], BF16, sp3), te=Trk(), tS=Trk(), tF=Trk(), twq=Trk()) for _ in range(NSET)]
                    ncars = [(sb("ncar", [128, 1], F32, sp3), Trk()) for _ in range(2)]
                    cctr = [0]
                    wT = [sb("wT%d" % i_, [128, 4, 128], BF16, sp3) for i_ in range(2)]; twT = [Trk(), Trk()]
                    ym2 = sb("ym2", [128, 4, 128], BF16, sp3); tym2 = Trk()

                    def sb_chunk(R, Wn, pz, bias_ap, diag_mask, first, nci=0):
                        cs_ = csets[cctr[0] % NSET]
                        cctr[0] += 1
                        eb, Sb, Fx, wq, te_, tS_, tF_, twq = (cs_[k_] for k_ in ("eb", "Sb", "Fx", "wq", "te", "tS", "tF", "twq"))
                        ncar, tnc = ncars[nci]
                        act(eb[:R, :Wn], psum[pz][:R, :Wn], AF.Exp, [ptrk[pz], CONST], [te_], bias=bias_ap, scale=0.125)
                        if first:
                            mset("pool", ncar[:R, :], 0.0, [tnc])
                        mset("pool", Sb[:R, 0:1], 0.0, [tS_])
                        act(Sb[:R, 1:Wn + 1], eb[:R, :Wn], AF.Ln, [te_, tS_], [tS_], bias=1.0)
                        if diag_mask is not None:
                            dm_ap, dw = diag_mask
                            tt("pool", Sb[:R, 1 + Wn - dw:1 + Wn], Sb[:R, 1 + Wn - dw:1 + Wn], dm_ap, ALU.mult, [tS_, CONST], [tS_])
                        S.op("dve", lambda e: e.tensor_tensor_scan(out=Fx[:R, :Wn + 1], data0=onec[:R, 0:1].to_broadcast([R, Wn + 1]),
                                                                   data1=Sb[:R, :Wn + 1], initial=0.0, op0=ALU.mult, op1=ALU.add),
                             [tS_, CONST], [tF_])
                        tt("dve", ncar[:R, :], ncar[:R, :], Fx[:R, Wn:Wn + 1], ALU.subtract, [tnc, tF_], [tnc])
                        act(Sb[:R, :Wn], Fx[:R, :Wn], AF.Exp, [tF_, tnc, tS_], [tS_], bias=ncar[:R, 0:1])
                        tt("dve", wq[:R, :Wn], eb[:R, :Wn], Sb[:R, :Wn], ALU.mult, [te_, tS_], [twq])
                        if diag_mask is not None:
                            dm_ap, dw = diag_mask
                            tt("pool", wq[:R, Wn - dw:Wn], wq[:R, Wn - dw:Wn], dm_ap, ALU.mult, [twq, CONST], [twq])
                        return wq, twq

                    def attn_tile(T, g0, i):
                        hc = g0 - b0
                        proj_tm(wb, twb, 0, T, hc, 0, 512)
                        qk_norm(d, 0, T, qn_bc)
                        to_fm(d, QT, tQT, T, 0)
                        nk = g0 + T
                        nck = (nk + KCH - 1) // KCH
                        nt_total = (nk + 127) // 128
                        items = [(h, c) for h in range(8) for c in reversed(range(nck))]
                        stt_ = {}

                        def qk(n):
                            h, c = items[n]
                            hp, hh = h // 2, h % 2
                            k0 = c * KCH
                            Wn = min(KCH, nk - k0)
                            pz = 1 + (n % 2)
                            ktl = [tKT[t_] for t_ in range(k0 // 128, (k0 + Wn + 127) // 128)]
                            mm(psum[pz][:T, :Wn], QT[hh * 64:(hh + 1) * 64, hp, :T], KT_[hh * 64:(hh + 1) * 64, hp, k0:k0 + Wn], True, True,
                               [tQT] + ktl, [ptrk[pz]])

                        def elem(n):
                            h, c = items[n]
                            k0 = c * KCH
                            Wn = min(KCH, nk - k0)
                            dmk = (m01F[:T, :T], T) if c == nck - 1 else None
                            stt_[n] = sb_chunk(T, Wn, 1 + (n % 2), sbias_bc[:T, l, h:h + 1], dmk, c == nck - 1, h % 2)

                        def trcp(n):
                            h, c = items[n]
                            k0 = c * KCH
                            Wn = min(KCH, nk - k0)
                            wq, twq = stt_[n]
                            wts = wT[n % 2]; twts = twT[n % 2]
                            njt = (Wn + 127) // 128
                            for j in range(njt):
                                ksz = min(128, Wn - 128 * j)
                                tr(pbf(3)[:ksz, j * 128:j * 128 + T], wq[:T, 128 * j:128 * j + ksz], identB[:T, :T], [twq, CONST], [ptrk[3]])
                            for j in range(njt):
                                ksz = min(128, Wn - 128 * j)
                                cp("act" if j % 2 else "dve", wts[:ksz, j, :T], pbf(3)[:ksz, j * 128:j * 128 + T], [ptrk[3]], [twts])

                        def pv(n):
                            h, c = items[n]
                            k0 = c * KCH
                            Wn = min(KCH, nk - k0)
                            wts = wT[n % 2]; twts = twT[n % 2]
                            njt = (Wn + 127) // 128
                            for j in range(njt):
                                ksz = min(128, Wn - 128 * j)
                                kt = k0 // 128 + j
                                first_ = (c == nck - 1 and j == 0)
                                last_ = (c == 0 and j == njt - 1)
                                mm(psum[4][:T, h * 64:(h + 1) * 64], wts[:ksz, j, :T], Vb_[:ksz, kt, h * 64:(h + 1) * 64],
                                   first_, last_, [twts, tVb[kt]], [ptrk[4]])

                        qk(0)
                        elem(0)
                        for n in range(len(items)):
                            if n + 1 < len(items):
                                qk(n + 1)
                            trcp(n)
                            if n + 1 < len(items):
                                elem(n + 1)
                            pv(n)
                        cp("act", raw[:T, :], psum[4][:T, :], [ptrk[4]], [traw])
                        tt("pool", sq2[:T, :], raw[:T, :], raw[:T, :], ALU.mult, [traw], [tsq2])
                        rsum("dve", s8[:T, 0:1], sq2[:T, :], [tsq2], [ts8])
                        rstd_small(s8[:T, 0:1], s8[:T, 0:1], 1.0 / 512, [ts8], ts8)
                        stt("dve", nb16[:T, :], raw[:T, :], s8[:T, 0:1], sbon[:T, :], ALU.mult, ALU.mult, [traw, ts8, tsbon], [tnb])
                        to_fm(d, ym2, tym2, T, 0)
                        wout_add(wo, two, ym2, tym2, T, g0)

                    def sample_attn():
                        with contextlib.ExitStack() as sp4:
                            Lq = sb("Lq", [128, 4, 4, 128], BF16, sp4); tLq = Trk()
                            osb = sb("osb", [8, 2, 512], F32, sp4); tosb = Trk()
                            Kbf = sb("Kbf", [128, 2, PGC, 512], BF16, sp4); tKbf = Trk()
                            Vbs = sb("Vbs", [128, 2, PGC, 512], BF16, sp4); tVbs = Trk()
                            hc = NPOS - b0
                            proj_tm(wb, twb, 0, 32, hc, 0, 512)
                            qk_norm(d, 0, 32, qn_bc)
                            to_fm(d, QT, tQT, 32, 0)
                            mset("pool", Lq[:], 0.0, [tLq])
                            for b in range(4):
                                bb = b % 2
                                for hp in range(4):
                                    for hh in range(2):
                                        h = 2 * hp + hh
                                        cp("pool", Lq[hh * 64:(hh + 1) * 64, b, hp, bb * 64 + h * 8:bb * 64 + h * 8 + 8],
                                           QT[hh * 64:(hh + 1) * 64, hp, 8 * b:8 * b + 8], [tQT, tLq], [tLq])
                            for pr in range(2):
                                bs = (2 * pr, 2 * pr + 1)
                                n = 0
                                for bb, b in enumerate(bs):
                                    for hp in range(4):
                                        mm(psum[1][:, 0:8], Lq[:, b, hp, :], KTs[:, hp, 8 * b:8 * b + 8], n == 0, n == 7, [tLq, tKTs], [ptrk[1]])
                                        n += 1
                                wq, twq = sb_chunk(128, 8, 1, bias_s[:, l:l + 1], (mask_s[:, :], 8), True)
                                if "w" in DBG and pr == 0 and l == 0:
                                    dbgt = sb("dbgt", [128, 16], F32, sp4); tdbg = Trk()
                                    cp("dve", dbgt[:, 0:8], wq[:, 0:8], [twq], [tdbg])
                                    cp("dve", dbgt[:, 8:16], eb[:, 0:8], [te_, tdbg], [tdbg])
                                    dma("sp", ks_o[1].rearrange("t (a c) -> (t a) c", c=16)[0:128, :], dbgt[:, :], [tdbg], [])
                                tr(pbf(3)[0:8, 0:128], wq[:, 0:8], identB[:, :], [twq, CONST], [ptrk[3]])
                                cp("dve", wT[0][0:8, 0, :], pbf(3)[0:8, 0:128], [ptrk[3]], [twT[0]])
                                for bb, b in enumerate(bs):
                                    for h in range(8):
                                        mm(psum[4 + bb][0:8, h * 64:(h + 1) * 64], wT[0][0:8, 0, bb * 64 + h * 8:bb * 64 + h * 8 + 8],
                                           Vn[0:8, b, h * 64:(h + 1) * 64], h == 0, False, [twT[0], tVn], [ptrk[4 + bb]])
                                k = 1
                                for cch in reversed(range(NCH_S)):
                                    for bb, b in enumerate(bs):
                                        for j in range(PGC):
                                            col = b * NPG + cch * PGC + j
                                            S.op("pool", lambda e, bb=bb, j=j, col=col: e.indirect_dma_start(
                                                out=Kbf[:, bb, j, :], out_offset=None, in_=ckT[l].rearrange("n p f -> (n p) f"),
                                                in_offset=bass.IndirectOffsetOnAxis(ap=idxT[:, col:col + 1], axis=0)), [CONST], [tKbf], dma=True)
                                            S.op("pool", lambda e, bb=bb, j=j, col=col: e.indirect_dma_start(
                                                out=Vbs[:, bb, j, :], out_offset=None, in_=cv[l].rearrange("n p f -> (n p) f"),
                                                in_offset=bass.IndirectOffsetOnAxis(ap=idxT[:, col:col + 1], axis=0)), [CONST], [tVbs], dma=True)
                                    Wn = PGC * 128
                                    pz = 1 + (k % 2)
                                    wts = wT[k % 2]; twts = twT[k % 2]
                                    k += 1
                                    for j in range(PGC):
                                        n = 0
                                        for bb, b in enumerate(bs):
                                            for hp in range(4):
                                                mm(psum[pz][:, j * 128:(j + 1) * 128], Lq[:, b, hp, :], Kbf[:, bb, j, hp * 128:(hp + 1) * 128],
                                                   n == 0, n == 7, [tLq, tKbf], [ptrk[pz]])
                                                n += 1
                                    wq, twq = sb_chunk(128, Wn, pz, bias_s[:, l:l + 1], None, False)
                                    for j in range(PGC):
                                        tr(pbf(3)[:, j * 128:(j + 1) * 128], wq[:, 128 * j:128 * (j + 1)], identB[:, :], [twq, CONST], [ptrk[3]])
                                    for j in range(PGC):
                                        cp("act" if j % 2 else "dve", wts[:, j, :], pbf(3)[:, j * 128:(j + 1) * 128], [ptrk[3]], [twts])
                                    for j in range(PGC):
                                        lastmm = (cch == 0 and j == PGC - 1)
                                        for bb, b in enumerate(bs):
                                            for h in range(8):
                                                mm(psum[4 + bb][0:8, h * 64:(h + 1) * 64], wts[:, j, bb * 64 + h * 8:bb * 64 + h * 8 + 8],
                                                   Vbs[:, bb, j, h * 64:(h + 1) * 64], False, lastmm, [twts, tVbs], [ptrk[4 + bb]])
                                for bb, b in enumerate(bs):
                                    cp("act", osb[0:8, bb, :], psum[4 + bb][0:8, :], [ptrk[4 + bb]], [tosb])
                                if "o" in DBG and pr == 0 and l == 0:
                                    dma("sp", vs_o[1].rearrange("(t a) c -> t (a c)", a=2)[0:8, :], osb[0:8, :, :].rearrange("p a c -> p (a c)"), [tosb], [])
                                for bb, b in enumerate(bs):
                                    tt("pool", sq2[0:8, :], osb[0:8, bb, :], osb[0:8, bb, :], ALU.mult, [tosb, tsq2], [tsq2])
                                    rsum("dve", s8[0:8, bb:bb + 1], sq2[0:8, :], [tsq2, ts8], [ts8])
                                rstd_small(s8[0:8, 0:2], s8[0:8, 0:2], 1.0 / 512, [ts8], ts8)
                                for bb, b in enumerate(bs):
                                    stt("dve", nb16[0:8, :], osb[0:8, bb, :], s8[0:8, bb:bb + 1], sbon[0:8, :], ALU.mult, ALU.mult,
                                        [tosb, ts8, tsbon, tnb], [tnb])
                                    for j in range(4):
                                        tr(pbf(3)[:, j * 128 + 8 * bb:j * 128 + 8 * bb + 8], nb16[0:8, j * 128:(j + 1) * 128], identB[0:8, 0:8],
                                           [tnb, CONST], [ptrk[3]])
                                cp("act", ym2[:, :, 16 * pr:16 * pr + 16], pbf(3)[:, 0:512].rearrange("p (j t) -> p j t", j=4)[:, :, 0:16],
                                   [ptrk[3]], [tym2])
                            wout_add(wo, two, ym2, tym2, 32, NPOS)
                            S.barrier(scr[0:1, 0:1], scr[0:1, 4:5])

                    for (T, g0, i) in tiles:
                        attn_tile(T, g0, i)
                    if has_tail:
                        attn_tile(16, NFT * 128, NFT)
                        S.barrier(scr[0:1, 0:1], scr[0:1, 4:5])
                        if allow(l, 5):
                            sample_attn()
                    S.barrier(scr[0:1, 0:1], scr[0:1, 4:5])

        STOP = cfg.get("stop", 99)
        SUB = cfg.get("sub", 99)
        DBG = cfg.get("dbg", "")

        def allow(l, code):
            return 10 * l + code <= STOP

        for l in range(2):
            for blk in blocks:
                if allow(l, 1):
                    ffn(blk, l, 1)
            with contextlib.ExitStack() as lay:
                KTt = sb("KT", [128, 4, NPOS], BF16, lay)
                Vbt = sb("Vb", [128, NFT + 1, 512], BF16, lay)
                tKT = [Trk() for _ in range(NFT + 1)]
                tVb = [Trk() for _ in range(NFT + 1)]
                for bi, blk in enumerate(blocks):
                    if allow(l, 2):
                        mixer(blk, l, bi, KTt, tKT, Vbt, tVb)
            for blk in blocks:
                if allow(l, 6):
                    ffn(blk, l, 2)
        if "m" in DBG:
            dma("sp", ks_o[1].rearrange("t (a c) -> (t a) c", c=8)[0:128, :], mask_s[:, :], [CONST], [])
            dma("sp", vs_o[1].rearrange("t (a c) -> (t a) c", c=2)[0:128, :], bias_s[:, :], [CONST], [])
        with contextlib.ExitStack() as phy:
            yst = [sb("yst%d" % i, [128, 1024], F32, phy) for i in range(2)]
            tyst = [Trk(), Trk()]
            k = 0
            for c0 in range(0, NT, 128):
                n = min(128, NT - c0)
                ys, tys = yst[k % 2], tyst[k % 2]
                for c in range(8):
                    pb = c % 2
                    tr(psum[pb][:n, 0:128], xT[:, c, c0:c0 + n], identF[:, :], [XT(c, c0), CONST], [ptrk[pb]])
                    cp("act" if c % 2 else "dve", ys[:n, c * 128:(c + 1) * 128], psum[pb][:n, 0:128], [ptrk[pb]], [tys])
                dma("sp", y_o[c0:c0 + n, :], ys[:n, :], [tys], [])
                k += 1
            S.fence_all()
            S.emit()
    return nc


_WNAMES = ("norm_ffn1", "ffn1_w_gu", "ffn1_w_down", "norm_mix", "w_in", "conv_w", "conv_b", "dt_bias", "A_log", "D_skip",
           "ssd_norm", "q_norm", "k_norm", "sb_bias", "sb_out_norm", "w_out", "norm_ffn2", "ffn2_w_gu", "ffn2_w_down")


def kernel(**inp):
    x_prompt = np.asarray(inp["x_prompt"], np.float32)
    x_sample = np.asarray(inp["x_sample"], np.float32)
    B, SEQ, DM = x_prompt.shape
    cache_k = np.asarray(inp["cache_k"], np.float32)
    cache_v = np.asarray(inp["cache_v"], np.float32)
    NPHYS = cache_k.shape[1]
    page_table = np.asarray(inp["page_table"], np.int32)
    NPG = page_table.shape[1]
    DFF = inp["ffn1_w_down"].shape[1]
    cfg = dict(SEQ=SEQ, DFF=DFF, PAST=NPG * 128, NPHYS=NPHYS, SEGN=512 if SEQ >= 1024 else 128)
    if "_stop" in inp:
        cfg["stop"] = int(inp["_stop"])
    if "_sub" in inp:
        cfg["sub"] = int(inp["_sub"])
    if "_dbg" in inp:
        cfg["dbg"] = inp["_dbg"]
    nc = build(cfg)
    NPOS = SEQ + 16
    meta = np.asarray(inp["meta_tokens"], np.float32)
    ckT = [np.ascontiguousarray(cache_k[i].reshape(NPHYS, 128, 4, 128).transpose(0, 3, 2, 1)).reshape(NPHYS, 128, 512) for i in range(2)]
    cvr = [np.ascontiguousarray(cache_v[i].reshape(NPHYS, 128, 512)) for i in range(2)]
    wts = {k: np.ascontiguousarray(np.asarray(inp[k], np.float32)) for k in _WNAMES}
    in_maps = []
    for c in range(8):
        xin = np.concatenate([meta, x_prompt[c], x_sample[4 * c:4 * c + 4].reshape(32, DM)], axis=0)
        m = dict(xin=np.ascontiguousarray(xin), ckT0=ckT[0], ckT1=ckT[1], cv0=cvr[0], cv1=cvr[1],
                 sssm=np.ascontiguousarray(np.asarray(inp["state_ssm"], np.float32)[:, 4 * c:4 * c + 4]),
                 sconv=np.ascontiguousarray(np.asarray(inp["state_conv"], np.float32)[:, 4 * c:4 * c + 4]),
                 ptab=np.ascontiguousarray(page_table[4 * c:4 * c + 4].reshape(1, 4 * NPG)))
        m.update(wts)
        in_maps.append(m)
    res = run_bass_kernel_spmd(nc, in_maps, core_ids=list(range(8))).results
    y = np.stack([r["y"] for r in res])
    y_prompt = np.ascontiguousarray(y[:, 16:NPOS])
    y_sample = np.ascontiguousarray(y[:, NPOS:].reshape(32, 8, DM))

    def st(name, shape_tail):
        return np.ascontiguousarray(np.stack([r[name] for r in res], axis=1).reshape((2, -1) + shape_tail))

    k_prompt = np.stack([r["kp"] for r in res], axis=1).reshape(2, 8, NPOS, 8, 64)
    v_prompt = np.stack([r["vp"] for r in res], axis=1).reshape(2, 8, NPOS, 8, 64)
    ssm_prompt = np.stack([r["ssmp"] for r in res], axis=1)
    conv_prompt = np.stack([r["convp"] for r in res], axis=1)
    k_sample = np.stack([r["ks"] for r in res], axis=1).reshape(2, 32, 8, 8, 64)
    v_sample = np.stack([r["vs"] for r in res], axis=1).reshape(2, 32, 8, 8, 64)
    ssm_sample = np.stack([r["ssms"] for r in res], axis=1).reshape(2, 32, 8, 64, 64)
    conv_sample = np.stack([r["convs"] for r in res], axis=1).reshape(2, 32, 3, 768)
    return tuple(np.ascontiguousarray(a, dtype=np.float32) for a in
                 (y_prompt, y_sample, k_prompt, v_prompt, ssm_prompt, conv_prompt, k_sample, v_sample, ssm_sample, conv_sample))
```
